# Optimizing a Trainium2 kernel written in Bass

```python
import math, functools
import jax, jax.numpy as jnp
from jax import lax
import numpy as np

D_MODEL = 1024
BATCH = 4
SEQ = 4096
DEPTH = 4
DEC_BATCH = 32
DEC_SEQ = 1
PAST_LEN = 8192
PAGE_SIZE = 128

POOL_WIDTH = D_MODEL // 2
POOL_WINDOWS = (2, 4, 8, 16)
N_POOL_GROUPS = len(POOL_WINDOWS)
POOL_GROUP = POOL_WIDTH // N_POOL_GROUPS
POOL_HIST = max(POOL_WINDOWS) - 1
HEAD_DIM = 64
ATTN_PATTERNS = ((128, 1), (512, 4), (2048, 16))
N_GROUPS = len(ATTN_PATTERNS)
HEADS_PER_GROUP = 4
N_HEADS = N_GROUPS * HEADS_PER_GROUP
ATTN_WIDTH = N_HEADS * HEAD_DIM
ATTN_OUT = HEADS_PER_GROUP * HEAD_DIM
N_KEYS = 128
BLK = 128
D_FF = 4 * D_MODEL
ROPE_THETA = 10000.0
EPS = 1e-6
IN_SPLITS = [POOL_WIDTH, POOL_WIDTH + ATTN_WIDTH, POOL_WIDTH + 2 * ATTN_WIDTH,
             POOL_WIDTH + 3 * ATTN_WIDTH, POOL_WIDTH + 3 * ATTN_WIDTH + D_MODEL]
IN_WIDTH = POOL_WIDTH + 3 * ATTN_WIDTH + 2 * D_MODEL

kernel_name = 'hybrid_pool_dilated_attn_decoder_step'


def rmsnorm(x, gain):
    xf = x.astype(jnp.float32)
    y = xf * lax.rsqrt(jnp.mean(xf * xf, axis=-1, keepdims=True) + EPS)
    return (y * gain.astype(jnp.float32)).astype(x.dtype)


def modulate(x, gain, shift, scale):
    return rmsnorm(x, gain) * (1 + scale) + shift


def rope(x, pos):
    inv_freq = ROPE_THETA ** (-jnp.arange(0, HEAD_DIM, 2, dtype=jnp.float32) / HEAD_DIM)
    ang = pos[:, None] * inv_freq[None, :]
    cos = jnp.cos(ang)[None, :, None, :]
    sin = jnp.sin(ang)[None, :, None, :]
    xf = x.astype(jnp.float32)
    x1, x2 = xf[..., :HEAD_DIM // 2], xf[..., HEAD_DIM // 2:]
    return jnp.concatenate([x1 * cos - x2 * sin, x2 * cos + x1 * sin], axis=-1).astype(x.dtype)


def project_inputs(h, pos, w_in, q_norm_g, k_norm_g):
    b, l = h.shape[0], h.shape[1]
    z = h @ w_in
    u, q, k, v, a_pool, a_attn = jnp.split(z, IN_SPLITS, axis=-1)
    q = rope(rmsnorm(q.reshape(b, l, N_HEADS, HEAD_DIM), q_norm_g), pos)
    k = rope(rmsnorm(k.reshape(b, l, N_HEADS, HEAD_DIM), k_norm_g), pos)
    v = v.reshape(b, l, N_HEADS, HEAD_DIM)
    return u, q, k, v, a_pool, a_attn


def pool_mix(ext, pos, w_pool_grp, pool_scale):
    b = ext.shape[0]
    l = ext.shape[1] - POOL_HIST
    ef = ext.astype(jnp.float32)
    cs = jnp.concatenate([jnp.zeros((b, 1, POOL_WIDTH), jnp.float32), jnp.cumsum(ef, axis=1)], axis=1)
    u = ef[:, POOL_HIST:]
    outs = []
    for gi, w in enumerate(POOL_WINDOWS):
        sl = slice(gi * POOL_GROUP, (gi + 1) * POOL_GROUP)
        wsum = cs[:, POOL_HIST + 1:POOL_HIST + 1 + l, sl] - cs[:, POOL_HIST + 1 - w:POOL_HIST + 1 - w + l, sl]
        cnt = jnp.minimum(pos + 1.0, float(w))[None, :, None]
        outs.append(wsum / cnt - u[..., sl])
    p = jnp.stack(outs, axis=2)
    y = jnp.einsum('blgc,gce->blge', p, w_pool_grp.astype(jnp.float32)).reshape(b, l, POOL_WIDTH)
    return (y * pool_scale.astype(jnp.float32)).astype(ext.dtype)


def band_dilated_attn(q, k, v, dil):
    b, s, h, dh = q.shape
    span = dil * BLK
    sp = -(-s // span) * span
    nb = sp // span

    def to_blocks(t):
        t = jnp.pad(t.astype(jnp.float32), ((0, 0), (0, sp - s), (0, 0), (0, 0)))
        t = jnp.moveaxis(t.reshape(b, sp // dil, dil, h, dh), 2, 1)
        return t.reshape(b, dil, nb, BLK, h, dh)

    def with_prev(t):
        prev = jnp.concatenate([jnp.zeros_like(t[:, :, :1]), t[:, :, :-1]], axis=2)
        return jnp.concatenate([prev, t], axis=3)

    qb = to_blocks(q)
    kk = with_prev(to_blocks(k))
    vv = with_prev(to_blocks(v))
    sc = jnp.einsum('brnqhd,brnkhd->brnhqk', qb, kk) / math.sqrt(dh)
    qi = jnp.arange(BLK)[:, None]
    kj = jnp.arange(2 * BLK)[None, :]
    dist = BLK + qi - kj
    band = (dist >= 0) & (dist <= N_KEYS)
    exists = (jnp.arange(nb)[:, None, None] * BLK - BLK + kj[None] >= 0)
    mask = (band[None] & exists)[None, None, :, None]
    sc = jnp.where(mask, sc, -jnp.inf)
    m = jnp.max(sc, axis=-1, keepdims=True)
    p = jnp.exp(sc - m)
    den = jnp.sum(p, axis=-1)
    o = jnp.einsum('brnhqk,brnkhd->brnqhd', p, vv) / jnp.swapaxes(den, -1, -2)[..., None]
    lse = jnp.swapaxes(m[..., 0] + jnp.log(den), -1, -2)

    def from_blocks(t):
        rest = t.shape[4:]
        t = jnp.moveaxis(t.reshape((b, dil, nb * BLK) + rest), 1, 2)
        return t.reshape((b, sp) + rest)[:, :s]

    return from_blocks(o), from_blocks(lse)


def gathered_dilated_attn(q, k_ext, v_ext, dil, hist):
    t = q.shape[1]
    idx = hist + jnp.arange(t)[:, None] - dil * jnp.arange(N_KEYS + 1)[None, :]
    valid = idx >= 0
    idxc = jnp.maximum(idx, 0)
    kg = k_ext[:, idxc].astype(jnp.float32)
    vg = v_ext[:, idxc].astype(jnp.float32)
    sc = jnp.einsum('bthd,btkhd->bthk', q.astype(jnp.float32), kg) / math.sqrt(HEAD_DIM)
    sc = jnp.where(valid[None, :, None, :], sc, -jnp.inf)
    m = jnp.max(sc, axis=-1, keepdims=True)
    p = jnp.exp(sc - m)
    den = jnp.sum(p, axis=-1)
    o = jnp.einsum('bthk,btkhd->bthd', p, vg) / den[..., None]
    return o, m[..., 0] + jnp.log(den)


def prompt_attend(g, q, k, v):
    win, dil = ATTN_PATTERNS[g]
    o, lse = band_dilated_attn(q, k, v, dil)
    keep = min(win, k.shape[1])
    return o, lse, k[:, k.shape[1] - keep:], v[:, v.shape[1] - keep:]


def sample_attend(cache_ks, cache_vs, g, q, k, v):
    win, dil = ATTN_PATTERNS[g]
    ck, cv = cache_ks[g], cache_vs[g]
    k_ext = jnp.concatenate([ck, k.astype(ck.dtype)], axis=1)
    v_ext = jnp.concatenate([cv, v.astype(cv.dtype)], axis=1)
    o, lse = gathered_dilated_attn(q, k_ext, v_ext, dil, ck.shape[1])
    return o, lse, k, v


def combine_groups(outs, lses):
    w = jax.nn.softmax(jnp.stack(lses, axis=0), axis=0)
    o = jnp.sum(w[..., None] * jnp.stack(outs, axis=0), axis=0)
    return o.reshape(o.shape[0], o.shape[1], ATTN_OUT)


def trunk_layer(x, c, pos, pool_prev, attend, norm1_g, norm2_g, w_ada, b_ada, w_in, q_norm_g, k_norm_g,
                w_pool_grp, pool_scale, w_pool_br, w_attn_br, w_out, w_up, w_down):
    mod = (jax.nn.silu(c) @ w_ada + b_ada)[:, None, :]
    sh1, sc1, g1, sh2, sc2, g2 = jnp.split(mod, 6, axis=-1)
    h = modulate(x, norm1_g, sh1, sc1)
    u, q, k, v, a_pool, a_attn = project_inputs(h, pos, w_in, q_norm_g, k_norm_g)
    ext = jnp.concatenate([pool_prev.astype(u.dtype), u], axis=1)
    pool_y = pool_mix(ext, pos, w_pool_grp, pool_scale)
    outs, lses, rows = [], [], []
    for g in range(N_GROUPS):
        hs = slice(g * HEADS_PER_GROUP, (g + 1) * HEADS_PER_GROUP)
        o, lse, kr, vr = attend(g, q[:, :, hs], k[:, :, hs], v[:, :, hs])
        outs.append(o)
        lses.append(lse)
        rows.append((kr, vr))
    attn_y = combine_groups(outs, lses).astype(x.dtype)
    merged = jax.nn.sigmoid(a_pool) * (pool_y @ w_pool_br) + jax.nn.sigmoid(a_attn) * (attn_y @ w_attn_br)
    x = x + g1 * (merged @ w_out)
    h2 = modulate(x, norm2_g, sh2, sc2)
    x = x + g2 * (jnp.square(jax.nn.relu(h2 @ w_up)) @ w_down)
    return x, rows, ext[:, ext.shape[1] - POOL_HIST:]


def setup_inputs(seed: int = 0) -> dict:
    key = jax.random.key(seed)
    ks = jax.random.split(key, 32)
    nrm = jax.random.normal
    f32 = jnp.float32
    inp = {}
    inp['x_prompt'] = nrm(ks[0], (BATCH, SEQ, D_MODEL), f32)
    inp['x_sample'] = nrm(ks[1], (DEC_BATCH, DEC_SEQ, D_MODEL), f32)
    for i, (win, dil) in enumerate(ATTN_PATTERNS):
        buf = min(win, PAST_LEN)
        inp['cache_k_w%d' % win] = nrm(ks[2 + 2 * i], (DEPTH, DEC_BATCH, buf, HEADS_PER_GROUP, HEAD_DIM), f32)
        inp['cache_v_w%d' % win] = nrm(ks[3 + 2 * i], (DEPTH, DEC_BATCH, buf, HEADS_PER_GROUP, HEAD_DIM), f32)
    inp['state_pool'] = nrm(ks[8], (DEPTH, DEC_BATCH, POOL_HIST, POOL_WIDTH), f32)
    inp['c_prompt'] = nrm(ks[9], (BATCH, D_MODEL), f32)
    inp['c_sample'] = nrm(ks[10], (DEC_BATCH, D_MODEL), f32)
    inp['norm1_g'] = 1.0 + 0.02 * nrm(ks[11], (DEPTH, D_MODEL), f32)
    inp['norm2_g'] = 1.0 + 0.02 * nrm(ks[12], (DEPTH, D_MODEL), f32)
    inp['w_ada'] = 0.3 * D_MODEL ** -0.5 * nrm(ks[13], (DEPTH, D_MODEL, 6 * D_MODEL), f32)
    inp['b_ada'] = 0.02 * nrm(ks[14], (DEPTH, 6 * D_MODEL), f32)
    inp['w_in'] = D_MODEL ** -0.5 * nrm(ks[15], (DEPTH, D_MODEL, IN_WIDTH), f32)
    inp['q_norm_g'] = 1.0 + 0.02 * nrm(ks[16], (DEPTH, HEAD_DIM), f32)
    inp['k_norm_g'] = 1.0 + 0.02 * nrm(ks[17], (DEPTH, HEAD_DIM), f32)
    inp['w_pool_grp'] = POOL_GROUP ** -0.5 * nrm(ks[18], (DEPTH, N_POOL_GROUPS, POOL_GROUP, POOL_GROUP), f32)
    inp['pool_scale'] = 1.0 + 0.02 * nrm(ks[19], (DEPTH, POOL_WIDTH), f32)
    inp['w_pool_br'] = POOL_WIDTH ** -0.5 * nrm(ks[20], (DEPTH, POOL_WIDTH, D_MODEL), f32)
    inp['w_attn_br'] = ATTN_OUT ** -0.5 * nrm(ks[21], (DEPTH, ATTN_OUT, D_MODEL), f32)
    inp['w_out'] = D_MODEL ** -0.5 * nrm(ks[22], (DEPTH, D_MODEL, D_MODEL), f32)
    inp['w_up'] = D_MODEL ** -0.5 * nrm(ks[23], (DEPTH, D_MODEL, D_FF), f32)
    inp['w_down'] = D_FF ** -0.5 * nrm(ks[24], (DEPTH, D_FF, D_MODEL), f32)
    return inp


def reference(x_prompt, x_sample, cache_k_w128, cache_v_w128, cache_k_w512, cache_v_w512,
              cache_k_w2048, cache_v_w2048, state_pool, c_prompt, c_sample, norm1_g, norm2_g,
              w_ada, b_ada, w_in, q_norm_g, k_norm_g, w_pool_grp, pool_scale, w_pool_br,
              w_attn_br, w_out, w_up, w_down):
    pos_p = jnp.arange(x_prompt.shape[1], dtype=jnp.float32)
    pos_s = PAST_LEN + jnp.arange(x_sample.shape[1], dtype=jnp.float32)
    caches_k = (cache_k_w128, cache_k_w512, cache_k_w2048)
    caches_v = (cache_v_w128, cache_v_w512, cache_v_w2048)
    kp = [[] for _ in range(N_GROUPS)]
    vp = [[] for _ in range(N_GROUPS)]
    ks_ = [[] for _ in range(N_GROUPS)]
    vs_ = [[] for _ in range(N_GROUPS)]
    pool_p_rows, pool_s_rows = [], []
    xp, xs = x_prompt, x_sample
    for l in range(DEPTH):
        weights = (norm1_g[l], norm2_g[l], w_ada[l], b_ada[l], w_in[l], q_norm_g[l], k_norm_g[l],
                   w_pool_grp[l], pool_scale[l], w_pool_br[l], w_attn_br[l], w_out[l], w_up[l], w_down[l])
        pool0 = jnp.zeros((xp.shape[0], POOL_HIST, POOL_WIDTH), xp.dtype)
        xp, rows_p, pool_p = trunk_layer(xp, c_prompt, pos_p, pool0, prompt_attend, *weights)
        attend_s = functools.partial(sample_attend, tuple(ck[l] for ck in caches_k),
                                     tuple(cv[l] for cv in caches_v))
        xs, rows_s, pool_s = trunk_layer(xs, c_sample, pos_s, state_pool[l], attend_s, *weights)
        for g in range(N_GROUPS):
            kp[g].append(rows_p[g][0])
            vp[g].append(rows_p[g][1])
            ks_[g].append(rows_s[g][0])
            vs_[g].append(rows_s[g][1])
        pool_p_rows.append(pool_p)
        pool_s_rows.append(pool_s)
    new_k_w128_p, new_k_w512_p, new_k_w2048_p = [jnp.stack(r, axis=0) for r in kp]
    new_v_w128_p, new_v_w512_p, new_v_w2048_p = [jnp.stack(r, axis=0) for r in vp]
    new_k_w128_s, new_k_w512_s, new_k_w2048_s = [jnp.stack(r, axis=0) for r in ks_]
    new_v_w128_s, new_v_w512_s, new_v_w2048_s = [jnp.stack(r, axis=0) for r in vs_]
    new_pool_p = jnp.stack(pool_p_rows, axis=0)
    new_pool_s = jnp.stack(pool_s_rows, axis=0)
    return (xp, xs,
            new_k_w128_p, new_v_w128_p, new_k_w512_p, new_v_w512_p, new_k_w2048_p, new_v_w2048_p, new_pool_p,
            new_k_w128_s, new_v_w128_s, new_k_w512_s, new_v_w512_s, new_k_w2048_s, new_v_w2048_s, new_pool_s)
```

```python
import contextlib
import os
import numpy as np
import concourse.bass as bass
import concourse.mybir as mybir
from concourse.bass_utils import run_bass_kernel_spmd

F32 = mybir.dt.float32
BF16 = mybir.dt.bfloat16
ALU = mybir.AluOpType
AF = mybir.ActivationFunctionType
AX = mybir.AxisListType
PE, ACT, DVE, POOL, SP = "pe", "act", "dve", "pool", "sp"
ENGS = (PE, ACT, DVE, POOL, SP)

DEPTH = 4
D = 1024
NT = 16
TOK = 2048
NS = 4
TT = TOK + NS
INW = 4864
DFF = 4096
EPS = 1e-6
WINS = (128, 512, 2048)
DILS = (1, 4, 16)
NBLK_SEND = (1, 4, 16)
RG = [[0, 1], [2, 3], [4, 5], [6, 7]]
NEG = -30000.0
STAGE = float(os.environ.get('KSTAGE', '99'))
NLAYER = int(os.environ.get('KLAYERS', str(DEPTH)))


class _Stop(Exception):
    pass


def stage(n):
    if STAGE <= n:
        raise _Stop()


class Res:
    __slots__ = ("name", "writers", "readers")

    def __init__(self, name):
        self.name = name
        self.writers = {}
        self.readers = {}

    def inherit(self, other):
        for d in (other.writers, other.readers):
            for k, o in d.items():
                c = self.writers.get(k)
                if c is None or c.seq < o.seq:
                    self.writers[k] = o


class Op:
    __slots__ = ("eng", "fn", "deps", "token", "needs_inc", "is_dma", "key", "inc", "seq")

    def __init__(self, eng, fn, is_dma=False):
        self.eng = eng
        self.fn = fn
        self.deps = []
        self.token = None
        self.needs_inc = False
        self.is_dma = is_dma
        self.key = eng
        self.inc = 16


class Sched:
    SEM_ROT = 30000
    NDMA = {SP: 14, POOL: 6, ACT: 2}

    def __init__(self):
        self.ops = {e: [] for e in ENGS}
        self.dma_rr = {q: 0 for q in self.NDMA}
        self.dma_last = {}
        self.dma_val = {}
        self.semnames = set()
        self.nseq = 0

    def op(self, eng, fn, reads=(), writes=(), dma=False, inc=16, semname=None):
        o = Op(eng, fn, is_dma=dma)
        o.inc = inc
        self.nseq += 1
        o.seq = self.nseq
        deps = []
        if dma:
            if semname is None:
                k = self.dma_rr[eng]
                self.dma_rr[eng] = (k + 1) % self.NDMA[eng]
                sname = "dq_%s_%d" % (eng, k)
            else:
                sname = semname
            self.semnames.add(sname)
            self.dma_val[sname] = self.dma_val.get(sname, 0) + inc
            o.token = (sname, self.dma_val[sname])
            o.key = sname
            o.needs_inc = True
            prev = self.dma_last.get(sname)
            if prev is not None:
                deps.append(prev)
            self.dma_last[sname] = o
        for r in reads:
            deps.extend(r.writers.values())
        for r in writes:
            deps.extend(r.writers.values())
            deps.extend(r.readers.values())
        for r in reads:
            r.readers[o.key] = o
        for r in writes:
            r.readers = {}
            r.writers[o.key] = o
        seen = set()
        for d in deps:
            if id(d) in seen or d is o:
                continue
            seen.add(id(d))
            if d.eng == PE and eng == PE and not d.is_dma and not dma:
                continue
            o.deps.append(d)
            if not d.is_dma:
                d.needs_inc = True
        self.ops[eng].append(o)
        return o

    def finalize(self):
        for e in ENGS:
            c = 0
            idx = 0
            for o in self.ops[e]:
                if o.is_dma or not o.needs_inc:
                    continue
                if c >= self.SEM_ROT:
                    idx += 1
                    c = 0
                c += 1
                sname = "pg_%s_%d" % (e, idx)
                self.semnames.add(sname)
                o.token = (sname, c)

    def emit_engine(self, eng, engobj, sems):
        waited = {}
        for o in self.ops[eng]:
            need = {}
            for d in o.deps:
                s, v = d.token
                if waited.get(s, 0) >= v:
                    continue
                if need.get(s, 0) < v:
                    need[s] = v
            for s, v in need.items():
                engobj.wait_ge(sems[s], v)
                waited[s] = v
            ins = o.fn(engobj)
            if o.needs_inc:
                s, v = o.token
                ins.then_inc(sems[s], o.inc if o.is_dma else 1)
        last = {}
        for o in self.ops[eng]:
            if o.is_dma:
                last[o.token[0]] = o.token[1]
        for s, v in last.items():
            if waited.get(s, 0) < v:
                engobj.wait_ge(sems[s], v)


class Buf:
    def __init__(self, t, off, size, nres, name):
        self.t = t
        self.off = off
        self.size = size
        self.res = [Res("%s_%d" % (name, i)) for i in range(nres)]

    def __getitem__(self, k):
        return self.t[k]


def build_program():
    nc = bass.Bass("TRN2", target_bir_lowering=False)
    S = Sched()

    def din(name, shape, dt=F32):
        return nc.dram_tensor(name, list(shape), dt, kind="ExternalInput").ap()

    def dout(name, shape, dt=F32):
        return nc.dram_tensor(name, list(shape), dt, kind="ExternalOutput").ap()

    xp = din("xp", [TOK, D])
    xs = din("xs", [NS, D])
    cpT = din("cpT", [128, 8])
    csT = din("csT", [128, 8, NS])
    ck = [din("ck%d" % g, [NLAYER, NS, WINS[g], 256]) for g in range(3)]
    cv = [din("cv%d" % g, [NLAYER, NS, WINS[g], 256]) for g in range(3)]
    spool = din("spool", [NLAYER, NS * 15, 512])
    norm1_g = din("norm1_g", [NLAYER, D])
    norm2_g = din("norm2_g", [NLAYER, D])
    w_ada = din("w_ada", [NLAYER, D, 6 * D])
    b_ada = din("b_ada", [NLAYER, 6 * D])
    w_in = din("w_in", [NLAYER, D, INW])
    qk_g = din("qk_g", [NLAYER, 128])
    w_pool_grp = din("w_pool_grp", [NLAYER, 4, 128, 128])
    pscT = din("pscT", [NLAYER, 128, 4])
    w_pool_br = din("w_pool_br", [NLAYER, 512, D])
    w_attn_br = din("w_attn_br", [NLAYER, 256, D])
    w_out = din("w_out", [NLAYER, D, D])
    w_up = din("w_up", [NLAYER, D, DFF])
    w_down = din("w_down", [NLAYER, DFF, D])
    c_ident = din("c_ident", [128, 128])
    c_mask = din("c_mask", [128, 3, 512])
    c_rope = din("c_rope", [48, 128, 64])
    c_ropes = din("c_ropes", [NS, 64])
    c_amat = din("c_amat", [128, 4, 4, 128])
    c_sel = din("c_sel", [NS, NS, 128])
    c_selc = din("c_selc", [128, NS, NS])
    c_psel = din("c_psel", [NS * 15, 4, NS])
    c_pcoef = din("c_pcoef", [NS, 512])

    yp = dout("yp", [TOK, D])
    ys = dout("ys", [NS, D])
    kp = [dout("kp%d" % g, [NLAYER, WINS[g], 256]) for g in range(3)]
    vp = [dout("vp%d" % g, [NLAYER, WINS[g], 256]) for g in range(3)]
    poolp = dout("poolp", [NLAYER, 15, 512])
    ks = [dout("ks%d" % g, [NLAYER, NS, 256]) for g in range(3)]
    vs = [dout("vs%d" % g, [NLAYER, NS, 256]) for g in range(3)]
    pools = dout("pools", [NLAYER, NS, 15, 512])

    NSB = 16
    send = [nc.dram_tensor("send_%d" % l, [NSB * 128, 512], BF16).ap() for l in range(NLAYER)]
    recv = [nc.dram_tensor("recv_%d" % l, [2 * NSB * 128, 512], BF16).ap() for l in range(NLAYER)]
    sendb = [nc.dram_tensor("sendb_%d" % l, [NSB * 128, 512], BF16).ap() for l in range(NLAYER)]
    recvb = [nc.dram_tensor("recvb_%d" % l, [2 * NSB * 128, 512], BF16).ap() for l in range(NLAYER)]
    R_send = [Res("send") for l in range(NLAYER)]
    R_recv = [Res("recv") for l in range(NLAYER)]
    R_sendb = [Res("sendb") for l in range(NLAYER)]
    R_recvb = [Res("recvb") for l in range(NLAYER)]

    es = contextlib.ExitStack()
    with es:
        base0 = (nc.sbuf_base + 63) // 64 * 64
        top = nc.sbuf_top
        allbufs = []
        cnt = [0]

        def alloc(name, shape, dt, off, nres=1):
            nb = int(np.prod(shape[1:])) * (2 if dt == BF16 else 4)
            assert off % 32 == 0, (name, off)
            assert off + nb <= top, (name, off, nb, top)
            cnt[0] += 1
            t = nc.alloc_sbuf_tensor_at("%s_%d" % (name, cnt[0]), list(shape), dt, offset=off)
            b = Buf(t, off, nb, nres, name)
            for o in allbufs:
                if o.off < off + nb and off < o.off + o.size:
                    for r in b.res:
                        for ro in o.res:
                            r.inherit(ro)
            allbufs.append(b)
            return b

        cur = [base0]

        def palloc(name, shape, dt, nres=1):
            nb = int(np.prod(shape[1:])) * (2 if dt == BF16 else 4)
            nb = (nb + 31) // 32 * 32
            b = alloc(name, shape, dt, cur[0], nres)
            cur[0] += nb
            return b

        X = palloc("X", [128, NT, D], F32, nres=NT)
        XS = palloc("XS", [NS, D], F32)
        HT = palloc("HT", [128, 8, TT], BF16, nres=NT + 1)
        WR = [palloc("WR%d" % i, [128, 6144], BF16) for i in range(2)]
        IDB = palloc("IDB", [128, 128], BF16)
        MASK = palloc("MASK", [128, 3, 512], BF16)
        ONES = palloc("ONES", [128, 64], BF16)
        SCP = palloc("SCP", [128, 8, 128], BF16)
        SCS = palloc("SCS", [128, 8, NS], BF16)
        SEL = palloc("SEL", [NS, NS, 128], BF16)
        SELC = palloc("SELC", [128, NS, NS], F32)
        PSEL = palloc("PSEL", [NS * 15, 4, NS], F32)
        PCOEF = palloc("PCOEF", [NS, 512], F32)
        EPSB = palloc("EPSB", [128, 1], F32)
        GQK = palloc("GQK", [128, 2, 64], F32)
        PSC = palloc("PSC", [128, 4], F32)
        WG = palloc("WG", [128, 4, 128], BF16)
        ROPES = palloc("ROPES", [NS, 64], F32)
        SS = [palloc("SS%d" % i, [128, 8], F32) for i in range(2)]
        QS = palloc("QS", [NS, 3, 256], BF16)
        SSELF = palloc("SSELF", [NS, 8], F32)
        SACC = palloc("SACC", [NS, 260], F32)
        CCD = palloc("CCD", [128, 8], F32)
        ARENA = (cur[0] + 63) // 64 * 64
        AR_SIZE = top - ARENA
        R1 = ARENA
        R2 = R1 + 32832 + 64
        R3 = R2 + 16416 + 32
        R4 = R3 + 8224
        assert R4 + 14 * 1024 <= top, (R4, top)

        psall = es.enter_context(nc.psum_tensor("psall", [128, 8, 512], F32))
        PB = [Res("bank%d" % i) for i in range(8)]
        pbi = [0]

        def bank():
            i = pbi[0]
            pbi[0] = (i + 1) % 8
            return psall[:, i, :], PB[i]

        def dma(q, out, in_, reads=(), writes=()):
            return S.op(q, lambda e: e.dma_start(out=out, in_=in_), reads, writes, dma=True)

        def mm(out, lhsT, rhs, start, stop, reads, writes):
            return S.op(PE, lambda e: e.matmul(out, lhsT=lhsT, rhs=rhs, start=start, stop=stop), reads, writes)

        def tr(out, in_, ident, reads, writes):
            return S.op(PE, lambda e: e.transpose(out=out, in_=in_, identity=ident), reads, writes)

        def act(out, in_, func, reads, writes, **kw):
            return S.op(ACT, lambda e: e.activation(out=out, in_=in_, func=func, **kw), reads, writes)

        def cp(eng, out, in_, reads, writes):
            if eng == ACT:
                return S.op(ACT, lambda e: e.copy(out=out, in_=in_), reads, writes)
            return S.op(eng, lambda e: e.tensor_copy(out=out, in_=in_), reads, writes)

        def tt(eng, out, in0, in1, op, reads, writes):
            return S.op(eng, lambda e: e.tensor_tensor(out=out, in0=in0, in1=in1, op=op), reads, writes)

        def stt(eng, out, in0, scalar, in1, op0, op1, reads, writes):
            return S.op(eng, lambda e: e.scalar_tensor_tensor(out=out, in0=in0, scalar=scalar, in1=in1,
                                                              op0=op0, op1=op1), reads, writes)

        def red(eng, out, in_, reads, writes):
            return S.op(eng, lambda e: e.tensor_reduce(out=out, in_=in_, axis=AX.X, op=ALU.add), reads, writes)

        def recip(out, in_, reads, writes):
            return S.op(DVE, lambda e: e.reciprocal(out=out, in_=in_), reads, writes)

        def mset(eng, ap, v, writes):
            return S.op(eng, lambda e: e.memset(ap, v), (), writes)

        def kcv(ap2d):
            return ap2d.rearrange("(kc k) n -> k kc n", k=128)

        wri = [0]

        def wslot():
            i = wri[0]
            wri[0] = (i + 1) % len(WR)
            return WR[i]

        stg_off = R1
        STG = alloc("STG", [128, 2048], F32, stg_off)
        dma(SP, STG[:, 0:128], c_ident, (), STG.res)
        cp(DVE, IDB[:, :], STG[:, 0:128], STG.res, IDB.res)
        dma(SP, STG[:, 0:1536], c_mask.rearrange("p a b -> p (a b)"), (), STG.res)
        cp(DVE, MASK[:, :, :].rearrange("p a b -> p (a b)"), STG[:, 0:1536], STG.res, MASK.res)
        mset(DVE, ONES[:, :], 1.0, ONES.res)
        mset(DVE, EPSB[:, :], EPS, EPSB.res)
        dma(SP, STG[:, 0:8], cpT, (), STG.res)
        act(STG[:, 8:16], STG[:, 0:8], AF.Silu, STG.res, STG.res)
        cp(DVE, SCP[:, :, :], STG[:, 8:16].unsqueeze(2).to_broadcast([128, 8, 128]), STG.res, SCP.res)
        dma(SP, STG[:, 0:32], csT.rearrange("p a b -> p (a b)"), (), STG.res)
        act(SCS[:, :, :].rearrange("p a b -> p (a b)"), STG[:, 0:32], AF.Silu, STG.res, SCS.res)
        dma(SP, STG[0:NS, 0:512], c_sel.rearrange("p a b -> p (a b)"), (), STG.res)
        cp(DVE, SEL[:, :, :].rearrange("p a b -> p (a b)"), STG[0:NS, 0:512], STG.res, SEL.res)
        dma(SP, SELC[:, :, :], c_selc, (), SELC.res)
        dma(SP, PSEL[:, :, :], c_psel, (), PSEL.res)
        dma(SP, PCOEF[:, :], c_pcoef, (), PCOEF.res)
        dma(SP, ROPES[:, :], c_ropes, (), ROPES.res)
        for t in range(NT):
            dma(SP, X[:, t, :], xp[t * 128:(t + 1) * 128, :], (), [X.res[t]])
        dma(SP, XS[:, :], xs, (), XS.res)

        def xtile(t):
            if t < NT:
                return X[:, t, :], X.res[t], 128
            return XS[:, :], XS.res[0], NS

        def tcols(t):
            if t < NT:
                return slice(t * 128, (t + 1) * 128)
            return slice(TOK, TT)

        def ada(l, j, MP, MS, BB, NG, scale_norm):
            dma(SP, BB[:, :], b_ada[l:l + 1, j * D:(j + 1) * D].partition_broadcast(128), (), BB.res)
            if scale_norm is not None:
                dma(SP, NG[:, :], scale_norm[l:l + 1, :].partition_broadcast(128), (), NG.res)
            for c in range(2):
                W = wslot()
                wv = W[:, 0:4096].rearrange("p (a b) -> p a b", a=8)
                dma(POOL, wv, kcv(w_ada[l])[:, :, j * D + c * 512: j * D + (c + 1) * 512], (), W.res)
                for (sc, M, np_) in ((SCP, MP, 128), (SCS, MS, NS)):
                    pb, pr = bank()
                    for kc in range(8):
                        mm(pb[0:np_, :], sc[:, kc, :], wv[:, kc, :], kc == 0, kc == 7, W.res + sc.res, [pr])
                    cs = slice(c * 512, (c + 1) * 512)
                    tt(DVE, M[0:np_, cs], pb[0:np_, :], BB[0:np_, cs], ALU.add, [pr] + BB.res, M.res)
                    if scale_norm is not None:
                        stt(DVE, M[0:np_, cs], M[0:np_, cs], 1.0, NG[0:np_, cs], ALU.add, ALU.mult,
                            M.res + NG.res, M.res)

        def norm_phase(GP, SHP, GS, SHS, toff):
            TMP = [alloc("NTMP%d" % i, [128, D], F32, toff + i * 6144) for i in range(2)]
            HB = [alloc("NHB%d" % i, [128, D], BF16, toff + i * 6144 + 4096) for i in range(2)]
            for t in range(NT + 1):
                xt, xr, np_ = xtile(t)
                G, SH = (GP, SHP) if t < NT else (GS, SHS)
                tmp = TMP[t % 2]
                hb = HB[t % 2]
                ss = SS[t % 2]
                mset(DVE, ss[:, 0:1], 0.0, ss.res)
                act(hb[0:np_, :], xt, AF.Square, [xr], hb.res + ss.res, accum_out=ss[0:np_, 0:1])
                act(ss[0:np_, 0:1], ss[0:np_, 0:1], AF.Ln, ss.res + EPSB.res, ss.res, scale=1.0 / D,
                    bias=EPSB[0:np_, 0:1])
                act(ss[0:np_, 0:1], ss[0:np_, 0:1], AF.Exp, ss.res, ss.res, scale=-0.5)
                stt(DVE, tmp[0:np_, :], xt, ss[0:np_, 0:1], G[0:np_, :], ALU.mult, ALU.mult,
                    [xr] + ss.res + G.res, tmp.res)
                tt(DVE, hb[0:np_, :], tmp[0:np_, :], SH[0:np_, :], ALU.add, tmp.res + SH.res, hb.res)
                pb, pr = bank()
                pbb = pb.bitcast(BF16)
                for kc in range(8):
                    tr(pbb[:, kc * 128: kc * 128 + np_], hb[0:np_, kc * 128:(kc + 1) * 128], IDB[0:np_, 0:np_],
                       hb.res + IDB.res, [pr])
                cp(ACT, HT[:, :, tcols(t)], pbb[:, 0:1024].rearrange("p (a b) -> p a b", a=8)[:, :, 0:np_],
                   [pr], [HT.res[t]])

        for l in range(NLAYER):
          try:
            MPA = alloc("MPA", [128, D], F32, R2)
            MPB = alloc("MPB", [128, D], F32, R2 + 4096)
            MSA = alloc("MSA", [NS, D], F32, R2 + 8192)
            MSB = alloc("MSB", [NS, D], F32, R2 + 12288)
            BB = alloc("BB", [128, D], F32, R3)
            NG = alloc("NG", [128, D], F32, R3 + 4096)
            ada(l, 0, MPB, MSB, BB, NG, None)
            ada(l, 1, MPA, MSA, BB, NG, norm1_g)
            stage(1)
            norm_phase(MPA, MPB, MSA, MSB, R4)

            stage(2)
            ACC = alloc("ACC", [128, 4, TOK], F32, R1, nres=NT)
            KV = [alloc("KV%d" % i, [128, 512], BF16, R2 + i * 1024) for i in range(8)]
            H0 = alloc("H0", [128, 512], BF16, R2 + 8192)
            H1 = [alloc("H1_%d" % i, [128, 512], BF16, R2 + 9216 + i * 1024) for i in range(4)]
            H2 = [alloc("H2_%d" % i, [128, 512], BF16, R2 + 13312 + i * 1024) for i in range(3)]
            o = R3
            QK = [alloc("QK%d" % i, [128, 512], F32, o + i * 2048) for i in range(2)]
            o += 4096
            QO = [alloc("QO%d" % i, [128, 512], F32, o + i * 2048) for i in range(2)]
            o += 4096
            SQ = alloc("SQ", [128, 512], F32, o)
            o += 2048
            TA = alloc("TA", [128, 256], F32, o)
            o += 1024
            TB = alloc("TB", [128, 256], F32, o)
            o += 1024
            QB = [alloc("QB%d" % i_, [128, 512], BF16, o + i_ * 1024) for i_ in range(2)]
            o += 2048
            VF = [alloc("VF%d" % i, [128, 256], F32, o + i * 1024) for i in range(2)]
            o += 2048
            QT = [alloc("QT%d" % i, [128, 2, 128], BF16, o + i * 512) for i in range(2)]
            o += 1024
            PTS = [alloc("PTS%d" % i, [128, 1024], BF16, o + i * 2048) for i in range(2)]
            o += 4096
            ROPE = [alloc("ROPE%d" % i, [128, 64], F32, o + i * 256) for i in range(4)]
            o += 1024
            assert o <= top, (o, top)
            dma(SP, GQK[:, :, :].rearrange("p a b -> p (a b)"), qk_g[l:l + 1, :].partition_broadcast(128), (), GQK.res)
            mset(DVE, SACC[:, :], 0.0, SACC.res)

            blkc = [0]

            def blk_cols(g, b):
                if g == 0:
                    return slice(128 * b, 128 * b + 128), [b]
                if g == 1:
                    n1, r1 = b // 4, b % 4
                    return slice(512 * n1 + r1, 512 * n1 + 512, 4), list(range(4 * n1, 4 * n1 + 4))
                return slice(b, TOK, 16), list(range(NT))

            def load_wq(g):
                W = wslot()
                wv = W[:, :].rearrange("p (a b) -> p a b", a=8)
                for i, c0 in enumerate((512, 1280, 2048)):
                    dma(POOL, wv[:, :, i * 256:(i + 1) * 256], kcv(w_in[l])[:, :, c0 + 256 * g: c0 + 256 * g + 256],
                        (), W.res)
                return W, wv

            def project(g, b, W, wv, kvslot, out_rows=None, sample=False):
                i = blkc[0]
                blkc[0] += 1
                qk, qo, vf, qt, rp, ss = QK[i % 2], QO[i % 2], VF[i % 2], QT[i % 2], ROPE[i % 4], SS[i % 2]
                if sample:
                    np_ = NS
                    cols, tl = slice(TOK, TT), [NT]
                    ropeap, roper = ROPES, ROPES.res
                else:
                    np_ = 128
                    cols, tl = blk_cols(g, b)
                    dma(SP, rp[:, :], c_rope[g * 16 + b], (), rp.res)
                    ropeap, roper = rp, rp.res
                hres = [HT.res[t] for t in tl]
                p1, r1_ = bank()
                p2, r2_ = bank()
                for kc in range(8):
                    mm(p1[0:np_, :], HT[:, kc, cols], wv[:, kc, 0:512], kc == 0, kc == 7, hres + W.res, [r1_])
                for kc in range(8):
                    mm(p2[0:np_, 0:256], HT[:, kc, cols], wv[:, kc, 512:768], kc == 0, kc == 7, hres + W.res, [r2_])
                P = slice(0, np_)
                cp(DVE, qk[P, :], p1[P, :], [r1_], qk.res)
                KP = int(os.environ.get("KPROJ", "9"))
                if KP <= 1:
                    return None
                tt(DVE, SQ[P, :], qk[P, :], qk[P, :], ALU.mult, qk.res, SQ.res)
                red(DVE, ss[P, 0:8], SQ[P, :].rearrange("p (h d) -> p h d", h=8), SQ.res, ss.res)
                act(ss[P, 0:8], ss[P, 0:8], AF.Ln, ss.res + EPSB.res, ss.res, scale=1.0 / 64, bias=EPSB[P, 0:1])
                act(ss[P, 0:8], ss[P, 0:8], AF.Exp, ss.res, ss.res, scale=-0.5)
                tt(DVE, qk[P, :].rearrange("p (a h d) -> p a h d", a=2, h=4),
                   qk[P, :].rearrange("p (a h d) -> p a h d", a=2, h=4),
                   GQK[P, :, :].unsqueeze(2).to_broadcast([np_, 2, 4, 64]), ALU.mult, qk.res + GQK.res, qk.res)
                qv = qk[P, :].rearrange("p (h d) -> p h d", h=8)
                ov = qo[P, :].rearrange("p (h d) -> p h d", h=8)
                cosb = ropeap[P, 0:32].unsqueeze(1).to_broadcast([np_, 8, 32])
                sinb = ropeap[P, 32:64].unsqueeze(1).to_broadcast([np_, 8, 32])
                ta = TA[P, :].rearrange("p (h d) -> p h d", h=8)
                tb = TB[P, :].rearrange("p (h d) -> p h d", h=8)
                tt(DVE, ta, qv[:, :, 0:32], cosb, ALU.mult, qk.res + roper, TA.res)
                tt(DVE, tb, qv[:, :, 32:64], sinb, ALU.mult, qk.res + roper, TB.res)
                tt(DVE, ov[:, :, 0:32], ta, tb, ALU.subtract, TA.res + TB.res, qo.res)
                tt(DVE, ta, qv[:, :, 32:64], cosb, ALU.mult, qk.res + roper, TA.res)
                tt(DVE, tb, qv[:, :, 0:32], sinb, ALU.mult, qk.res + roper, TB.res)
                tt(DVE, ov[:, :, 32:64], ta, tb, ALU.add, TA.res + TB.res, qo.res)
                tt(DVE, ov, ov, ss[P, 0:8].unsqueeze(2).to_broadcast([np_, 8, 64]), ALU.mult, qo.res + ss.res, qo.res)
                cp(ACT, vf[P, :], p2[P, 0:256], [r2_], vf.res)
                if KP <= 2:
                    return None
                if sample:
                    cp(DVE, QS[:, g, :], qo[P, 0:256], qo.res, QS.res)
                    tt(DVE, TA[P, :], qo[P, 0:256], qo[P, 256:512], ALU.mult, qo.res, TA.res)
                    red(DVE, SSELF[:, 0:4], TA[P, :].rearrange("p (h d) -> p h d", h=4), TA.res, SSELF.res)
                    act(SSELF[:, 4:8], SSELF[:, 0:4], AF.Exp, SSELF.res, SSELF.res, scale=0.125)
                    tt(DVE, TA[P, :].rearrange("p (h d) -> p h d", h=4), vf[P, :].rearrange("p (h d) -> p h d", h=4),
                       SSELF[:, 4:8].unsqueeze(2).to_broadcast([NS, 4, 64]), ALU.mult, vf.res + SSELF.res, TA.res)
                    tt(DVE, SACC[:, 0:256], SACC[:, 0:256], TA[P, :], ALU.add, SACC.res + TA.res, SACC.res)
                    tt(DVE, SACC[:, 256:260], SACC[:, 256:260], SSELF[:, 4:8], ALU.add, SACC.res + SSELF.res, SACC.res)
                    dma(SP, ks[g][l], qo[P, 256:512], qo.res, ())
                    dma(SP, vs[g][l], vf[P, :], vf.res, ())
                    return None
                cp(DVE, QB[0][:, :], qo[:, :], qo.res, QB[0].res)
                cp(ACT, kvslot[:, 256:512], p2[:, 0:256], [r2_], kvslot.res)
                pt, rt = bank()
                ptb = pt.bitcast(BF16)
                for j in range(4):
                    tr(ptb[:, j * 128:(j + 1) * 128], QB[0][:, j * 128:(j + 1) * 128], IDB[:, :], QB[0].res + IDB.res, [rt])
                cp(ACT, qt[:, :, :].rearrange("p a b -> p (a b)"), ptb[:, 0:256], [rt], qt.res)
                cp(ACT, kvslot[:, 0:256], ptb[:, 256:512], [rt], kvslot.res)
                if KP <= 3:
                    return qt
                if out_rows is not None:
                    kd, vd = out_rows
                    dma(SP, kd, qo[:, 256:512], qo.res, ())
                    dma(SP, vd, vf[:, :], vf.res, ())
                return qt

            def attend(g, b, qt, kvc, kvp, mprev, first):
                i = blkc[0]
                pts = PTS[i % 2]
                cols, tl = blk_cols(g, b)
                for half in range(2):
                    rows = slice(64 * half, 64 * half + 64)
                    pb, pr = bank()
                    mm(pb[:, :], IDB[:, :], MASK[:, mprev, :], True, False, IDB.res + MASK.res, [pr])
                    for j, (kvb, pair) in enumerate(((kvp, 0), (kvp, 1), (kvc, 0), (kvc, 1))):
                        mm(pb[:, 128 * j:128 * j + 128], kvb[rows, pair * 128:(pair + 1) * 128], qt[rows, pair, :],
                           False, j == 3, kvb.res + qt.res, [pr])
                    act(pts[:, half * 512:(half + 1) * 512], pb[:, :], AF.Exp, [pr], pts.res, scale=0.125)
                po, pro = bank()
                for h in range(4):
                    pair, half = h // 2, h % 2
                    rows = slice(64 * half, 64 * half + 64)
                    vsl = slice(256 + 64 * h, 256 + 64 * h + 64)
                    pp = pts[:, half * 512 + pair * 128:half * 512 + pair * 128 + 128]
                    pc = pts[:, half * 512 + 256 + pair * 128:half * 512 + 256 + pair * 128 + 128]
                    mm(po[rows, pair * 128:(pair + 1) * 128], kvp[:, vsl], pp, True, False, kvp.res + pts.res, [pro])
                    mm(po[rows, pair * 128:(pair + 1) * 128], kvc[:, vsl], pc, False, True, kvc.res + pts.res, [pro])
                    mm(po[rows, 256 + pair * 128:256 + (pair + 1) * 128], ONES[:, :], pp, True, False,
                       ONES.res + pts.res, [pro])
                    mm(po[rows, 256 + pair * 128:256 + (pair + 1) * 128], ONES[:, :], pc, False, True,
                       ONES.res + pts.res, [pro])
                av = ACC[:, :, cols]
                ares = [ACC.res[t] for t in tl]
                pov = po[:, :].rearrange("p (a b) -> p a b", a=4)
                if first:
                    cp(DVE, av, pov, [pro], ares)
                else:
                    tt(DVE, av, av, pov, ALU.add, [pro] + ares, ares)

            def block_task(g, b, W, wv, kvslot, orows=None, send_dst=None, prev=None, mprev=1, first=False,
                           hist=None):
                i = blkc[0]
                blkc[0] += 1
                qk, qo, vf, qt, rp, ss = QK[i % 2], QO[i % 2], VF[i % 2], QT[i % 2], ROPE[i % 4], SS[i % 2]
                pts = PTS[i % 2]
                cols, tl = blk_cols(g, b)
                if hist is not None:
                    hbuf, hsrc, hres_ = hist
                    dma(SP, hbuf[:, :], hsrc, [hres_], hbuf.res)
                dma(SP, rp[:, :], c_rope[g * 16 + b], (), rp.res)
                hres = [HT.res[t] for t in tl]
                p1, r1_ = bank()
                p2, r2_ = bank()
                for kc in range(8):
                    mm(p1[:, :], HT[:, kc, cols], wv[:, kc, 0:512], kc == 0, kc == 7, hres + W.res, [r1_])
                for kc in range(8):
                    mm(p2[:, 0:256], HT[:, kc, cols], wv[:, kc, 512:768], kc == 0, kc == 7, hres + W.res, [r2_])
                cp(ACT, qk[:, :], p1[:, :], [r1_], qk.res)
                act(SQ[:, :], p1[:, :], AF.Square, [r1_], SQ.res)
                red(DVE, ss[:, 0:8], SQ[:, :].rearrange("p (h d) -> p h d", h=8), SQ.res, ss.res)
                act(ss[:, 0:8], ss[:, 0:8], AF.Ln, ss.res + EPSB.res, ss.res, scale=1.0 / 64, bias=EPSB[:, 0:1])
                act(ss[:, 0:8], ss[:, 0:8], AF.Exp, ss.res, ss.res, scale=-0.5)
                tt(DVE, qk[:, :].rearrange("p (a h d) -> p a h d", a=2, h=4),
                   qk[:, :].rearrange("p (a h d) -> p a h d", a=2, h=4),
                   GQK[:, :, :].unsqueeze(2).to_broadcast([128, 2, 4, 64]), ALU.mult, qk.res + GQK.res, qk.res)
                qv = qk[:, :].rearrange("p (h d) -> p h d", h=8)
                ov = qo[:, :].rearrange("p (h d) -> p h d", h=8)
                cosb = rp[:, 0:32].unsqueeze(1).to_broadcast([128, 8, 32])
                sinb = rp[:, 32:64].unsqueeze(1).to_broadcast([128, 8, 32])
                ta = TA[:, :].rearrange("p (h d) -> p h d", h=8)
                tb = TB[:, :].rearrange("p (h d) -> p h d", h=8)
                tc_ = SQ[:, 0:256].rearrange("p (h d) -> p h d", h=8)
                td_ = SQ[:, 256:512].rearrange("p (h d) -> p h d", h=8)
                tt(DVE, ta, qv[:, :, 0:32], cosb, ALU.mult, qk.res + rp.res, TA.res)
                tt(DVE, tb, qv[:, :, 32:64], sinb, ALU.mult, qk.res + rp.res, TB.res)
                tt(DVE, tc_, qv[:, :, 32:64], cosb, ALU.mult, qk.res + rp.res, SQ.res)
                tt(DVE, td_, qv[:, :, 0:32], sinb, ALU.mult, qk.res + rp.res, SQ.res)
                tt(DVE, ov[:, :, 32:64], tc_, td_, ALU.add, SQ.res, qo.res)
                tt(DVE, ov[:, :, 0:32], ta, tb, ALU.subtract, TA.res + TB.res, qo.res)
                tt(DVE, ov, ov, ss[:, 0:8].unsqueeze(2).to_broadcast([128, 8, 64]), ALU.mult, qo.res + ss.res, qo.res)
                cp(ACT, vf[:, :], p2[:, 0:256], [r2_], vf.res)
                cp(ACT, kvslot[:, 256:512], p2[:, 0:256], [r2_], kvslot.res)
                cp(ACT, QB[i % 2][:, :], qo[:, :], qo.res, QB[i % 2].res)
                yield
                pt, rt = bank()
                ptb = pt.bitcast(BF16)
                qb = QB[i % 2]
                for j in range(4):
                    tr(ptb[:, j * 128:(j + 1) * 128], qb[:, j * 128:(j + 1) * 128], IDB[:, :], qb.res + IDB.res, [rt])
                cp(ACT, qt[:, :, :].rearrange("p a b -> p (a b)"), ptb[:, 0:256], [rt], qt.res)
                cp(ACT, kvslot[:, 0:256], ptb[:, 256:512], [rt], kvslot.res)
                if orows is not None:
                    kd, vd = orows
                    dma(SP, kd, qo[:, 256:512], qo.res, ())
                    dma(SP, vd, vf[:, :], vf.res, ())
                if send_dst is not None:
                    sd, rs = send_dst
                    dma(SP, sd, kvslot[:, :], kvslot.res, [rs])
                yield
                if prev is None:
                    return
                kvc, kvp = kvslot, prev
                for half in range(2):
                    rows = slice(64 * half, 64 * half + 64)
                    pb, pr = bank()
                    mm(pb[:, :], IDB[:, :], MASK[:, mprev, :], True, False, IDB.res + MASK.res, [pr])
                    for j, (kvb, pair) in enumerate(((kvp, 0), (kvp, 1), (kvc, 0), (kvc, 1))):
                        mm(pb[:, 128 * j:128 * j + 128], kvb[rows, pair * 128:(pair + 1) * 128], qt[rows, pair, :],
                           False, j == 3, kvb.res + qt.res, [pr])
                    act(pts[:, half * 512:(half + 1) * 512], pb[:, :], AF.Exp, [pr], pts.res, scale=0.125)
                yield
                po, pro = bank()
                for h in range(4):
                    pair, half = h // 2, h % 2
                    rows = slice(64 * half, 64 * half + 64)
                    vsl = slice(256 + 64 * h, 256 + 64 * h + 64)
                    pp = pts[:, half * 512 + pair * 128:half * 512 + pair * 128 + 128]
                    pc = pts[:, half * 512 + 256 + pair * 128:half * 512 + 256 + pair * 128 + 128]
                    mm(po[rows, pair * 128:(pair + 1) * 128], kvp[:, vsl], pp, True, False, kvp.res + pts.res, [pro])
                    mm(po[rows, pair * 128:(pair + 1) * 128], kvc[:, vsl], pc, False, True, kvc.res + pts.res, [pro])
                    mm(po[rows, 256 + pair * 128:256 + (pair + 1) * 128], ONES[:, :], pp, True, False,
                       ONES.res + pts.res, [pro])
                    mm(po[rows, 256 + pair * 128:256 + (pair + 1) * 128], ONES[:, :], pc, False, True,
                       ONES.res + pts.res, [pro])
                av = ACC[:, :, cols]
                ares = [ACC.res[t] for t in tl]
                pov = po[:, :].rearrange("p (a b) -> p a b", a=4)
                if first:
                    cp(DVE, av, pov, [pro], ares)
                else:
                    tt(DVE, av, av, pov, ALU.add, [pro] + ares, ares)

            def run_pipeline(gens):
                n = len(gens)
                for it in range(n + 3):
                    for k in (3, 2, 1, 0):
                        idx = it - k
                        if 0 <= idx < n:
                            next(gens[idx], None)

            def send_blocks(g, blks, W, wv, sb0):
                sd, rs = (send[l], R_send[l]) if g == 2 else (sendb[l], R_sendb[l])
                for si, b in enumerate(blks):
                    kvs = KV[si % 8]
                    project(g, b, W, wv, kvs)
                    dma(SP, sd[(sb0 + si) * 128:(sb0 + si + 1) * 128, :], kvs[:, :], kvs.res, [rs])

            def collective(sd, rv, rs, rr):
                if os.environ.get("KNOCC"):
                    return
                S.op(POOL, lambda e: e.collective_compute("AllGather", ALU.bypass, replica_groups=RG,
                                                          ins=[sd], outs=[rv]),
                     [rs], [rr], dma=True, inc=1, semname="cc_sem")
                S.op(POOL, lambda e: e.memset(CCD[:, :], 0.0), [rr], CCD.res)

            def out_rows(g, b):
                if g == 0:
                    return (kp[0][l], vp[0][l]) if b == 15 else None
                if g == 1:
                    if b < 12:
                        return None
                    r1 = b - 12
                    return (kp[1][l].rearrange("(i f) c -> f i c", f=4)[r1],
                            vp[1][l].rearrange("(i f) c -> f i c", f=4)[r1])
                return (kp[2][l].rearrange("(i f) c -> f i c", f=16)[b],
                        vp[2][l].rearrange("(i f) c -> f i c", f=16)[b])

            W2, wv2 = load_wq(2)
            run_pipeline([block_task(2, b, W2, wv2, KV[b % 8],
                                     send_dst=(send[l][b * 128:(b + 1) * 128, :], R_send[l])) for b in range(16)])
            W0, wv0 = load_wq(0)
            W1, wv1 = load_wq(1)
            collective(send[l], recv[l], R_send[l], R_recv[l])
            stage(3)
            run_pipeline([block_task(0, 15, W0, wv0, KV[0], send_dst=(sendb[l][0:128, :], R_sendb[l]))] +
                         [block_task(1, 12 + i_, W1, wv1, KV[1 + i_],
                                     send_dst=(sendb[l][(1 + i_) * 128:(2 + i_) * 128, :], R_sendb[l]))
                          for i_ in range(4)])
            WU = wslot()
            wu = WU[:, 0:4096].rearrange("p (a b) -> p a b", a=8)
            dma(POOL, wu, kcv(w_in[l])[:, :, 0:512], (), WU.res)
            pb, pr = bank()
            for kc in range(8):
                mm(pb[:, :], HT[:, kc, tcols(15)], wu[:, kc, :], kc == 0, kc == 7, [HT.res[15]] + WU.res, [pr])
            cp(ACT, H2[0][:, :], pb[:, :], [pr], H2[0].res)
            cp(ACT, SQ[:, :], pb[:, :], [pr], SQ.res)
            dma(SP, sendb[l][5 * 128:6 * 128, :], H2[0][:, :], H2[0].res, [R_sendb[l]])
            dma(SP, poolp[l], SQ[113:128, :], SQ.res, ())
            W0, wv0 = load_wq(0)
            collective(sendb[l], recvb[l], R_sendb[l], R_recvb[l])
            stage(3.5)
            tasks = [block_task(0, 0, W0, wv0, KV[0])]
            for b in range(1, 16):
                tasks.append(block_task(0, b, W0, wv0, KV[b % 8], orows=out_rows(0, b), prev=KV[(b - 1) % 8],
                                        mprev=1, first=True))
            tasks.append(block_task(0, 0, W0, wv0, KV[0], prev=H0, mprev=2, first=True,
                                    hist=(H0, recvb[l][0:128, :], R_recvb[l])))
            run_pipeline(tasks)
            project(0, 0, W0, wv0, None, sample=True)
            W1, wv1 = load_wq(1)
            tasks = [block_task(1, b, W1, wv1, KV[b % 8]) for b in range(4)]
            for b in range(4, 16):
                tasks.append(block_task(1, b, W1, wv1, KV[b % 8], orows=out_rows(1, b), prev=KV[(b - 4) % 8],
                                        mprev=1, first=False))
            for b in range(4):
                tasks.append(block_task(1, b, W1, wv1, KV[b % 8], prev=H1[b], mprev=2, first=False,
                                        hist=(H1[b], recvb[l][(1 + b) * 128:(2 + b) * 128, :], R_recvb[l])))
            run_pipeline(tasks)
            project(1, 0, W1, wv1, None, sample=True)
            W2, wv2 = load_wq(2)
            run_pipeline([block_task(2, b, W2, wv2, KV[b % 8], orows=out_rows(2, b), prev=H2[b % 3], mprev=2,
                                     first=False, hist=(H2[b % 3], recv[l][b * 128:(b + 1) * 128, :], R_recv[l]))
                          for b in range(16)])
            project(2, 0, W2, wv2, None, sample=True)

            stage(4)
            AYT = alloc("AYT", [128, 2, TT], BF16, R3, nres=5)
            RD = [alloc("RD%d" % i, [128, 2, 512], F32, R4 + i * 4096) for i in range(2)]
            for tg in range(4):
                cs = slice(tg * 512, (tg + 1) * 512)
                ares = [ACC.res[t] for t in range(4 * tg, 4 * tg + 4)]
                rd = RD[tg % 2]
                recip(rd[:, :, :], ACC[:, 2:4, cs], ares, rd.res)
                tt(DVE, AYT[:, :, cs], ACC[:, 0:2, cs], rd[:, :, :], ALU.mult, ares + rd.res, [AYT.res[tg]])

            stage(5)
            KC = alloc("KC", [128, NS, 256], F32, R1)
            VC = alloc("VC", [128, NS, 256], F32, R1 + 4096)
            PROD = alloc("PROD", [128, NS * 256], F32, R1 + 8192)
            PVP = alloc("PVP", [128, NS, 260], F32, R1 + 12288)
            SCO = alloc("SCO", [128, 16], F32, R1 + 12288 + 4160)
            SPX = alloc("SPX", [NS, 16], F32, R1 + 12288 + 4160 + 64)
            AYS = alloc("AYS", [NS, 256], BF16, R1 + 12288 + 4160 + 128)
            for g in range(3):
                dil = DILS[g]
                dma(SP, KC[:, :, :], ck[g][l][:, 0:WINS[g]:dil, :].rearrange("b j c -> j b c"), (), KC.res)
                dma(SP, VC[:, :, :], cv[g][l][:, 0:WINS[g]:dil, :].rearrange("b j c -> j b c"), (), VC.res)
                pq = [bank(), bank()]
                for bb in range(NS):
                    pb, pr = pq[bb // 2]
                    mm(pb[:, (bb % 2) * 256:(bb % 2) * 256 + 256], SEL[:, bb, :], QS[:, g, :], True, True,
                       SEL.res + QS.res, [pr])
                for hf in range(2):
                    pb, pr = pq[hf]
                    tt(DVE, PROD[:, hf * 512:(hf + 1) * 512],
                       KC[:, 2 * hf:2 * hf + 2, :].rearrange("p a b -> p (a b)"), pb[:, :], ALU.mult,
                       KC.res + [pr], PROD.res)
                red(DVE, SCO[:, :], PROD[:, :].rearrange("p (a d) -> p a d", d=64), PROD.res, SCO.res)
                act(PVP[:, :, 256:260], SCO[:, :].rearrange("p (a b) -> p a b", a=NS), AF.Exp, SCO.res, PVP.res,
                    scale=0.125)
                tt(DVE, PVP[:, :, 0:256].rearrange("p a (h d) -> p a h d", h=4),
                   VC[:, :, :].rearrange("p a (h d) -> p a h d", h=4),
                   PVP[:, :, 256:260].unsqueeze(3).to_broadcast([128, NS, 4, 64]), ALU.mult,
                   VC.res + PVP.res, PVP.res)
                pb, pr = bank()
                for bb in range(NS):
                    mm(pb[0:NS, 0:260], SELC[:, bb, :], PVP[:, bb, :], bb == 0, bb == NS - 1, SELC.res + PVP.res, [pr])
                tt(DVE, SACC[:, :], SACC[:, :], pb[0:NS, 0:260], ALU.add, SACC.res + [pr], SACC.res)
            recip(SPX[:, 4:8], SACC[:, 256:260], SACC.res, SPX.res)
            tt(DVE, AYS[:, :].rearrange("p (h d) -> p h d", h=4), SACC[:, 0:256].rearrange("p (h d) -> p h d", h=4),
               SPX[:, 4:8].unsqueeze(2).to_broadcast([NS, 4, 64]), ALU.mult, SACC.res + SPX.res, AYS.res)
            pt, rt = bank()
            ptb = pt.bitcast(BF16)
            for pr_ in range(2):
                tr(ptb[:, pr_ * 128:pr_ * 128 + NS], AYS[:, pr_ * 128:(pr_ + 1) * 128], IDB[0:NS, 0:NS],
                   AYS.res + IDB.res, [rt])
            cp(ACT, AYT[:, :, TOK:TT], ptb[:, 0:256].rearrange("p (a b) -> p a b", a=2)[:, :, 0:NS], [rt],
               [AYT.res[4]])

            stage(6)
            PYT = alloc("PYT", [128, 4, TT], BF16, R2, nres=NT + 1)
            o = R1 + 17408
            UB = [alloc("UB%d" % i, [128, 512], BF16, o + i * 1024) for i in range(3)]
            o += 3072
            UH = alloc("UH", [128, 512], BF16, o)
            o += 1024
            UF = alloc("UF", [128, 512], F32, o)
            o += 2048
            PTB = [alloc("PTB%d" % i, [128, 512], BF16, o + i * 1024) for i in range(2)]
            o += 2048
            AM = alloc("AM", [128, 16, 128], BF16, o)
            o += 4096
            assert o <= R2
            o = R4
            ST = alloc("ST", [NS * 15, 512], F32, o)
            o += 2048
            USF = alloc("USF", [NS, 512], F32, o)
            o += 2048
            PSB = alloc("PSB", [NS, 512], BF16, o)
            o += 1024
            PTS_ = alloc("PTSs", [128, 4, NS], BF16, o)
            o += 64
            assert o <= top
            STG2 = alloc("STG2", [128, 2048], F32, R1)
            dma(SP, STG2[:, :], c_amat.rearrange("p a b c -> p (a b c)"), (), STG2.res)
            cp(DVE, AM[:, :, :].rearrange("p a b -> p (a b)"), STG2[:, :], STG2.res, AM.res)
            dma(POOL, WG[:, :, :], w_pool_grp[l].rearrange("g c e -> c g e"), (), WG.res)
            dma(SP, PSC[:, :], pscT[l], (), PSC.res)
            WU = wslot()
            wu = WU[:, 0:4096].rearrange("p (a b) -> p a b", a=8)
            dma(POOL, wu, kcv(w_in[l])[:, :, 0:512], (), WU.res)

            def uproj(t, dst, fp32dst=None):
                np_ = 128 if t < NT else NS
                pb, pr = bank()
                for kc in range(8):
                    mm(pb[0:np_, :], HT[:, kc, tcols(t)], wu[:, kc, :], kc == 0, kc == 7, [HT.res[t]] + WU.res, [pr])
                if dst is not None:
                    cp(ACT, dst[0:np_, :], pb[0:np_, :], [pr], dst.res)
                if fp32dst is not None:
                    cp(ACT, fp32dst[0:np_, :], pb[0:np_, :], [pr], fp32dst.res)

            def pool_tile(t, ucur, uprev, acur, aprev):
                pb, pr = bank()
                for gi in range(4):
                    gs = slice(gi * 128, (gi + 1) * 128)
                    mm(pb[:, gs], ucur[:, gs], AM[:, acur * 4 + gi, :], True, False, ucur.res + AM.res, [pr])
                    mm(pb[:, gs], uprev[:, gs], AM[:, aprev * 4 + gi, :], False, True, uprev.res + AM.res, [pr])
                ptb_ = PTB[t % 2]
                cp(ACT, ptb_[:, :], pb[:, :], [pr], ptb_.res)
                pb2, pr2 = bank()
                for gi in range(4):
                    gs = slice(gi * 128, (gi + 1) * 128)
                    mm(pb2[:, gs], WG[:, gi, :], ptb_[:, gs], True, True, WG.res + ptb_.res, [pr2])
                tt(DVE, PYT[:, :, tcols(t)], pb2[:, :].rearrange("p (a b) -> p a b", a=4),
                   PSC[:, :].unsqueeze(2).to_broadcast([128, 4, 128]), ALU.mult, [pr2] + PSC.res, [PYT.res[t]])

            uproj(0, UB[0])
            uproj(1, UB[1])
            for t in range(1, NT):
                if t + 1 < NT:
                    uproj(t + 1, UB[(t + 1) % 3])
                pool_tile(t, UB[t % 3], UB[(t - 1) % 3], 0, 1)
            dma(SP, UH[:, :], recvb[l][5 * 128:6 * 128, :], [R_recvb[l]], UH.res)
            uproj(0, UB[0])
            pool_tile(0, UB[0], UH, 2, 3)
            uproj(NT, None, USF)
            dma(SP, ST[:, :], spool[l], (), ST.res)
            dma(SP, pools[l][:, 0:14, :], spool[l].rearrange("(b r) c -> b r c", r=15)[:, 1:15, :], (), ())
            dma(SP, pools[l][:, 14, :], USF[:, :], USF.res, ())
            pb, pr = bank()
            for gi in range(4):
                gs = slice(gi * 128, (gi + 1) * 128)
                mm(pb[0:NS, gs], PSEL[:, gi, :], ST[:, gs], True, True, PSEL.res + ST.res, [pr])
            tt(DVE, UF[0:NS, :], USF[:, :], PCOEF[:, :], ALU.mult, USF.res + PCOEF.res, UF.res)
            tt(DVE, PSB[:, :], UF[0:NS, :], pb[0:NS, :], ALU.add, UF.res + [pr], PSB.res)
            pt, rt = bank()
            ptb = pt.bitcast(BF16)
            for gi in range(4):
                tr(ptb[:, gi * 128:gi * 128 + NS], PSB[:, gi * 128:(gi + 1) * 128], IDB[0:NS, 0:NS],
                   PSB.res + IDB.res, [rt])
            cp(ACT, PTS_[:, :, :], ptb[:, 0:512].rearrange("p (a b) -> p a b", a=4)[:, :, 0:NS], [rt], PTS_.res)
            pb2, pr2 = bank()
            for gi in range(4):
                mm(pb2[:, gi * NS:(gi + 1) * NS], WG[:, gi, :], PTS_[:, gi, :], True, True, WG.res + PTS_.res, [pr2])
            tt(DVE, PYT[:, :, TOK:TT], pb2[:, 0:4 * NS].rearrange("p (a b) -> p a b", a=4),
               PSC[:, :].unsqueeze(2).to_broadcast([128, 4, NS]), ALU.mult, [pr2] + PSC.res, [PYT.res[NT]])

            stage(7)
            MGT = alloc("MGT", [128, 8, TT], BF16, R1, nres=5)
            SG = [alloc("SG%d" % i, [128, 512], F32, R4 + i * 2048) for i in range(4)]
            TG = [alloc("TG%d" % i, [128, 512], F32, R4 + 8192 + i * 2048) for i in range(2)]
            for f in range(8):
                W = wslot()
                wap = W[:, 0:1024].rearrange("p (a b) -> p a b", a=8)
                waa = W[:, 1024:2048].rearrange("p (a b) -> p a b", a=8)
                wpb = W[:, 2048:2560].rearrange("p (a b) -> p a b", a=4)
                wab = W[:, 2560:2816].rearrange("p (a b) -> p a b", a=2)
                fs = slice(f * 128, (f + 1) * 128)
                dma(POOL, wap, kcv(w_in[l])[:, :, 2816 + f * 128:2816 + (f + 1) * 128], (), W.res)
                dma(POOL, waa, kcv(w_in[l])[:, :, 3840 + f * 128:3840 + (f + 1) * 128], (), W.res)
                dma(POOL, wpb, kcv(w_pool_br[l])[:, :, fs], (), W.res)
                dma(POOL, wab, kcv(w_attn_br[l])[:, :, fs], (), W.res)
                for tg in range(5):
                    cs = slice(tg * 512, (tg + 1) * 512) if tg < 4 else slice(TOK, TT)
                    n = 512 if tg < 4 else NS
                    tl = list(range(4 * tg, 4 * tg + 4)) if tg < 4 else [NT]
                    hres = [HT.res[t] for t in tl]
                    pyres = [PYT.res[t] for t in tl]
                    b1, r1_ = bank()
                    b2, r2_ = bank()
                    b3, r3_ = bank()
                    b4, r4_ = bank()
                    for kc in range(8):
                        mm(b1[:, 0:n], wap[:, kc, :], HT[:, kc, cs], kc == 0, kc == 7, W.res + hres, [r1_])
                    for kc in range(8):
                        mm(b2[:, 0:n], waa[:, kc, :], HT[:, kc, cs], kc == 0, kc == 7, W.res + hres, [r2_])
                    for gi in range(4):
                        mm(b3[:, 0:n], wpb[:, gi, :], PYT[:, gi, cs], gi == 0, gi == 3, W.res + pyres, [r3_])
                    for p_ in range(2):
                        mm(b4[:, 0:n], wab[:, p_, :], AYT[:, p_, cs], p_ == 0, p_ == 1, W.res + [AYT.res[tg]], [r4_])
                    k = (f * 5 + tg) % 2
                    sp_, sa_, tg_ = SG[2 * k], SG[2 * k + 1], TG[k]
                    act(sp_[:, 0:n], b1[:, 0:n], AF.Sigmoid, [r1_], sp_.res)
                    act(sa_[:, 0:n], b2[:, 0:n], AF.Sigmoid, [r2_], sa_.res)
                    tt(DVE, sp_[:, 0:n], sp_[:, 0:n], b3[:, 0:n], ALU.mult, sp_.res + [r3_], sp_.res)
                    tt(DVE, tg_[:, 0:n], sa_[:, 0:n], b4[:, 0:n], ALU.mult, sa_.res + [r4_], tg_.res)
                    tt(DVE, MGT[:, f, cs], sp_[:, 0:n], tg_[:, 0:n], ALU.add, sp_.res + tg_.res, [MGT.res[tg]])

            stage(8)
            MPA = alloc("MPA", [128, D], F32, R2)
            MPB = alloc("MPB", [128, D], F32, R2 + 4096)
            MSA = alloc("MSA", [NS, D], F32, R2 + 8192)
            MSB = alloc("MSB", [NS, D], F32, R2 + 12288)
            BB = alloc("BB", [128, D], F32, R3)
            NG = alloc("NG", [128, D], F32, R3 + 4096)
            ada(l, 2, MPA, MSA, BB, NG, None)
            TO = [alloc("TO%d" % i, [128, 512], F32, R4 + i * 2048) for i in range(2)]

            def resid_update(t, c, pb, pr, MP, MS, k):
                xt, xr, np_ = xtile(t)
                M = MP if t < NT else MS
                cs = slice(c * 512, (c + 1) * 512)
                to = TO[k % 2]
                tt(DVE, to[0:np_, :], pb[0:np_, :], M[0:np_, cs], ALU.mult, [pr] + M.res, to.res)
                tt(DVE, xt[:, cs], xt[:, cs], to[0:np_, :], ALU.add, [xr] + to.res, [xr])

            kk = 0
            for c in range(2):
                W = wslot()
                wo = W[:, 0:4096].rearrange("p (a b) -> p a b", a=8)
                dma(POOL, wo, kcv(w_out[l])[:, :, c * 512:(c + 1) * 512], (), W.res)
                for t in range(NT + 1):
                    np_ = 128 if t < NT else NS
                    tg = t // 4 if t < NT else 4
                    pb, pr = bank()
                    for kc in range(8):
                        mm(pb[0:np_, :], MGT[:, kc, tcols(t)], wo[:, kc, :], kc == 0, kc == 7, [MGT.res[tg]] + W.res, [pr])
                    resid_update(t, c, pb, pr, MPA, MSA, kk)
                    kk += 1

            stage(9)
            ada(l, 3, MPB, MSB, BB, NG, None)
            MPC = alloc("MPC", [128, D], F32, R1)
            MSC = alloc("MSC", [NS, D], F32, R1 + 4096)
            ada(l, 4, MPC, MSC, BB, NG, norm2_g)
            norm_phase(MPC, MPB, MSC, MSB, R4)
            stage(10)
            ada(l, 5, MPA, MSA, BB, NG, None)
            AT = alloc("AT", [128, 8, TT], BF16, R1, nres=5)
            RL = [alloc("RL%d" % i, [128, 512], F32, R4 + 4096 + i * 2048) for i in range(2)]
            kk = 0
            for j in range(4):
                for c2 in range(2):
                    W = wslot()
                    wup = W[:, 0:4096].rearrange("p (a b) -> p a b", a=8)
                    dma(POOL, wup, kcv(w_up[l])[:, :, 1024 * j + 512 * c2:1024 * j + 512 * (c2 + 1)], (), W.res)
                    for fc in range(4):
                        for tg in range(5):
                            cs = slice(tg * 512, (tg + 1) * 512) if tg < 4 else slice(TOK, TT)
                            n = 512 if tg < 4 else NS
                            tl = list(range(4 * tg, 4 * tg + 4)) if tg < 4 else [NT]
                            hres = [HT.res[t] for t in tl]
                            pb, pr = bank()
                            for kc in range(8):
                                mm(pb[:, 0:n], wup[:, kc, fc * 128:(fc + 1) * 128], HT[:, kc, cs], kc == 0, kc == 7,
                                   W.res + hres, [pr])
                            rl = RL[kk % 2]
                            kk += 1
                            act(rl[:, 0:n], pb[:, 0:n], AF.Relu, [pr], rl.res)
                            tt(DVE, AT[:, 4 * c2 + fc, cs], rl[:, 0:n], rl[:, 0:n], ALU.mult, rl.res, [AT.res[tg]])
                for c in range(2):
                    W = wslot()
                    wd = W[:, 0:4096].rearrange("p (a b) -> p a b", a=8)
                    dma(POOL, wd, kcv(w_down[l])[:, 8 * j:8 * j + 8, c * 512:(c + 1) * 512], (), W.res)
                    for t in range(NT + 1):
                        np_ = 128 if t < NT else NS
                        tg = t // 4 if t < NT else 4
                        pb, pr = bank()
                        for kc in range(8):
                            mm(pb[0:np_, :], AT[:, kc, tcols(t)], wd[:, kc, :], kc == 0, kc == 7,
                               [AT.res[tg]] + W.res, [pr])
                        resid_update(t, c, pb, pr, MPA, MSA, kk)
                        kk += 1

          except _Stop:
            break
        for t in range(NT):
            dma(SP, yp[t * 128:(t + 1) * 128, :], X[:, t, :], [X.res[t]], ())
        dma(SP, ys, XS[:, :], XS.res, ())

        S.finalize()
        sems = {n: es.enter_context(nc.semaphore(n)) for n in sorted(S.semnames)}
        with nc.Block() as block:
            @block.tensor
            def _(e):
                S.emit_engine(PE, e, sems)

            @block.scalar
            def _(e):
                S.emit_engine(ACT, e, sems)

            @block.vector
            def _(e):
                S.emit_engine(DVE, e, sems)

            @block.gpsimd
            def _(e):
                S.emit_engine(POOL, e, sems)

            @block.sync
            def _(e):
                S.emit_engine(SP, e, sems)
    return nc


def _consts(core):
    half = core % 2
    c = {}
    c["c_ident"] = np.eye(128, dtype=np.float32)
    kk = np.arange(128)[:, None]
    qq = np.arange(128)[None, :]
    cur = np.where(kk <= qq, 0.0, NEG).astype(np.float32)
    prev = np.where(kk >= qq, 0.0, NEG).astype(np.float32)
    pf = prev if half == 1 else np.full((128, 128), NEG, np.float32)
    m = np.stack([np.tile(cur, (1, 4)), np.concatenate([prev, prev, cur, cur], 1),
                  np.concatenate([pf, pf, cur, cur], 1)], axis=1)
    c["c_mask"] = np.ascontiguousarray(m, dtype=np.float32)
    inv = 10000.0 ** (-np.arange(0, 64, 2, dtype=np.float64) / 64)
    rope = np.zeros((48, 128, 64), np.float32)
    i = np.arange(128)
    for g in range(3):
        for b in range(16):
            if g == 0:
                tk = 128 * b + i
            elif g == 1:
                tk = 512 * (b // 4) + (b % 4) + 4 * i
            else:
                tk = b + 16 * i
            pos = (2048 * half + tk).astype(np.float32)
            ang = (pos[:, None] * inv[None, :].astype(np.float32)).astype(np.float32)
            rope[g * 16 + b, :, 0:32] = np.cos(ang)
            rope[g * 16 + b, :, 32:64] = np.sin(ang)
    c["c_rope"] = rope
    angs = (np.float32(8192.0) * inv.astype(np.float32)).astype(np.float32)
    c["c_ropes"] = np.tile(np.concatenate([np.cos(angs), np.sin(angs)])[None, :], (NS, 1)).astype(np.float32)
    am = np.zeros((128, 4, 4, 128), np.float32)
    tp = np.arange(128)[:, None]
    t = np.arange(128)[None, :]
    for gi, w in enumerate((2, 4, 8, 16)):
        inwin = (tp <= t) & (t - tp < w)
        curm = np.where(inwin, 1.0 / w, 0.0) - np.eye(128)
        prevm = np.where(t + 128 - tp < w, 1.0 / w, 0.0)
        am[:, 0, gi, :] = curm
        am[:, 1, gi, :] = prevm
        if half == 0:
            cntv = np.minimum(t + 1, w).astype(np.float64)
            am[:, 2, gi, :] = np.where(inwin, 1.0 / cntv, 0.0) - np.eye(128)
            am[:, 3, gi, :] = 0.0
        else:
            am[:, 2, gi, :] = curm
            am[:, 3, gi, :] = prevm
    c["c_amat"] = am
    sel = np.zeros((NS, NS, 128), np.float32)
    selc = np.zeros((128, NS, NS), np.float32)
    for b in range(NS):
        sel[b, b, :] = 1.0
        selc[:, b, b] = 1.0
    c["c_sel"] = sel
    c["c_selc"] = selc
    psel = np.zeros((NS * 15, 4, NS), np.float32)
    pcoef = np.zeros((NS, 512), np.float32)
    for gi, w in enumerate((2, 4, 8, 16)):
        for b in range(NS):
            for r in range(16 - w, 15):
                psel[b * 15 + r, gi, b] = 1.0 / w
        pcoef[:, gi * 128:(gi + 1) * 128] = 1.0 / w - 1.0
    c["c_psel"] = psel
    c["c_pcoef"] = pcoef
    return c


_NC_CACHE = {}


def kernel(x_prompt, x_sample, cache_k_w128, cache_v_w128, cache_k_w512, cache_v_w512,
           cache_k_w2048, cache_v_w2048, state_pool, c_prompt, c_sample, norm1_g, norm2_g,
           w_ada, b_ada, w_in, q_norm_g, k_norm_g, w_pool_grp, pool_scale, w_pool_br,
           w_attn_br, w_out, w_up, w_down):
    f = lambda a: np.ascontiguousarray(np.asarray(a), dtype=np.float32)
    L = NLAYER
    if L < DEPTH:
        (cache_k_w128, cache_v_w128, cache_k_w512, cache_v_w512, cache_k_w2048, cache_v_w2048, state_pool,
         norm1_g, norm2_g, w_ada, b_ada, w_in, q_norm_g, k_norm_g, w_pool_grp, pool_scale, w_pool_br,
         w_attn_br, w_out, w_up, w_down) = [np.asarray(a)[:L] for a in (
            cache_k_w128, cache_v_w128, cache_k_w512, cache_v_w512, cache_k_w2048, cache_v_w2048, state_pool,
            norm1_g, norm2_g, w_ada, b_ada, w_in, q_norm_g, k_norm_g, w_pool_grp, pool_scale, w_pool_br,
            w_attn_br, w_out, w_up, w_down)]
    x_prompt, x_sample = f(x_prompt), f(x_sample)
    cks = [f(cache_k_w128), f(cache_k_w512), f(cache_k_w2048)]
    cvs = [f(cache_v_w128), f(cache_v_w512), f(cache_v_w2048)]
    state_pool, c_prompt, c_sample = f(state_pool), f(c_prompt), f(c_sample)
    shared = {
        "norm1_g": f(norm1_g), "norm2_g": f(norm2_g), "w_ada": f(w_ada), "b_ada": f(b_ada), "w_in": f(w_in),
        "qk_g": f(np.concatenate([np.asarray(q_norm_g), np.asarray(k_norm_g)], axis=1)),
        "w_pool_grp": f(w_pool_grp),
        "pscT": f(np.asarray(pool_scale).reshape(L, 4, 128).transpose(0, 2, 1)),
        "w_pool_br": f(w_pool_br), "w_attn_br": f(w_attn_br), "w_out": f(w_out), "w_up": f(w_up),
        "w_down": f(w_down),
    }
    in_maps = []
    for c in range(8):
        b, h = c // 2, c % 2
        m = dict(shared)
        m["xp"] = np.ascontiguousarray(x_prompt[b, h * TOK:(h + 1) * TOK, :])
        m["xs"] = np.ascontiguousarray(x_sample[NS * c:NS * (c + 1), 0, :])
        m["cpT"] = np.ascontiguousarray(c_prompt[b].reshape(8, 128).T)
        m["csT"] = np.ascontiguousarray(c_sample[NS * c:NS * (c + 1)].reshape(NS, 8, 128).transpose(2, 1, 0))
        for g in range(3):
            m["ck%d" % g] = np.ascontiguousarray(cks[g][:, NS * c:NS * (c + 1)].reshape(L, NS, WINS[g], 256))
            m["cv%d" % g] = np.ascontiguousarray(cvs[g][:, NS * c:NS * (c + 1)].reshape(L, NS, WINS[g], 256))
        m["spool"] = np.ascontiguousarray(state_pool[:, NS * c:NS * (c + 1)].reshape(L, NS * 15, 512))
        m.update(_consts(c))
        in_maps.append(m)
    if "nc" not in _NC_CACHE:
        _NC_CACHE["nc"] = build_program()
    res = run_bass_kernel_spmd(_NC_CACHE["nc"], in_maps, core_ids=list(range(8)))
    R = res.results
    B = 4
    y_prompt = np.zeros((B, 2 * TOK, D), np.float32)
    y_sample = np.zeros((32, 1, D), np.float32)
    for c in range(8):
        y_prompt[c // 2, (c % 2) * TOK:(c % 2 + 1) * TOK] = R[c]["yp"]
        y_sample[NS * c:NS * (c + 1), 0] = R[c]["ys"]
    outs = [y_prompt, y_sample]
    for g in range(3):
        for nm in ("kp", "vp"):
            outs.append(np.stack([R[2 * b + 1]["%s%d" % (nm, g)] for b in range(B)], axis=1)
                        .reshape(L, B, WINS[g], 4, 64).astype(np.float32))
    outs.append(np.stack([R[2 * b + 1]["poolp"] for b in range(B)], axis=1).astype(np.float32))
    for g in range(3):
        for nm in ("ks", "vs"):
            outs.append(np.concatenate([R[c]["%s%d" % (nm, g)] for c in range(8)], axis=1)
                        .reshape(L, 32, 1, 4, 64).astype(np.float32))
    outs.append(np.concatenate([R[c]["pools"] for c in range(8)], axis=1).astype(np.float32))
    return tuple(outs)
```

```python
import contextlib
import os
import numpy as np
import concourse.bass as bass
import concourse.mybir as mybir
from concourse.bass_utils import run_bass_kernel_spmd

F32 = mybir.dt.float32
BF16 = mybir.dt.bfloat16
ALU = mybir.AluOpType
AF = mybir.ActivationFunctionType
AX = mybir.AxisListType
PE, ACT, DVE, POOL, SP = "pe", "act", "dve", "pool", "sp"
ENGS = (PE, ACT, DVE, POOL, SP)

DEPTH = 4
D = 1024
NT = 16
TOK = 2048
NS = 4
TT = TOK + NS
INW = 4864
DFF = 4096
EPS = 1e-6
WINS = (128, 512, 2048)
DILS = (1, 4, 16)
NBLK_SEND = (1, 4, 16)
RG = [[0, 1], [2, 3], [4, 5], [6, 7]]
NEG = -30000.0
STAGE = float(os.environ.get('KSTAGE', '99'))
NLAYER = int(os.environ.get('KLAYERS', str(DEPTH)))


class _Stop(Exception):
    pass


def stage(n):
    if STAGE <= n:
        raise _Stop()


class Res:
    __slots__ = ("name", "writers", "readers")

    def __init__(self, name):
        self.name = name
        self.writers = {}
        self.readers = {}

    def inherit(self, other):
        for d in (other.writers, other.readers):
            for k, o in d.items():
                c = self.writers.get(k)
                if c is None or c.seq < o.seq:
                    self.writers[k] = o


class Op:
    __slots__ = ("eng", "fn", "deps", "token", "needs_inc", "is_dma", "key", "inc", "seq")

    def __init__(self, eng, fn, is_dma=False):
        self.eng = eng
        self.fn = fn
        self.deps = []
        self.token = None
        self.needs_inc = False
        self.is_dma = is_dma
        self.key = eng
        self.inc = 16


class Sched:
    SEM_ROT = 30000
    NDMA = {SP: 14, POOL: 6, ACT: 2}

    def __init__(self):
        self.ops = {e: [] for e in ENGS}
        self.dma_rr = {q: 0 for q in self.NDMA}
        self.dma_last = {}
        self.dma_val = {}
        self.semnames = set()
        self.nseq = 0

    def op(self, eng, fn, reads=(), writes=(), dma=False, inc=16, semname=None):
        o = Op(eng, fn, is_dma=dma)
        o.inc = inc
        self.nseq += 1
        o.seq = self.nseq
        deps = []
        if dma:
            if semname is None:
                k = self.dma_rr[eng]
                self.dma_rr[eng] = (k + 1) % self.NDMA[eng]
                sname = "dq_%s_%d" % (eng, k)
            else:
                sname = semname
            self.semnames.add(sname)
            self.dma_val[sname] = self.dma_val.get(sname, 0) + inc
            o.token = (sname, self.dma_val[sname])
            o.key = sname
            o.needs_inc = True
            prev = self.dma_last.get(sname)
            if prev is not None:
                deps.append(prev)
            self.dma_last[sname] = o
        for r in reads:
            deps.extend(r.writers.values())
        for r in writes:
            deps.extend(r.writers.values())
            deps.extend(r.readers.values())
        for r in reads:
            r.readers[o.key] = o
        for r in writes:
            r.readers = {}
            r.writers[o.key] = o
        seen = set()
        for d in deps:
            if id(d) in seen or d is o:
                continue
            seen.add(id(d))
            if d.eng == PE and eng == PE and not d.is_dma and not dma:
                continue
            o.deps.append(d)
            if not d.is_dma:
                d.needs_inc = True
        self.ops[eng].append(o)
        return o

    def finalize(self):
        for e in ENGS:
            c = 0
            idx = 0
            for o in self.ops[e]:
                if o.is_dma or not o.needs_inc:
                    continue
                if c >= self.SEM_ROT:
                    idx += 1
                    c = 0
                c += 1
                sname = "pg_%s_%d" % (e, idx)
                self.semnames.add(sname)
                o.token = (sname, c)

    def emit_engine(self, eng, engobj, sems):
        waited = {}
        for o in self.ops[eng]:
            need = {}
            for d in o.deps:
                s, v = d.token
                if waited.get(s, 0) >= v:
                    continue
                if need.get(s, 0) < v:
                    need[s] = v
            for s, v in need.items():
                engobj.wait_ge(sems[s], v)
                waited[s] = v
            ins = o.fn(engobj)
            if o.needs_inc:
                s, v = o.token
                ins.then_inc(sems[s], o.inc if o.is_dma else 1)
        last = {}
        for o in self.ops[eng]:
            if o.is_dma:
                last[o.token[0]] = o.token[1]
        for s, v in last.items():
            if waited.get(s, 0) < v:
                engobj.wait_ge(sems[s], v)


class Buf:
    def __init__(self, t, off, size, nres, name):
        self.t = t
        self.off = off
        self.size = size
        self.res = [Res("%s_%d" % (name, i)) for i in range(nres)]

    def __getitem__(self, k):
        return self.t[k]


def build_program():
    nc = bass.Bass("TRN2", target_bir_lowering=False)
    S = Sched()

    def din(name, shape, dt=F32):
        return nc.dram_tensor(name, list(shape), dt, kind="ExternalInput").ap()

    def dout(name, shape, dt=F32):
        return nc.dram_tensor(name, list(shape), dt, kind="ExternalOutput").ap()

    xp = din("xp", [TOK, D])
    xs = din("xs", [NS, D])
    cpT = din("cpT", [128, 8])
    csT = din("csT", [128, 8, NS])
    ck = [din("ck%d" % g, [NLAYER, NS, WINS[g], 256]) for g in range(3)]
    cv = [din("cv%d" % g, [NLAYER, NS, WINS[g], 256]) for g in range(3)]
    spool = din("spool", [NLAYER, NS * 15, 512])
    norm1_g = din("norm1_g", [NLAYER, D])
    norm2_g = din("norm2_g", [NLAYER, D])
    w_ada = din("w_ada", [NLAYER, D, 6 * D])
    b_ada = din("b_ada", [NLAYER, 6 * D])
    w_in = din("w_in", [NLAYER, D, INW])
    qk_g = din("qk_g", [NLAYER, 128])
    w_pool_grp = din("w_pool_grp", [NLAYER, 4, 128, 128])
    pscT = din("pscT", [NLAYER, 128, 4])
    w_pool_br = din("w_pool_br", [NLAYER, 512, D])
    w_attn_br = din("w_attn_br", [NLAYER, 256, D])
    w_out = din("w_out", [NLAYER, D, D])
    w_up = din("w_up", [NLAYER, D, DFF])
    w_down = din("w_down", [NLAYER, DFF, D])
    c_ident = din("c_ident", [128, 128])
    c_mask = din("c_mask", [128, 3, 512])
    c_rope = din("c_rope", [48, 128, 64])
    c_ropes = din("c_ropes", [NS, 64])
    c_amat = din("c_amat", [128, 4, 4, 128])
    c_sel = din("c_sel", [NS, NS, 128])
    c_selc = din("c_selc", [128, NS, NS])
    c_psel = din("c_psel", [NS * 15, 4, NS])
    c_pcoef = din("c_pcoef", [NS, 512])

    yp = dout("yp", [TOK, D])
    ys = dout("ys", [NS, D])
    kp = [dout("kp%d" % g, [NLAYER, WINS[g], 256]) for g in range(3)]
    vp = [dout("vp%d" % g, [NLAYER, WINS[g], 256]) for g in range(3)]
    poolp = dout("poolp", [NLAYER, 15, 512])
    ks = [dout("ks%d" % g, [NLAYER, NS, 256]) for g in range(3)]
    vs = [dout("vs%d" % g, [NLAYER, NS, 256]) for g in range(3)]
    pools = dout("pools", [NLAYER, NS, 15, 512])

    NSB = 16
    send = [nc.dram_tensor("send_%d" % l, [NSB * 128, 512], BF16).ap() for l in range(NLAYER)]
    recv = [nc.dram_tensor("recv_%d" % l, [2 * NSB * 128, 512], BF16).ap() for l in range(NLAYER)]
    sendb = [nc.dram_tensor("sendb_%d" % l, [NSB * 128, 512], BF16).ap() for l in range(NLAYER)]
    recvb = [nc.dram_tensor("recvb_%d" % l, [2 * NSB * 128, 512], BF16).ap() for l in range(NLAYER)]
    R_send = [Res("send") for l in range(NLAYER)]
    R_recv = [Res("recv") for l in range(NLAYER)]
    R_sendb = [Res("sendb") for l in range(NLAYER)]
    R_recvb = [Res("recvb") for l in range(NLAYER)]

    es = contextlib.ExitStack()
    with es:
        base0 = (nc.sbuf_base + 63) // 64 * 64
        top = nc.sbuf_top
        allbufs = []
        cnt = [0]

        def alloc(name, shape, dt, off, nres=1):
            nb = int(np.prod(shape[1:])) * (2 if dt == BF16 else 4)
            assert off % 32 == 0, (name, off)
            assert off + nb <= top, (name, off, nb, top)
            cnt[0] += 1
            t = nc.alloc_sbuf_tensor_at("%s_%d" % (name, cnt[0]), list(shape), dt, offset=off)
            b = Buf(t, off, nb, nres, name)
            for o in allbufs:
                if o.off < off + nb and off < o.off + o.size:
                    for r in b.res:
                        for ro in o.res:
                            r.inherit(ro)
            allbufs.append(b)
            return b

        cur = [base0]

        def palloc(name, shape, dt, nres=1):
            nb = int(np.prod(shape[1:])) * (2 if dt == BF16 else 4)
            nb = (nb + 31) // 32 * 32
            b = alloc(name, shape, dt, cur[0], nres)
            cur[0] += nb
            return b

        X = palloc("X", [128, NT, D], F32, nres=NT)
        XS = palloc("XS", [NS, D], F32)
        HT = palloc("HT", [128, 8, TT], BF16, nres=NT + 1)
        WR = [palloc("WR%d" % i, [128, 6144], BF16) for i in range(2)]
        IDB = palloc("IDB", [128, 128], BF16)
        MASK = palloc("MASK", [128, 3, 512], BF16)
        ONES = palloc("ONES", [128, 64], BF16)
        SCP = palloc("SCP", [128, 8, 128], BF16)
        SCS = palloc("SCS", [128, 8, NS], BF16)
        SEL = palloc("SEL", [NS, NS, 128], BF16)
        SELC = palloc("SELC", [128, NS, NS], F32)
        PSEL = palloc("PSEL", [NS * 15, 4, NS], F32)
        PCOEF = palloc("PCOEF", [NS, 512], F32)
        EPSB = palloc("EPSB", [128, 1], F32)
        GQK = palloc("GQK", [128, 2, 64], F32)
        PSC = palloc("PSC", [128, 4], F32)
        WG = palloc("WG", [128, 4, 128], BF16)
        ROPES = palloc("ROPES", [NS, 64], F32)
        SS = [palloc("SS%d" % i, [128, 8], F32) for i in range(2)]
        QS = palloc("QS", [NS, 3, 256], BF16)
        SSELF = palloc("SSELF", [NS, 8], F32)
        SACC = palloc("SACC", [NS, 260], F32)
        CCD = palloc("CCD", [128, 8], F32)
        ARENA = (cur[0] + 63) // 64 * 64
        AR_SIZE = top - ARENA
        R1 = ARENA
        R2 = R1 + 32832 + 64
        R3 = R2 + 16416 + 32
        R4 = R3 + 8224
        assert R4 + 14 * 1024 <= top, (R4, top)

        psall = es.enter_context(nc.psum_tensor("psall", [128, 8, 512], F32))
        PB = [Res("bank%d" % i) for i in range(8)]
        pbi = [0]

        def bank():
            i = pbi[0]
            pbi[0] = (i + 1) % 8
            return psall[:, i, :], PB[i]

        def dma(q, out, in_, reads=(), writes=()):
            return S.op(q, lambda e: e.dma_start(out=out, in_=in_), reads, writes, dma=True)

        def mm(out, lhsT, rhs, start, stop, reads, writes):
            return S.op(PE, lambda e: e.matmul(out, lhsT=lhsT, rhs=rhs, start=start, stop=stop), reads, writes)

        def tr(out, in_, ident, reads, writes):
            return S.op(PE, lambda e: e.transpose(out=out, in_=in_, identity=ident), reads, writes)

        def act(out, in_, func, reads, writes, **kw):
            return S.op(ACT, lambda e: e.activation(out=out, in_=in_, func=func, **kw), reads, writes)

        def cp(eng, out, in_, reads, writes):
            if eng == ACT:
                return S.op(ACT, lambda e: e.copy(out=out, in_=in_), reads, writes)
            return S.op(eng, lambda e: e.tensor_copy(out=out, in_=in_), reads, writes)

        def tt(eng, out, in0, in1, op, reads, writes):
            return S.op(eng, lambda e: e.tensor_tensor(out=out, in0=in0, in1=in1, op=op), reads, writes)

        def stt(eng, out, in0, scalar, in1, op0, op1, reads, writes):
            return S.op(eng, lambda e: e.scalar_tensor_tensor(out=out, in0=in0, scalar=scalar, in1=in1,
                                                              op0=op0, op1=op1), reads, writes)

        def red(eng, out, in_, reads, writes):
            return S.op(eng, lambda e: e.tensor_reduce(out=out, in_=in_, axis=AX.X, op=ALU.add), reads, writes)

        def recip(out, in_, reads, writes):
            return S.op(DVE, lambda e: e.reciprocal(out=out, in_=in_), reads, writes)

        def mset(eng, ap, v, writes):
            return S.op(eng, lambda e: e.memset(ap, v), (), writes)

        def kcv(ap2d):
            return ap2d.rearrange("(kc k) n -> k kc n", k=128)

        wri = [0]

        def wslot():
            i = wri[0]
            wri[0] = (i + 1) % len(WR)
            return WR[i]

        stg_off = R1
        STG = alloc("STG", [128, 2048], F32, stg_off)
        dma(SP, STG[:, 0:128], c_ident, (), STG.res)
        cp(DVE, IDB[:, :], STG[:, 0:128], STG.res, IDB.res)
        dma(SP, STG[:, 0:1536], c_mask.rearrange("p a b -> p (a b)"), (), STG.res)
        cp(DVE, MASK[:, :, :].rearrange("p a b -> p (a b)"), STG[:, 0:1536], STG.res, MASK.res)
        mset(DVE, ONES[:, :], 1.0, ONES.res)
        mset(DVE, EPSB[:, :], EPS, EPSB.res)
        dma(SP, STG[:, 0:8], cpT, (), STG.res)
        act(STG[:, 8:16], STG[:, 0:8], AF.Silu, STG.res, STG.res)
        cp(DVE, SCP[:, :, :], STG[:, 8:16].unsqueeze(2).to_broadcast([128, 8, 128]), STG.res, SCP.res)
        dma(SP, STG[:, 0:32], csT.rearrange("p a b -> p (a b)"), (), STG.res)
        act(SCS[:, :, :].rearrange("p a b -> p (a b)"), STG[:, 0:32], AF.Silu, STG.res, SCS.res)
        dma(SP, STG[0:NS, 0:512], c_sel.rearrange("p a b -> p (a b)"), (), STG.res)
        cp(DVE, SEL[:, :, :].rearrange("p a b -> p (a b)"), STG[0:NS, 0:512], STG.res, SEL.res)
        dma(SP, SELC[:, :, :], c_selc, (), SELC.res)
        dma(SP, PSEL[:, :, :], c_psel, (), PSEL.res)
        dma(SP, PCOEF[:, :], c_pcoef, (), PCOEF.res)
        dma(SP, ROPES[:, :], c_ropes, (), ROPES.res)
        for t in range(NT):
            dma(SP, X[:, t, :], xp[t * 128:(t + 1) * 128, :], (), [X.res[t]])
        dma(SP, XS[:, :], xs, (), XS.res)

        def xtile(t):
            if t < NT:
                return X[:, t, :], X.res[t], 128
            return XS[:, :], XS.res[0], NS

        def tcols(t):
            if t < NT:
                return slice(t * 128, (t + 1) * 128)
            return slice(TOK, TT)

        def ada(l, j, MP, MS, BB, NG, scale_norm):
            dma(SP, BB[:, :], b_ada[l:l + 1, j * D:(j + 1) * D].partition_broadcast(128), (), BB.res)
            if scale_norm is not None:
                dma(SP, NG[:, :], scale_norm[l:l + 1, :].partition_broadcast(128), (), NG.res)
            for c in range(2):
                W = wslot()
                wv = W[:, 0:4096].rearrange("p (a b) -> p a b", a=8)
                dma(POOL, wv, kcv(w_ada[l])[:, :, j * D + c * 512: j * D + (c + 1) * 512], (), W.res)
                for (sc, M, np_) in ((SCP, MP, 128), (SCS, MS, NS)):
                    pb, pr = bank()
                    for kc in range(8):
                        mm(pb[0:np_, :], sc[:, kc, :], wv[:, kc, :], kc == 0, kc == 7, W.res + sc.res, [pr])
                    cs = slice(c * 512, (c + 1) * 512)
                    tt(DVE, M[0:np_, cs], pb[0:np_, :], BB[0:np_, cs], ALU.add, [pr] + BB.res, M.res)
                    if scale_norm is not None:
                        stt(DVE, M[0:np_, cs], M[0:np_, cs], 1.0, NG[0:np_, cs], ALU.add, ALU.mult,
                            M.res + NG.res, M.res)

        def norm_phase(GP, SHP, GS, SHS, toff):
            TMP = [alloc("NTMP%d" % i, [128, D], F32, toff + i * 6144) for i in range(2)]
            HB = [alloc("NHB%d" % i, [128, D], BF16, toff + i * 6144 + 4096) for i in range(2)]
            def tile_gen(t):
                xt, xr, np_ = xtile(t)
                G, SH = (GP, SHP) if t < NT else (GS, SHS)
                tmp = TMP[t % 2]
                hb = HB[t % 2]
                ss = SS[t % 2]
                mset(DVE, ss[:, 0:1], 0.0, ss.res)
                act(tmp[0:np_, :], xt, AF.Square, [xr], tmp.res + ss.res, accum_out=ss[0:np_, 0:1])
                act(ss[0:np_, 0:1], ss[0:np_, 0:1], AF.Ln, ss.res + EPSB.res, ss.res, scale=1.0 / D,
                    bias=EPSB[0:np_, 0:1])
                act(ss[0:np_, 0:1], ss[0:np_, 0:1], AF.Exp, ss.res, ss.res, scale=-0.5)
                yield
                stt(DVE, tmp[0:np_, :], xt, ss[0:np_, 0:1], G[0:np_, :], ALU.mult, ALU.mult,
                    [xr] + ss.res + G.res, tmp.res)
                tt(DVE, hb[0:np_, :], tmp[0:np_, :], SH[0:np_, :], ALU.add, tmp.res + SH.res, hb.res)
                yield
                pb, pr = bank()
                pbb = pb.bitcast(BF16)
                for kc in range(8):
                    tr(pbb[:, kc * 128: kc * 128 + np_], hb[0:np_, kc * 128:(kc + 1) * 128], IDB[0:np_, 0:np_],
                       hb.res + IDB.res, [pr])
                cp(ACT, HT[:, :, tcols(t)], pbb[:, 0:1024].rearrange("p (a b) -> p a b", a=8)[:, :, 0:np_],
                   [pr], [HT.res[t]])

            gens = [tile_gen(t) for t in range(NT + 1)]
            n = len(gens)
            for it in range(n + 2):
                for k in (2, 1, 0):
                    idx = it - k
                    if 0 <= idx < n:
                        next(gens[idx], None)

        for l in range(NLAYER):
          try:
            MPA = alloc("MPA", [128, D], F32, R2)
            MPB = alloc("MPB", [128, D], F32, R2 + 4096)
            MSA = alloc("MSA", [NS, D], F32, R2 + 8192)
            MSB = alloc("MSB", [NS, D], F32, R2 + 12288)
            BB = alloc("BB", [128, D], F32, R3)
            NG = alloc("NG", [128, D], F32, R3 + 4096)
            ada(l, 0, MPB, MSB, BB, NG, None)
            ada(l, 1, MPA, MSA, BB, NG, norm1_g)
            stage(1)
            norm_phase(MPA, MPB, MSA, MSB, R4)

            stage(2)
            ACC = alloc("ACC", [128, 4, TOK], F32, R1, nres=NT)
            KV = [alloc("KV%d" % i, [128, 512], BF16, R2 + i * 1024) for i in range(8)]
            H0 = alloc("H0", [128, 512], BF16, R2 + 8192)
            H1 = [alloc("H1_%d" % i, [128, 512], BF16, R2 + 9216 + i * 1024) for i in range(4)]
            H2 = [alloc("H2_%d" % i, [128, 512], BF16, R2 + 13312 + i * 1024) for i in range(3)]
            o = R3
            QK = [alloc("QK%d" % i, [128, 512], F32, o + i * 2048) for i in range(2)]
            o += 4096
            QO = [alloc("QO%d" % i, [128, 512], F32, o + i * 2048) for i in range(2)]
            o += 4096
            SQ = alloc("SQ", [128, 512], F32, o)
            o += 2048
            TA = alloc("TA", [128, 256], F32, o)
            o += 1024
            TB = alloc("TB", [128, 256], F32, o)
            o += 1024
            QB = [alloc("QB%d" % i_, [128, 512], BF16, o + i_ * 1024) for i_ in range(2)]
            o += 2048
            VF = [alloc("VF%d" % i, [128, 256], F32, o + i * 1024) for i in range(2)]
            o += 2048
            QT = [alloc("QT%d" % i, [128, 2, 128], BF16, o + i * 512) for i in range(2)]
            o += 1024
            PTS = [alloc("PTS%d" % i, [128, 1024], BF16, o + i * 2048) for i in range(2)]
            o += 4096
            ROPE = [alloc("ROPE%d" % i, [128, 64], F32, o + i * 256) for i in range(4)]
            o += 1024
            assert o <= top, (o, top)
            dma(SP, GQK[:, :, :].rearrange("p a b -> p (a b)"), qk_g[l:l + 1, :].partition_broadcast(128), (), GQK.res)
            mset(DVE, SACC[:, :], 0.0, SACC.res)

            blkc = [0]

            def blk_cols(g, b):
                if g == 0:
                    return slice(128 * b, 128 * b + 128), [b]
                if g == 1:
                    n1, r1 = b // 4, b % 4
                    return slice(512 * n1 + r1, 512 * n1 + 512, 4), list(range(4 * n1, 4 * n1 + 4))
                return slice(b, TOK, 16), list(range(NT))

            def load_wq(g):
                W = wslot()
                wv = W[:, :].rearrange("p (a b) -> p a b", a=8)
                for i, c0 in enumerate((512, 1280, 2048)):
                    dma(POOL, wv[:, :, i * 256:(i + 1) * 256], kcv(w_in[l])[:, :, c0 + 256 * g: c0 + 256 * g + 256],
                        (), W.res)
                return W, wv

            def project(g, b, W, wv, kvslot, out_rows=None, sample=False):
                i = blkc[0]
                blkc[0] += 1
                qk, qo, vf, qt, rp, ss = QK[i % 2], QO[i % 2], VF[i % 2], QT[i % 2], ROPE[i % 4], SS[i % 2]
                if sample:
                    np_ = NS
                    cols, tl = slice(TOK, TT), [NT]
                    ropeap, roper = ROPES, ROPES.res
                else:
                    np_ = 128
                    cols, tl = blk_cols(g, b)
                    dma(SP, rp[:, :], c_rope[g * 16 + b], (), rp.res)
                    ropeap, roper = rp, rp.res
                hres = [HT.res[t] for t in tl]
                p1, r1_ = bank()
                p2, r2_ = bank()
                for kc in range(8):
                    mm(p1[0:np_, :], HT[:, kc, cols], wv[:, kc, 0:512], kc == 0, kc == 7, hres + W.res, [r1_])
                for kc in range(8):
                    mm(p2[0:np_, 0:256], HT[:, kc, cols], wv[:, kc, 512:768], kc == 0, kc == 7, hres + W.res, [r2_])
                P = slice(0, np_)
                cp(DVE, qk[P, :], p1[P, :], [r1_], qk.res)
                KP = int(os.environ.get("KPROJ", "9"))
                if KP <= 1:
                    return None
                tt(DVE, SQ[P, :], qk[P, :], qk[P, :], ALU.mult, qk.res, SQ.res)
                red(DVE, ss[P, 0:8], SQ[P, :].rearrange("p (h d) -> p h d", h=8), SQ.res, ss.res)
                act(ss[P, 0:8], ss[P, 0:8], AF.Ln, ss.res + EPSB.res, ss.res, scale=1.0 / 64, bias=EPSB[P, 0:1])
                act(ss[P, 0:8], ss[P, 0:8], AF.Exp, ss.res, ss.res, scale=-0.5)
                tt(DVE, qk[P, :].rearrange("p (a h d) -> p a h d", a=2, h=4),
                   qk[P, :].rearrange("p (a h d) -> p a h d", a=2, h=4),
                   GQK[P, :, :].unsqueeze(2).to_broadcast([np_, 2, 4, 64]), ALU.mult, qk.res + GQK.res, qk.res)
                qv = qk[P, :].rearrange("p (h d) -> p h d", h=8)
                ov = qo[P, :].rearrange("p (h d) -> p h d", h=8)
                cosb = ropeap[P, 0:32].unsqueeze(1).to_broadcast([np_, 8, 32])
                sinb = ropeap[P, 32:64].unsqueeze(1).to_broadcast([np_, 8, 32])
                ta = TA[P, :].rearrange("p (h d) -> p h d", h=8)
                tb = TB[P, :].rearrange("p (h d) -> p h d", h=8)
                tt(DVE, ta, qv[:, :, 0:32], cosb, ALU.mult, qk.res + roper, TA.res)
                tt(DVE, tb, qv[:, :, 32:64], sinb, ALU.mult, qk.res + roper, TB.res)
                tt(DVE, ov[:, :, 0:32], ta, tb, ALU.subtract, TA.res + TB.res, qo.res)
                tt(DVE, ta, qv[:, :, 32:64], cosb, ALU.mult, qk.res + roper, TA.res)
                tt(DVE, tb, qv[:, :, 0:32], sinb, ALU.mult, qk.res + roper, TB.res)
                tt(DVE, ov[:, :, 32:64], ta, tb, ALU.add, TA.res + TB.res, qo.res)
                tt(DVE, ov, ov, ss[P, 0:8].unsqueeze(2).to_broadcast([np_, 8, 64]), ALU.mult, qo.res + ss.res, qo.res)
                cp(ACT, vf[P, :], p2[P, 0:256], [r2_], vf.res)
                if KP <= 2:
                    return None
                if sample:
                    cp(DVE, QS[:, g, :], qo[P, 0:256], qo.res, QS.res)
                    tt(DVE, TA[P, :], qo[P, 0:256], qo[P, 256:512], ALU.mult, qo.res, TA.res)
                    red(DVE, SSELF[:, 0:4], TA[P, :].rearrange("p (h d) -> p h d", h=4), TA.res, SSELF.res)
                    act(SSELF[:, 4:8], SSELF[:, 0:4], AF.Exp, SSELF.res, SSELF.res, scale=0.125)
                    tt(DVE, TA[P, :].rearrange("p (h d) -> p h d", h=4), vf[P, :].rearrange("p (h d) -> p h d", h=4),
                       SSELF[:, 4:8].unsqueeze(2).to_broadcast([NS, 4, 64]), ALU.mult, vf.res + SSELF.res, TA.res)
                    tt(DVE, SACC[:, 0:256], SACC[:, 0:256], TA[P, :], ALU.add, SACC.res + TA.res, SACC.res)
                    tt(DVE, SACC[:, 256:260], SACC[:, 256:260], SSELF[:, 4:8], ALU.add, SACC.res + SSELF.res, SACC.res)
                    dma(SP, ks[g][l], qo[P, 256:512], qo.res, ())
                    dma(SP, vs[g][l], vf[P, :], vf.res, ())
                    return None
                cp(DVE, QB[0][:, :], qo[:, :], qo.res, QB[0].res)
                cp(ACT, kvslot[:, 256:512], p2[:, 0:256], [r2_], kvslot.res)
                pt, rt = bank()
                ptb = pt.bitcast(BF16)
                for j in range(4):
                    tr(ptb[:, j * 128:(j + 1) * 128], QB[0][:, j * 128:(j + 1) * 128], IDB[:, :], QB[0].res + IDB.res, [rt])
                cp(ACT, qt[:, :, :].rearrange("p a b -> p (a b)"), ptb[:, 0:256], [rt], qt.res)
                cp(ACT, kvslot[:, 0:256], ptb[:, 256:512], [rt], kvslot.res)
                if KP <= 3:
                    return qt
                if out_rows is not None:
                    kd, vd = out_rows
                    dma(SP, kd, qo[:, 256:512], qo.res, ())
                    dma(SP, vd, vf[:, :], vf.res, ())
                return qt

            def attend(g, b, qt, kvc, kvp, mprev, first):
                i = blkc[0]
                pts = PTS[i % 2]
                cols, tl = blk_cols(g, b)
                for half in range(2):
                    rows = slice(64 * half, 64 * half + 64)
                    pb, pr = bank()
                    mm(pb[:, :], IDB[:, :], MASK[:, mprev, :], True, False, IDB.res + MASK.res, [pr])
                    for j, (kvb, pair) in enumerate(((kvp, 0), (kvp, 1), (kvc, 0), (kvc, 1))):
                        mm(pb[:, 128 * j:128 * j + 128], kvb[rows, pair * 128:(pair + 1) * 128], qt[rows, pair, :],
                           False, j == 3, kvb.res + qt.res, [pr])
                    act(pts[:, half * 512:(half + 1) * 512], pb[:, :], AF.Exp, [pr], pts.res, scale=0.125)
                po, pro = bank()
                for h in range(4):
                    pair, half = h // 2, h % 2
                    rows = slice(64 * half, 64 * half + 64)
                    vsl = slice(256 + 64 * h, 256 + 64 * h + 64)
                    pp = pts[:, half * 512 + pair * 128:half * 512 + pair * 128 + 128]
                    pc = pts[:, half * 512 + 256 + pair * 128:half * 512 + 256 + pair * 128 + 128]
                    mm(po[rows, pair * 128:(pair + 1) * 128], kvp[:, vsl], pp, True, False, kvp.res + pts.res, [pro])
                    mm(po[rows, pair * 128:(pair + 1) * 128], kvc[:, vsl], pc, False, True, kvc.res + pts.res, [pro])
                    mm(po[rows, 256 + pair * 128:256 + (pair + 1) * 128], ONES[:, :], pp, True, False,
                       ONES.res + pts.res, [pro])
                    mm(po[rows, 256 + pair * 128:256 + (pair + 1) * 128], ONES[:, :], pc, False, True,
                       ONES.res + pts.res, [pro])
                av = ACC[:, :, cols]
                ares = [ACC.res[t] for t in tl]
                pov = po[:, :].rearrange("p (a b) -> p a b", a=4)
                if first:
                    cp(DVE, av, pov, [pro], ares)
                else:
                    tt(DVE, av, av, pov, ALU.add, [pro] + ares, ares)

            def block_task(g, b, W, wv, kvslot, orows=None, send_dst=None, prev=None, mprev=1, first=False,
                           hist=None):
                i = blkc[0]
                blkc[0] += 1
                qk, qo, vf, qt, rp, ss = QK[i % 2], QO[i % 2], VF[i % 2], QT[i % 2], ROPE[i % 4], SS[i % 2]
                pts = PTS[i % 2]
                cols, tl = blk_cols(g, b)
                if hist is not None:
                    hbuf, hsrc, hres_ = hist
                    dma(SP, hbuf[:, :], hsrc, [hres_], hbuf.res)
                dma(SP, rp[:, :], c_rope[g * 16 + b], (), rp.res)
                hres = [HT.res[t] for t in tl]
                p1, r1_ = bank()
                p2, r2_ = bank()
                for kc in range(8):
                    mm(p1[:, :], HT[:, kc, cols], wv[:, kc, 0:512], kc == 0, kc == 7, hres + W.res, [r1_])
                for kc in range(8):
                    mm(p2[:, 0:256], HT[:, kc, cols], wv[:, kc, 512:768], kc == 0, kc == 7, hres + W.res, [r2_])
                cp(ACT, qk[:, :], p1[:, :], [r1_], qk.res)
                act(SQ[:, :], p1[:, :], AF.Square, [r1_], SQ.res)
                red(DVE, ss[:, 0:8], SQ[:, :].rearrange("p (h d) -> p h d", h=8), SQ.res, ss.res)
                act(ss[:, 0:8], ss[:, 0:8], AF.Ln, ss.res + EPSB.res, ss.res, scale=1.0 / 64, bias=EPSB[:, 0:1])
                act(ss[:, 0:8], ss[:, 0:8], AF.Exp, ss.res, ss.res, scale=-0.5)
                tt(DVE, qk[:, :].rearrange("p (a h d) -> p a h d", a=2, h=4),
                   qk[:, :].rearrange("p (a h d) -> p a h d", a=2, h=4),
                   GQK[:, :, :].unsqueeze(2).to_broadcast([128, 2, 4, 64]), ALU.mult, qk.res + GQK.res, qk.res)
                qv = qk[:, :].rearrange("p (h d) -> p h d", h=8)
                ov = qo[:, :].rearrange("p (h d) -> p h d", h=8)
                cosb = rp[:, 0:32].unsqueeze(1).to_broadcast([128, 8, 32])
                sinb = rp[:, 32:64].unsqueeze(1).to_broadcast([128, 8, 32])
                ta = TA[:, :].rearrange("p (h d) -> p h d", h=8)
                tb = TB[:, :].rearrange("p (h d) -> p h d", h=8)
                tc_ = SQ[:, 0:256].rearrange("p (h d) -> p h d", h=8)
                td_ = SQ[:, 256:512].rearrange("p (h d) -> p h d", h=8)
                tt(DVE, ta, qv[:, :, 0:32], cosb, ALU.mult, qk.res + rp.res, TA.res)
                tt(DVE, tb, qv[:, :, 32:64], sinb, ALU.mult, qk.res + rp.res, TB.res)
                tt(DVE, tc_, qv[:, :, 32:64], cosb, ALU.mult, qk.res + rp.res, SQ.res)
                tt(DVE, td_, qv[:, :, 0:32], sinb, ALU.mult, qk.res + rp.res, SQ.res)
                tt(DVE, ov[:, :, 32:64], tc_, td_, ALU.add, SQ.res, qo.res)
                tt(DVE, ov[:, :, 0:32], ta, tb, ALU.subtract, TA.res + TB.res, qo.res)
                tt(DVE, ov, ov, ss[:, 0:8].unsqueeze(2).to_broadcast([128, 8, 64]), ALU.mult, qo.res + ss.res, qo.res)
                cp(ACT, vf[:, :], p2[:, 0:256], [r2_], vf.res)
                cp(ACT, kvslot[:, 256:512], p2[:, 0:256], [r2_], kvslot.res)
                cp(ACT, QB[i % 2][:, :], qo[:, :], qo.res, QB[i % 2].res)
                yield
                pt, rt = bank()
                ptb = pt.bitcast(BF16)
                qb = QB[i % 2]
                for j in range(4):
                    tr(ptb[:, j * 128:(j + 1) * 128], qb[:, j * 128:(j + 1) * 128], IDB[:, :], qb.res + IDB.res, [rt])
                cp(ACT, qt[:, :, :].rearrange("p a b -> p (a b)"), ptb[:, 0:256], [rt], qt.res)
                cp(ACT, kvslot[:, 0:256], ptb[:, 256:512], [rt], kvslot.res)
                if orows is not None:
                    kd, vd = orows
                    dma(SP, kd, qo[:, 256:512], qo.res, ())
                    dma(SP, vd, vf[:, :], vf.res, ())
                if send_dst is not None:
                    sd, rs = send_dst
                    dma(SP, sd, kvslot[:, :], kvslot.res, [rs])
                yield
                if prev is None:
                    return
                kvc, kvp = kvslot, prev
                for half in range(2):
                    rows = slice(64 * half, 64 * half + 64)
                    pb, pr = bank()
                    mm(pb[:, :], IDB[:, :], MASK[:, mprev, :], True, False, IDB.res + MASK.res, [pr])
                    for j, (kvb, pair) in enumerate(((kvp, 0), (kvp, 1), (kvc, 0), (kvc, 1))):
                        mm(pb[:, 128 * j:128 * j + 128], kvb[rows, pair * 128:(pair + 1) * 128], qt[rows, pair, :],
                           False, j == 3, kvb.res + qt.res, [pr])
                    act(pts[:, half * 512:(half + 1) * 512], pb[:, :], AF.Exp, [pr], pts.res, scale=0.125)
                yield
                po, pro = bank()
                for h in range(4):
                    pair, half = h // 2, h % 2
                    rows = slice(64 * half, 64 * half + 64)
                    vsl = slice(256 + 64 * h, 256 + 64 * h + 64)
                    pp = pts[:, half * 512 + pair * 128:half * 512 + pair * 128 + 128]
                    pc = pts[:, half * 512 + 256 + pair * 128:half * 512 + 256 + pair * 128 + 128]
                    mm(po[rows, pair * 128:(pair + 1) * 128], kvp[:, vsl], pp, True, False, kvp.res + pts.res, [pro])
                    mm(po[rows, pair * 128:(pair + 1) * 128], kvc[:, vsl], pc, False, True, kvc.res + pts.res, [pro])
                    mm(po[rows, 256 + pair * 128:256 + (pair + 1) * 128], ONES[:, :], pp, True, False,
                       ONES.res + pts.res, [pro])
                    mm(po[rows, 256 + pair * 128:256 + (pair + 1) * 128], ONES[:, :], pc, False, True,
                       ONES.res + pts.res, [pro])
                av = ACC[:, :, cols]
                ares = [ACC.res[t] for t in tl]
                pov = po[:, :].rearrange("p (a b) -> p a b", a=4)
                if first:
                    cp(DVE, av, pov, [pro], ares)
                else:
                    tt(DVE, av, av, pov, ALU.add, [pro] + ares, ares)

            def run_pipeline(gens):
                n = len(gens)
                for it in range(n + 3):
                    for k in (3, 2, 1, 0):
                        idx = it - k
                        if 0 <= idx < n:
                            next(gens[idx], None)

            def send_blocks(g, blks, W, wv, sb0):
                sd, rs = (send[l], R_send[l]) if g == 2 else (sendb[l], R_sendb[l])
                for si, b in enumerate(blks):
                    kvs = KV[si % 8]
                    project(g, b, W, wv, kvs)
                    dma(SP, sd[(sb0 + si) * 128:(sb0 + si + 1) * 128, :], kvs[:, :], kvs.res, [rs])

            def collective(sd, rv, rs, rr):
                if os.environ.get("KNOCC"):
                    return
                S.op(POOL, lambda e: e.collective_compute("AllGather", ALU.bypass, replica_groups=RG,
                                                          ins=[sd], outs=[rv]),
                     [rs], [rr], dma=True, inc=1, semname="cc_sem")
                S.op(POOL, lambda e: e.memset(CCD[:, :], 0.0), [rr], CCD.res)

            def out_rows(g, b):
                if g == 0:
                    return (kp[0][l], vp[0][l]) if b == 15 else None
                if g == 1:
                    if b < 12:
                        return None
                    r1 = b - 12
                    return (kp[1][l].rearrange("(i f) c -> f i c", f=4)[r1],
                            vp[1][l].rearrange("(i f) c -> f i c", f=4)[r1])
                return (kp[2][l].rearrange("(i f) c -> f i c", f=16)[b],
                        vp[2][l].rearrange("(i f) c -> f i c", f=16)[b])

            W2, wv2 = load_wq(2)
            run_pipeline([block_task(2, b, W2, wv2, KV[b % 8],
                                     send_dst=(send[l][b * 128:(b + 1) * 128, :], R_send[l])) for b in range(16)])
            W0, wv0 = load_wq(0)
            W1, wv1 = load_wq(1)
            collective(send[l], recv[l], R_send[l], R_recv[l])
            stage(3)
            run_pipeline([block_task(0, 15, W0, wv0, KV[0], send_dst=(sendb[l][0:128, :], R_sendb[l]))] +
                         [block_task(1, 12 + i_, W1, wv1, KV[1 + i_],
                                     send_dst=(sendb[l][(1 + i_) * 128:(2 + i_) * 128, :], R_sendb[l]))
                          for i_ in range(4)])
            WU = wslot()
            wu = WU[:, 0:4096].rearrange("p (a b) -> p a b", a=8)
            dma(POOL, wu, kcv(w_in[l])[:, :, 0:512], (), WU.res)
            pb, pr = bank()
            for kc in range(8):
                mm(pb[:, :], HT[:, kc, tcols(15)], wu[:, kc, :], kc == 0, kc == 7, [HT.res[15]] + WU.res, [pr])
            cp(ACT, H2[0][:, :], pb[:, :], [pr], H2[0].res)
            cp(ACT, SQ[:, :], pb[:, :], [pr], SQ.res)
            dma(SP, sendb[l][5 * 128:6 * 128, :], H2[0][:, :], H2[0].res, [R_sendb[l]])
            dma(SP, poolp[l], SQ[113:128, :], SQ.res, ())
            W0, wv0 = load_wq(0)
            collective(sendb[l], recvb[l], R_sendb[l], R_recvb[l])
            stage(3.5)
            tasks = [block_task(0, 0, W0, wv0, KV[0])]
            for b in range(1, 16):
                tasks.append(block_task(0, b, W0, wv0, KV[b % 8], orows=out_rows(0, b), prev=KV[(b - 1) % 8],
                                        mprev=1, first=True))
            tasks.append(block_task(0, 0, W0, wv0, KV[0], prev=H0, mprev=2, first=True,
                                    hist=(H0, recvb[l][0:128, :], R_recvb[l])))
            run_pipeline(tasks)
            project(0, 0, W0, wv0, None, sample=True)
            W1, wv1 = load_wq(1)
            tasks = [block_task(1, b, W1, wv1, KV[b % 8]) for b in range(4)]
            for b in range(4, 16):
                tasks.append(block_task(1, b, W1, wv1, KV[b % 8], orows=out_rows(1, b), prev=KV[(b - 4) % 8],
                                        mprev=1, first=False))
            for b in range(4):
                tasks.append(block_task(1, b, W1, wv1, KV[b % 8], prev=H1[b], mprev=2, first=False,
                                        hist=(H1[b], recvb[l][(1 + b) * 128:(2 + b) * 128, :], R_recvb[l])))
            run_pipeline(tasks)
            project(1, 0, W1, wv1, None, sample=True)
            W2, wv2 = load_wq(2)
            run_pipeline([block_task(2, b, W2, wv2, KV[b % 8], orows=out_rows(2, b), prev=H2[b % 3], mprev=2,
                                     first=False, hist=(H2[b % 3], recv[l][b * 128:(b + 1) * 128, :], R_recv[l]))
                          for b in range(16)])
            project(2, 0, W2, wv2, None, sample=True)

            stage(4)
            AYT = alloc("AYT", [128, 2, TT], BF16, R3, nres=5)
            RD = [alloc("RD%d" % i, [128, 2, 512], F32, R4 + i * 4096) for i in range(2)]
            for tg in range(4):
                cs = slice(tg * 512, (tg + 1) * 512)
                ares = [ACC.res[t] for t in range(4 * tg, 4 * tg + 4)]
                rd = RD[tg % 2]
                recip(rd[:, :, :], ACC[:, 2:4, cs], ares, rd.res)
                tt(DVE, AYT[:, :, cs], ACC[:, 0:2, cs], rd[:, :, :], ALU.mult, ares + rd.res, [AYT.res[tg]])

            stage(5)
            KC = alloc("KC", [128, NS, 256], F32, R1)
            VC = alloc("VC", [128, NS, 256], F32, R1 + 4096)
            PROD = alloc("PROD", [128, NS * 256], F32, R1 + 8192)
            PVP = alloc("PVP", [128, NS, 260], F32, R1 + 12288)
            SCO = alloc("SCO", [128, 16], F32, R1 + 12288 + 4160)
            SPX = alloc("SPX", [NS, 16], F32, R1 + 12288 + 4160 + 64)
            AYS = alloc("AYS", [NS, 256], BF16, R1 + 12288 + 4160 + 128)
            for g in range(3):
                dil = DILS[g]
                dma(SP, KC[:, :, :], ck[g][l][:, 0:WINS[g]:dil, :].rearrange("b j c -> j b c"), (), KC.res)
                dma(SP, VC[:, :, :], cv[g][l][:, 0:WINS[g]:dil, :].rearrange("b j c -> j b c"), (), VC.res)
                pq = [bank(), bank()]
                for bb in range(NS):
                    pb, pr = pq[bb // 2]
                    mm(pb[:, (bb % 2) * 256:(bb % 2) * 256 + 256], SEL[:, bb, :], QS[:, g, :], True, True,
                       SEL.res + QS.res, [pr])
                for hf in range(2):
                    pb, pr = pq[hf]
                    tt(DVE, PROD[:, hf * 512:(hf + 1) * 512],
                       KC[:, 2 * hf:2 * hf + 2, :].rearrange("p a b -> p (a b)"), pb[:, :], ALU.mult,
                       KC.res + [pr], PROD.res)
                red(DVE, SCO[:, :], PROD[:, :].rearrange("p (a d) -> p a d", d=64), PROD.res, SCO.res)
                act(PVP[:, :, 256:260], SCO[:, :].rearrange("p (a b) -> p a b", a=NS), AF.Exp, SCO.res, PVP.res,
                    scale=0.125)
                tt(DVE, PVP[:, :, 0:256].rearrange("p a (h d) -> p a h d", h=4),
                   VC[:, :, :].rearrange("p a (h d) -> p a h d", h=4),
                   PVP[:, :, 256:260].unsqueeze(3).to_broadcast([128, NS, 4, 64]), ALU.mult,
                   VC.res + PVP.res, PVP.res)
                pb, pr = bank()
                for bb in range(NS):
                    mm(pb[0:NS, 0:260], SELC[:, bb, :], PVP[:, bb, :], bb == 0, bb == NS - 1, SELC.res + PVP.res, [pr])
                tt(DVE, SACC[:, :], SACC[:, :], pb[0:NS, 0:260], ALU.add, SACC.res + [pr], SACC.res)
            recip(SPX[:, 4:8], SACC[:, 256:260], SACC.res, SPX.res)
            tt(DVE, AYS[:, :].rearrange("p (h d) -> p h d", h=4), SACC[:, 0:256].rearrange("p (h d) -> p h d", h=4),
               SPX[:, 4:8].unsqueeze(2).to_broadcast([NS, 4, 64]), ALU.mult, SACC.res + SPX.res, AYS.res)
            pt, rt = bank()
            ptb = pt.bitcast(BF16)
            for pr_ in range(2):
                tr(ptb[:, pr_ * 128:pr_ * 128 + NS], AYS[:, pr_ * 128:(pr_ + 1) * 128], IDB[0:NS, 0:NS],
                   AYS.res + IDB.res, [rt])
            cp(ACT, AYT[:, :, TOK:TT], ptb[:, 0:256].rearrange("p (a b) -> p a b", a=2)[:, :, 0:NS], [rt],
               [AYT.res[4]])

            stage(6)
            PYT = alloc("PYT", [128, 4, TT], BF16, R2, nres=NT + 1)
            o = R1 + 17408
            UB = [alloc("UB%d" % i, [128, 512], BF16, o + i * 1024) for i in range(3)]
            o += 3072
            UH = alloc("UH", [128, 512], BF16, o)
            o += 1024
            UF = alloc("UF", [128, 512], F32, o)
            o += 2048
            PTB = [alloc("PTB%d" % i, [128, 512], BF16, o + i * 1024) for i in range(2)]
            o += 2048
            AM = alloc("AM", [128, 16, 128], BF16, o)
            o += 4096
            assert o <= R2
            o = R4
            ST = alloc("ST", [NS * 15, 512], F32, o)
            o += 2048
            USF = alloc("USF", [NS, 512], F32, o)
            o += 2048
            PSB = alloc("PSB", [NS, 512], BF16, o)
            o += 1024
            PTS_ = alloc("PTSs", [128, 4, NS], BF16, o)
            o += 64
            assert o <= top
            STG2 = alloc("STG2", [128, 2048], F32, R1)
            dma(SP, STG2[:, :], c_amat.rearrange("p a b c -> p (a b c)"), (), STG2.res)
            cp(DVE, AM[:, :, :].rearrange("p a b -> p (a b)"), STG2[:, :], STG2.res, AM.res)
            dma(POOL, WG[:, :, :], w_pool_grp[l].rearrange("g c e -> c g e"), (), WG.res)
            dma(SP, PSC[:, :], pscT[l], (), PSC.res)
            WU = wslot()
            wu = WU[:, 0:4096].rearrange("p (a b) -> p a b", a=8)
            dma(POOL, wu, kcv(w_in[l])[:, :, 0:512], (), WU.res)

            def uproj(t, dst, fp32dst=None):
                np_ = 128 if t < NT else NS
                pb, pr = bank()
                for kc in range(8):
                    mm(pb[0:np_, :], HT[:, kc, tcols(t)], wu[:, kc, :], kc == 0, kc == 7, [HT.res[t]] + WU.res, [pr])
                if dst is not None:
                    cp(ACT, dst[0:np_, :], pb[0:np_, :], [pr], dst.res)
                if fp32dst is not None:
                    cp(ACT, fp32dst[0:np_, :], pb[0:np_, :], [pr], fp32dst.res)

            def pool_tile(t, ucur, uprev, acur, aprev):
                pb, pr = bank()
                for gi in range(4):
                    gs = slice(gi * 128, (gi + 1) * 128)
                    mm(pb[:, gs], ucur[:, gs], AM[:, acur * 4 + gi, :], True, False, ucur.res + AM.res, [pr])
                    mm(pb[:, gs], uprev[:, gs], AM[:, aprev * 4 + gi, :], False, True, uprev.res + AM.res, [pr])
                ptb_ = PTB[t % 2]
                cp(ACT, ptb_[:, :], pb[:, :], [pr], ptb_.res)
                pb2, pr2 = bank()
                for gi in range(4):
                    gs = slice(gi * 128, (gi + 1) * 128)
                    mm(pb2[:, gs], WG[:, gi, :], ptb_[:, gs], True, True, WG.res + ptb_.res, [pr2])
                tt(DVE, PYT[:, :, tcols(t)], pb2[:, :].rearrange("p (a b) -> p a b", a=4),
                   PSC[:, :].unsqueeze(2).to_broadcast([128, 4, 128]), ALU.mult, [pr2] + PSC.res, [PYT.res[t]])

            uproj(0, UB[0])
            uproj(1, UB[1])
            for t in range(1, NT):
                if t + 1 < NT:
                    uproj(t + 1, UB[(t + 1) % 3])
                pool_tile(t, UB[t % 3], UB[(t - 1) % 3], 0, 1)
            dma(SP, UH[:, :], recvb[l][5 * 128:6 * 128, :], [R_recvb[l]], UH.res)
            uproj(0, UB[0])
            pool_tile(0, UB[0], UH, 2, 3)
            uproj(NT, None, USF)
            dma(SP, ST[:, :], spool[l], (), ST.res)
            dma(SP, pools[l][:, 0:14, :], spool[l].rearrange("(b r) c -> b r c", r=15)[:, 1:15, :], (), ())
            dma(SP, pools[l][:, 14, :], USF[:, :], USF.res, ())
            pb, pr = bank()
            for gi in range(4):
                gs = slice(gi * 128, (gi + 1) * 128)
                mm(pb[0:NS, gs], PSEL[:, gi, :], ST[:, gs], True, True, PSEL.res + ST.res, [pr])
            tt(DVE, UF[0:NS, :], USF[:, :], PCOEF[:, :], ALU.mult, USF.res + PCOEF.res, UF.res)
            tt(DVE, PSB[:, :], UF[0:NS, :], pb[0:NS, :], ALU.add, UF.res + [pr], PSB.res)
            pt, rt = bank()
            ptb = pt.bitcast(BF16)
            for gi in range(4):
                tr(ptb[:, gi * 128:gi * 128 + NS], PSB[:, gi * 128:(gi + 1) * 128], IDB[0:NS, 0:NS],
                   PSB.res + IDB.res, [rt])
            cp(ACT, PTS_[:, :, :], ptb[:, 0:512].rearrange("p (a b) -> p a b", a=4)[:, :, 0:NS], [rt], PTS_.res)
            pb2, pr2 = bank()
            for gi in range(4):
                mm(pb2[:, gi * NS:(gi + 1) * NS], WG[:, gi, :], PTS_[:, gi, :], True, True, WG.res + PTS_.res, [pr2])
            tt(DVE, PYT[:, :, TOK:TT], pb2[:, 0:4 * NS].rearrange("p (a b) -> p a b", a=4),
               PSC[:, :].unsqueeze(2).to_broadcast([128, 4, NS]), ALU.mult, [pr2] + PSC.res, [PYT.res[NT]])

            stage(7)
            MGT = alloc("MGT", [128, 8, TT], BF16, R1, nres=5)
            SG = [alloc("SG%d" % i, [128, 512], F32, R4 + i * 2048) for i in range(4)]
            TG = [alloc("TG%d" % i, [128, 512], F32, R4 + 8192 + i * 2048) for i in range(2)]
            for f in range(8):
                W = wslot()
                wap = W[:, 0:1024].rearrange("p (a b) -> p a b", a=8)
                waa = W[:, 1024:2048].rearrange("p (a b) -> p a b", a=8)
                wpb = W[:, 2048:2560].rearrange("p (a b) -> p a b", a=4)
                wab = W[:, 2560:2816].rearrange("p (a b) -> p a b", a=2)
                fs = slice(f * 128, (f + 1) * 128)
                dma(POOL, wap, kcv(w_in[l])[:, :, 2816 + f * 128:2816 + (f + 1) * 128], (), W.res)
                dma(POOL, waa, kcv(w_in[l])[:, :, 3840 + f * 128:3840 + (f + 1) * 128], (), W.res)
                dma(POOL, wpb, kcv(w_pool_br[l])[:, :, fs], (), W.res)
                dma(POOL, wab, kcv(w_attn_br[l])[:, :, fs], (), W.res)
                for tg in range(5):
                    cs = slice(tg * 512, (tg + 1) * 512) if tg < 4 else slice(TOK, TT)
                    n = 512 if tg < 4 else NS
                    tl = list(range(4 * tg, 4 * tg + 4)) if tg < 4 else [NT]
                    hres = [HT.res[t] for t in tl]
                    pyres = [PYT.res[t] for t in tl]
                    b1, r1_ = bank()
                    b2, r2_ = bank()
                    b3, r3_ = bank()
                    b4, r4_ = bank()
                    for kc in range(8):
                        mm(b1[:, 0:n], wap[:, kc, :], HT[:, kc, cs], kc == 0, kc == 7, W.res + hres, [r1_])
                    for kc in range(8):
                        mm(b2[:, 0:n], waa[:, kc, :], HT[:, kc, cs], kc == 0, kc == 7, W.res + hres, [r2_])
                    for gi in range(4):
                        mm(b3[:, 0:n], wpb[:, gi, :], PYT[:, gi, cs], gi == 0, gi == 3, W.res + pyres, [r3_])
                    for p_ in range(2):
                        mm(b4[:, 0:n], wab[:, p_, :], AYT[:, p_, cs], p_ == 0, p_ == 1, W.res + [AYT.res[tg]], [r4_])
                    k = (f * 5 + tg) % 2
                    sp_, sa_, tg_ = SG[2 * k], SG[2 * k + 1], TG[k]
                    act(sp_[:, 0:n], b1[:, 0:n], AF.Sigmoid, [r1_], sp_.res)
                    act(sa_[:, 0:n], b2[:, 0:n], AF.Sigmoid, [r2_], sa_.res)
                    tt(DVE, sp_[:, 0:n], sp_[:, 0:n], b3[:, 0:n], ALU.mult, sp_.res + [r3_], sp_.res)
                    tt(DVE, tg_[:, 0:n], sa_[:, 0:n], b4[:, 0:n], ALU.mult, sa_.res + [r4_], tg_.res)
                    tt(DVE, MGT[:, f, cs], sp_[:, 0:n], tg_[:, 0:n], ALU.add, sp_.res + tg_.res, [MGT.res[tg]])

            stage(8)
            MPA = alloc("MPA", [128, D], F32, R2)
            MPB = alloc("MPB", [128, D], F32, R2 + 4096)
            MSA = alloc("MSA", [NS, D], F32, R2 + 8192)
            MSB = alloc("MSB", [NS, D], F32, R2 + 12288)
            BB = alloc("BB", [128, D], F32, R3)
            NG = alloc("NG", [128, D], F32, R3 + 4096)
            ada(l, 2, MPA, MSA, BB, NG, None)
            TO = [alloc("TO%d" % i, [128, 512], F32, R4 + i * 2048) for i in range(2)]

            def resid_update(t, c, pb, pr, MP, MS, k):
                xt, xr, np_ = xtile(t)
                M = MP if t < NT else MS
                cs = slice(c * 512, (c + 1) * 512)
                to = TO[k % 2]
                tt(DVE, to[0:np_, :], pb[0:np_, :], M[0:np_, cs], ALU.mult, [pr] + M.res, to.res)
                tt(DVE, xt[:, cs], xt[:, cs], to[0:np_, :], ALU.add, [xr] + to.res, [xr])

            kk = 0
            for c in range(2):
                W = wslot()
                wo = W[:, 0:4096].rearrange("p (a b) -> p a b", a=8)
                dma(POOL, wo, kcv(w_out[l])[:, :, c * 512:(c + 1) * 512], (), W.res)
                for t in range(NT + 1):
                    np_ = 128 if t < NT else NS
                    tg = t // 4 if t < NT else 4
                    pb, pr = bank()
                    for kc in range(8):
                        mm(pb[0:np_, :], MGT[:, kc, tcols(t)], wo[:, kc, :], kc == 0, kc == 7, [MGT.res[tg]] + W.res, [pr])
                    resid_update(t, c, pb, pr, MPA, MSA, kk)
                    kk += 1

            stage(9)
            ada(l, 3, MPB, MSB, BB, NG, None)
            MPC = alloc("MPC", [128, D], F32, R1)
            MSC = alloc("MSC", [NS, D], F32, R1 + 4096)
            ada(l, 4, MPC, MSC, BB, NG, norm2_g)
            norm_phase(MPC, MPB, MSC, MSB, R4)
            stage(10)
            ada(l, 5, MPA, MSA, BB, NG, None)
            AT = alloc("AT", [128, 8, TT], BF16, R1, nres=5)
            RL = [alloc("RL%d" % i, [128, 512], F32, R4 + 4096 + i * 2048) for i in range(2)]
            kk = 0
            for j in range(4):
                for c2 in range(2):
                    W = wslot()
                    wup = W[:, 0:4096].rearrange("p (a b) -> p a b", a=8)
                    dma(POOL, wup, kcv(w_up[l])[:, :, 1024 * j + 512 * c2:1024 * j + 512 * (c2 + 1)], (), W.res)
                    for fc in range(4):
                        for tg in range(5):
                            cs = slice(tg * 512, (tg + 1) * 512) if tg < 4 else slice(TOK, TT)
                            n = 512 if tg < 4 else NS
                            tl = list(range(4 * tg, 4 * tg + 4)) if tg < 4 else [NT]
                            hres = [HT.res[t] for t in tl]
                            pb, pr = bank()
                            for kc in range(8):
                                mm(pb[:, 0:n], wup[:, kc, fc * 128:(fc + 1) * 128], HT[:, kc, cs], kc == 0, kc == 7,
                                   W.res + hres, [pr])
                            rl = RL[kk % 2]
                            kk += 1
                            act(rl[:, 0:n], pb[:, 0:n], AF.Relu, [pr], rl.res)
                            tt(DVE, AT[:, 4 * c2 + fc, cs], rl[:, 0:n], rl[:, 0:n], ALU.mult, rl.res, [AT.res[tg]])
                for c in range(2):
                    W = wslot()
                    wd = W[:, 0:4096].rearrange("p (a b) -> p a b", a=8)
                    dma(POOL, wd, kcv(w_down[l])[:, 8 * j:8 * j + 8, c * 512:(c + 1) * 512], (), W.res)
                    for t in range(NT + 1):
                        np_ = 128 if t < NT else NS
                        tg = t // 4 if t < NT else 4
                        pb, pr = bank()
                        for kc in range(8):
                            mm(pb[0:np_, :], AT[:, kc, tcols(t)], wd[:, kc, :], kc == 0, kc == 7,
                               [AT.res[tg]] + W.res, [pr])
                        resid_update(t, c, pb, pr, MPA, MSA, kk)
                        kk += 1

          except _Stop:
            break
        for t in range(NT):
            dma(SP, yp[t * 128:(t + 1) * 128, :], X[:, t, :], [X.res[t]], ())
        dma(SP, ys, XS[:, :], XS.res, ())

        S.finalize()
        sems = {n: es.enter_context(nc.semaphore(n)) for n in sorted(S.semnames)}
        with nc.Block() as block:
            @block.tensor
            def _(e):
                S.emit_engine(PE, e, sems)

            @block.scalar
            def _(e):
                S.emit_engine(ACT, e, sems)

            @block.vector
            def _(e):
                S.emit_engine(DVE, e, sems)

            @block.gpsimd
            def _(e):
                S.emit_engine(POOL, e, sems)

            @block.sync
            def _(e):
                S.emit_engine(SP, e, sems)
    return nc


def _consts(core):
    half = core % 2
    c = {}
    c["c_ident"] = np.eye(128, dtype=np.float32)
    kk = np.arange(128)[:, None]
    qq = np.arange(128)[None, :]
    cur = np.where(kk <= qq, 0.0, NEG).astype(np.float32)
    prev = np.where(kk >= qq, 0.0, NEG).astype(np.float32)
    pf = prev if half == 1 else np.full((128, 128), NEG, np.float32)
    m = np.stack([np.tile(cur, (1, 4)), np.concatenate([prev, prev, cur, cur], 1),
                  np.concatenate([pf, pf, cur, cur], 1)], axis=1)
    c["c_mask"] = np.ascontiguousarray(m, dtype=np.float32)
    inv = 10000.0 ** (-np.arange(0, 64, 2, dtype=np.float64) / 64)
    rope = np.zeros((48, 128, 64), np.float32)
    i = np.arange(128)
    for g in range(3):
        for b in range(16):
            if g == 0:
                tk = 128 * b + i
            elif g == 1:
                tk = 512 * (b // 4) + (b % 4) + 4 * i
            else:
                tk = b + 16 * i
            pos = (2048 * half + tk).astype(np.float32)
            ang = (pos[:, None] * inv[None, :].astype(np.float32)).astype(np.float32)
            rope[g * 16 + b, :, 0:32] = np.cos(ang)
            rope[g * 16 + b, :, 32:64] = np.sin(ang)
    c["c_rope"] = rope
    angs = (np.float32(8192.0) * inv.astype(np.float32)).astype(np.float32)
    c["c_ropes"] = np.tile(np.concatenate([np.cos(angs), np.sin(angs)])[None, :], (NS, 1)).astype(np.float32)
    am = np.zeros((128, 4, 4, 128), np.float32)
    tp = np.arange(128)[:, None]
    t = np.arange(128)[None, :]
    for gi, w in enumerate((2, 4, 8, 16)):
        inwin = (tp <= t) & (t - tp < w)
        curm = np.where(inwin, 1.0 / w, 0.0) - np.eye(128)
        prevm = np.where(t + 128 - tp < w, 1.0 / w, 0.0)
        am[:, 0, gi, :] = curm
        am[:, 1, gi, :] = prevm
        if half == 0:
            cntv = np.minimum(t + 1, w).astype(np.float64)
            am[:, 2, gi, :] = np.where(inwin, 1.0 / cntv, 0.0) - np.eye(128)
            am[:, 3, gi, :] = 0.0
        else:
            am[:, 2, gi, :] = curm
            am[:, 3, gi, :] = prevm
    c["c_amat"] = am
    sel = np.zeros((NS, NS, 128), np.float32)
    selc = np.zeros((128, NS, NS), np.float32)
    for b in range(NS):
        sel[b, b, :] = 1.0
        selc[:, b, b] = 1.0
    c["c_sel"] = sel
    c["c_selc"] = selc
    psel = np.zeros((NS * 15, 4, NS), np.float32)
    pcoef = np.zeros((NS, 512), np.float32)
    for gi, w in enumerate((2, 4, 8, 16)):
        for b in range(NS):
            for r in range(16 - w, 15):
                psel[b * 15 + r, gi, b] = 1.0 / w
        pcoef[:, gi * 128:(gi + 1) * 128] = 1.0 / w - 1.0
    c["c_psel"] = psel
    c["c_pcoef"] = pcoef
    return c


_NC_CACHE = {}


def kernel(x_prompt, x_sample, cache_k_w128, cache_v_w128, cache_k_w512, cache_v_w512,
           cache_k_w2048, cache_v_w2048, state_pool, c_prompt, c_sample, norm1_g, norm2_g,
           w_ada, b_ada, w_in, q_norm_g, k_norm_g, w_pool_grp, pool_scale, w_pool_br,
           w_attn_br, w_out, w_up, w_down):
    f = lambda a: np.ascontiguousarray(np.asarray(a), dtype=np.float32)
    L = NLAYER
    if L < DEPTH:
        (cache_k_w128, cache_v_w128, cache_k_w512, cache_v_w512, cache_k_w2048, cache_v_w2048, state_pool,
         norm1_g, norm2_g, w_ada, b_ada, w_in, q_norm_g, k_norm_g, w_pool_grp, pool_scale, w_pool_br,
         w_attn_br, w_out, w_up, w_down) = [np.asarray(a)[:L] for a in (
            cache_k_w128, cache_v_w128, cache_k_w512, cache_v_w512, cache_k_w2048, cache_v_w2048, state_pool,
            norm1_g, norm2_g, w_ada, b_ada, w_in, q_norm_g, k_norm_g, w_pool_grp, pool_scale, w_pool_br,
            w_attn_br, w_out, w_up, w_down)]
    x_prompt, x_sample = f(x_prompt), f(x_sample)
    cks = [f(cache_k_w128), f(cache_k_w512), f(cache_k_w2048)]
    cvs = [f(cache_v_w128), f(cache_v_w512), f(cache_v_w2048)]
    state_pool, c_prompt, c_sample = f(state_pool), f(c_prompt), f(c_sample)
    shared = {
        "norm1_g": f(norm1_g), "norm2_g": f(norm2_g), "w_ada": f(w_ada), "b_ada": f(b_ada), "w_in": f(w_in),
        "qk_g": f(np.concatenate([np.asarray(q_norm_g), np.asarray(k_norm_g)], axis=1)),
        "w_pool_grp": f(w_pool_grp),
        "pscT": f(np.asarray(pool_scale).reshape(L, 4, 128).transpose(0, 2, 1)),
        "w_pool_br": f(w_pool_br), "w_attn_br": f(w_attn_br), "w_out": f(w_out), "w_up": f(w_up),
        "w_down": f(w_down),
    }
    in_maps = []
    for c in range(8):
        b, h = c // 2, c % 2
        m = dict(shared)
        m["xp"] = np.ascontiguousarray(x_prompt[b, h * TOK:(h + 1) * TOK, :])
        m["xs"] = np.ascontiguousarray(x_sample[NS * c:NS * (c + 1), 0, :])
        m["cpT"] = np.ascontiguousarray(c_prompt[b].reshape(8, 128).T)
        m["csT"] = np.ascontiguousarray(c_sample[NS * c:NS * (c + 1)].reshape(NS, 8, 128).transpose(2, 1, 0))
        for g in range(3):
            m["ck%d" % g] = np.ascontiguousarray(cks[g][:, NS * c:NS * (c + 1)].reshape(L, NS, WINS[g], 256))
            m["cv%d" % g] = np.ascontiguousarray(cvs[g][:, NS * c:NS * (c + 1)].reshape(L, NS, WINS[g], 256))
        m["spool"] = np.ascontiguousarray(state_pool[:, NS * c:NS * (c + 1)].reshape(L, NS * 15, 512))
        m.update(_consts(c))
        in_maps.append(m)
    if "nc" not in _NC_CACHE:
        _NC_CACHE["nc"] = build_program()
    res = run_bass_kernel_spmd(_NC_CACHE["nc"], in_maps, core_ids=list(range(8)))
    R = res.results
    B = 4
    y_prompt = np.zeros((B, 2 * TOK, D), np.float32)
    y_sample = np.zeros((32, 1, D), np.float32)
    for c in range(8):
        y_prompt[c // 2, (c % 2) * TOK:(c % 2 + 1) * TOK] = R[c]["yp"]
        y_sample[NS * c:NS * (c + 1), 0] = R[c]["ys"]
    outs = [y_prompt, y_sample]
    for g in range(3):
        for nm in ("kp", "vp"):
            outs.append(np.stack([R[2 * b + 1]["%s%d" % (nm, g)] for b in range(B)], axis=1)
                        .reshape(L, B, WINS[g], 4, 64).astype(np.float32))
    outs.append(np.stack([R[2 * b + 1]["poolp"] for b in range(B)], axis=1).astype(np.float32))
    for g in range(3):
        for nm in ("ks", "vs"):
            outs.append(np.concatenate([R[c]["%s%d" % (nm, g)] for c in range(8)], axis=1)
                        .reshape(L, 32, 1, 4, 64).astype(np.float32))
    outs.append(np.concatenate([R[c]["pools"] for c in range(8)], axis=1).astype(np.float32))
    return tuple(outs)
```

```python
import contextlib
import os
import numpy as np
import concourse.bass as bass
import concourse.mybir as mybir
from concourse.bass_utils import run_bass_kernel_spmd

F32 = mybir.dt.float32
BF16 = mybir.dt.bfloat16
ALU = mybir.AluOpType
AF = mybir.ActivationFunctionType
AX = mybir.AxisListType
PE, ACT, DVE, POOL, SP = "pe", "act", "dve", "pool", "sp"
ENGS = (PE, ACT, DVE, POOL, SP)

DEPTH = 4
D = 1024
NT = 16
TOK = 2048
NS = 4
TT = TOK + NS
INW = 4864
DFF = 4096
EPS = 1e-6
WINS = (128, 512, 2048)
DILS = (1, 4, 16)
NBLK_SEND = (1, 4, 16)
RG = [[0, 1], [2, 3], [4, 5], [6, 7]]
NEG = -30000.0
STAGE = float(os.environ.get('KSTAGE', '99'))
NLAYER = int(os.environ.get('KLAYERS', str(DEPTH)))


class _Stop(Exception):
    pass


def stage(n):
    if STAGE <= n:
        raise _Stop()


class Res:
    __slots__ = ("name", "writers", "readers")

    def __init__(self, name):
        self.name = name
        self.writers = {}
        self.readers = {}

    def inherit(self, other):
        for d in (other.writers, other.readers):
            for k, o in d.items():
                c = self.writers.get(k)
                if c is None or c.seq < o.seq:
                    self.writers[k] = o


class Op:
    __slots__ = ("eng", "fn", "deps", "token", "needs_inc", "is_dma", "key", "inc", "seq")

    def __init__(self, eng, fn, is_dma=False):
        self.eng = eng
        self.fn = fn
        self.deps = []
        self.token = None
        self.needs_inc = False
        self.is_dma = is_dma
        self.key = eng
        self.inc = 16


class Sched:
    SEM_ROT = 30000
    NDMA = {SP: 14, POOL: 6, ACT: 2}

    def __init__(self):
        self.ops = {e: [] for e in ENGS}
        self.dma_rr = {q: 0 for q in self.NDMA}
        self.dma_last = {}
        self.dma_val = {}
        self.semnames = set()
        self.nseq = 0

    def op(self, eng, fn, reads=(), writes=(), dma=False, inc=16, semname=None):
        o = Op(eng, fn, is_dma=dma)
        o.inc = inc
        self.nseq += 1
        o.seq = self.nseq
        deps = []
        if dma:
            if semname is None:
                k = self.dma_rr[eng]
                self.dma_rr[eng] = (k + 1) % self.NDMA[eng]
                sname = "dq_%s_%d" % (eng, k)
            else:
                sname = semname
            self.semnames.add(sname)
            self.dma_val[sname] = self.dma_val.get(sname, 0) + inc
            o.token = (sname, self.dma_val[sname])
            o.key = sname
            o.needs_inc = True
            prev = self.dma_last.get(sname)
            if prev is not None:
                deps.append(prev)
            self.dma_last[sname] = o
        for r in reads:
            deps.extend(r.writers.values())
        for r in writes:
            deps.extend(r.writers.values())
            deps.extend(r.readers.values())
        for r in reads:
            r.readers[o.key] = o
        for r in writes:
            r.readers = {}
            r.writers[o.key] = o
        seen = set()
        for d in deps:
            if id(d) in seen or d is o:
                continue
            seen.add(id(d))
            if d.eng == PE and eng == PE and not d.is_dma and not dma:
                continue
            o.deps.append(d)
            if not d.is_dma:
                d.needs_inc = True
        self.ops[eng].append(o)
        return o

    def finalize(self):
        for e in ENGS:
            c = 0
            idx = 0
            for o in self.ops[e]:
                if o.is_dma or not o.needs_inc:
                    continue
                if c >= self.SEM_ROT:
                    idx += 1
                    c = 0
                c += 1
                sname = "pg_%s_%d" % (e, idx)
                self.semnames.add(sname)
                o.token = (sname, c)

    def emit_engine(self, eng, engobj, sems):
        waited = {}
        for o in self.ops[eng]:
            need = {}
            for d in o.deps:
                s, v = d.token
                if waited.get(s, 0) >= v:
                    continue
                if need.get(s, 0) < v:
                    need[s] = v
            for s, v in need.items():
                engobj.wait_ge(sems[s], v)
                waited[s] = v
            ins = o.fn(engobj)
            if o.needs_inc:
                s, v = o.token
                ins.then_inc(sems[s], o.inc if o.is_dma else 1)
        last = {}
        for o in self.ops[eng]:
            if o.is_dma:
                last[o.token[0]] = o.token[1]
        for s, v in last.items():
            if waited.get(s, 0) < v:
                engobj.wait_ge(sems[s], v)


class Buf:
    def __init__(self, t, off, size, nres, name):
        self.t = t
        self.off = off
        self.size = size
        self.res = [Res("%s_%d" % (name, i)) for i in range(nres)]

    def __getitem__(self, k):
        return self.t[k]


def build_program():
    nc = bass.Bass("TRN2", target_bir_lowering=False)
    S = Sched()

    def din(name, shape, dt=F32):
        return nc.dram_tensor(name, list(shape), dt, kind="ExternalInput").ap()

    def dout(name, shape, dt=F32):
        return nc.dram_tensor(name, list(shape), dt, kind="ExternalOutput").ap()

    xp = din("xp", [TOK, D])
    xs = din("xs", [NS, D])
    cpT = din("cpT", [128, 8])
    csT = din("csT", [128, 8, NS])
    ck = [din("ck%d" % g, [NLAYER, NS, WINS[g], 256]) for g in range(3)]
    cv = [din("cv%d" % g, [NLAYER, NS, WINS[g], 256]) for g in range(3)]
    spool = din("spool", [NLAYER, NS * 15, 512])
    norm1_g = din("norm1_g", [NLAYER, D])
    norm2_g = din("norm2_g", [NLAYER, D])
    w_ada = din("w_ada", [NLAYER, D, 6 * D])
    b_ada = din("b_ada", [NLAYER, 6 * D])
    w_in = din("w_in", [NLAYER, D, INW])
    qk_g = din("qk_g", [NLAYER, 128])
    w_pool_grp = din("w_pool_grp", [NLAYER, 4, 128, 128])
    pscT = din("pscT", [NLAYER, 128, 4])
    w_pool_br = din("w_pool_br", [NLAYER, 512, D])
    w_attn_br = din("w_attn_br", [NLAYER, 256, D])
    w_out = din("w_out", [NLAYER, D, D])
    w_up = din("w_up", [NLAYER, D, DFF])
    w_down = din("w_down", [NLAYER, DFF, D])
    c_ident = din("c_ident", [128, 128])
    c_mask = din("c_mask", [128, 3, 512])
    c_rope = din("c_rope", [48, 128, 64])
    c_ropes = din("c_ropes", [NS, 64])
    c_amat = din("c_amat", [128, 4, 4, 128])
    c_sel = din("c_sel", [NS, NS, 128])
    c_selc = din("c_selc", [128, NS, NS])
    c_psel = din("c_psel", [NS * 15, 4, NS])
    c_pcoef = din("c_pcoef", [NS, 512])

    yp = dout("yp", [TOK, D])
    ys = dout("ys", [NS, D])
    kp = [dout("kp%d" % g, [NLAYER, WINS[g], 256]) for g in range(3)]
    vp = [dout("vp%d" % g, [NLAYER, WINS[g], 256]) for g in range(3)]
    poolp = dout("poolp", [NLAYER, 15, 512])
    ks = [dout("ks%d" % g, [NLAYER, NS, 256]) for g in range(3)]
    vs = [dout("vs%d" % g, [NLAYER, NS, 256]) for g in range(3)]
    pools = dout("pools", [NLAYER, NS, 15, 512])

    NSB = 16
    send = [nc.dram_tensor("send_%d" % l, [NSB * 128, 512], BF16).ap() for l in range(NLAYER)]
    recv = [nc.dram_tensor("recv_%d" % l, [2 * NSB * 128, 512], BF16).ap() for l in range(NLAYER)]
    sendb = [nc.dram_tensor("sendb_%d" % l, [NSB * 128, 512], BF16).ap() for l in range(NLAYER)]
    recvb = [nc.dram_tensor("recvb_%d" % l, [2 * NSB * 128, 512], BF16).ap() for l in range(NLAYER)]
    R_send = [Res("send") for l in range(NLAYER)]
    R_recv = [Res("recv") for l in range(NLAYER)]
    R_sendb = [Res("sendb") for l in range(NLAYER)]
    R_recvb = [Res("recvb") for l in range(NLAYER)]

    es = contextlib.ExitStack()
    with es:
        base0 = (nc.sbuf_base + 63) // 64 * 64
        top = nc.sbuf_top
        allbufs = []
        cnt = [0]

        def alloc(name, shape, dt, off, nres=1):
            nb = int(np.prod(shape[1:])) * (2 if dt == BF16 else 4)
            assert off % 32 == 0, (name, off)
            assert off + nb <= top, (name, off, nb, top)
            cnt[0] += 1
            t = nc.alloc_sbuf_tensor_at("%s_%d" % (name, cnt[0]), list(shape), dt, offset=off)
            b = Buf(t, off, nb, nres, name)
            for o in allbufs:
                if o.off < off + nb and off < o.off + o.size:
                    for r in b.res:
                        for ro in o.res:
                            r.inherit(ro)
            allbufs.append(b)
            return b

        cur = [base0]

        def palloc(name, shape, dt, nres=1):
            nb = int(np.prod(shape[1:])) * (2 if dt == BF16 else 4)
            nb = (nb + 31) // 32 * 32
            b = alloc(name, shape, dt, cur[0], nres)
            cur[0] += nb
            return b

        X = palloc("X", [128, NT, D], F32, nres=NT)
        XS = palloc("XS", [NS, D], F32)
        HT = palloc("HT", [128, 8, TT], BF16, nres=NT + 1)
        WR = [palloc("WR%d" % i, [128, 6144], BF16) for i in range(2)]
        IDB = palloc("IDB", [128, 128], BF16)
        MASK = palloc("MASK", [128, 3, 512], BF16)
        ONES = palloc("ONES", [128, 64], BF16)
        SCP = palloc("SCP", [128, 8, 128], BF16)
        SCS = palloc("SCS", [128, 8, NS], BF16)
        SEL = palloc("SEL", [NS, NS, 128], BF16)
        SELC = palloc("SELC", [128, NS, NS], F32)
        PSEL = palloc("PSEL", [NS * 15, 4, NS], F32)
        PCOEF = palloc("PCOEF", [NS, 512], F32)
        EPSB = palloc("EPSB", [128, 1], F32)
        GQK = palloc("GQK", [128, 2, 64], F32)
        PSC = palloc("PSC", [128, 4], F32)
        WG = palloc("WG", [128, 4, 128], BF16)
        ROPES = palloc("ROPES", [NS, 64], F32)
        SS = [palloc("SS%d" % i, [128, 8], F32) for i in range(2)]
        QS = palloc("QS", [NS, 3, 256], BF16)
        SSELF = palloc("SSELF", [NS, 8], F32)
        SACC = palloc("SACC", [NS, 260], F32)
        CCD = palloc("CCD", [128, 8], F32)
        ARENA = (cur[0] + 63) // 64 * 64
        AR_SIZE = top - ARENA
        R1 = ARENA
        R2 = R1 + 32832 + 64
        R3 = R2 + 16416 + 32
        R4 = R3 + 8224
        assert R4 + 14 * 1024 <= top, (R4, top)

        psall = es.enter_context(nc.psum_tensor("psall", [128, 8, 512], F32))
        PB = [Res("bank%d" % i) for i in range(8)]
        pbi = [0]

        def bank():
            i = pbi[0]
            pbi[0] = (i + 1) % 8
            return psall[:, i, :], PB[i]

        def dma(q, out, in_, reads=(), writes=()):
            return S.op(q, lambda e: e.dma_start(out=out, in_=in_), reads, writes, dma=True)

        def mm(out, lhsT, rhs, start, stop, reads, writes):
            return S.op(PE, lambda e: e.matmul(out, lhsT=lhsT, rhs=rhs, start=start, stop=stop), reads, writes)

        def tr(out, in_, ident, reads, writes):
            return S.op(PE, lambda e: e.transpose(out=out, in_=in_, identity=ident), reads, writes)

        def act(out, in_, func, reads, writes, **kw):
            return S.op(ACT, lambda e: e.activation(out=out, in_=in_, func=func, **kw), reads, writes)

        def cp(eng, out, in_, reads, writes):
            if eng == ACT:
                return S.op(ACT, lambda e: e.copy(out=out, in_=in_), reads, writes)
            return S.op(eng, lambda e: e.tensor_copy(out=out, in_=in_), reads, writes)

        def tt(eng, out, in0, in1, op, reads, writes):
            return S.op(eng, lambda e: e.tensor_tensor(out=out, in0=in0, in1=in1, op=op), reads, writes)

        def stt(eng, out, in0, scalar, in1, op0, op1, reads, writes):
            return S.op(eng, lambda e: e.scalar_tensor_tensor(out=out, in0=in0, scalar=scalar, in1=in1,
                                                              op0=op0, op1=op1), reads, writes)

        def red(eng, out, in_, reads, writes):
            return S.op(eng, lambda e: e.tensor_reduce(out=out, in_=in_, axis=AX.X, op=ALU.add), reads, writes)

        def recip(out, in_, reads, writes):
            return S.op(DVE, lambda e: e.reciprocal(out=out, in_=in_), reads, writes)

        def mset(eng, ap, v, writes):
            return S.op(eng, lambda e: e.memset(ap, v), (), writes)

        def kcv(ap2d):
            return ap2d.rearrange("(kc k) n -> k kc n", k=128)

        wri = [0]

        def wslot():
            i = wri[0]
            wri[0] = (i + 1) % len(WR)
            return WR[i]

        stg_off = R1
        STG = alloc("STG", [128, 2048], F32, stg_off)
        dma(SP, STG[:, 0:128], c_ident, (), STG.res)
        cp(DVE, IDB[:, :], STG[:, 0:128], STG.res, IDB.res)
        dma(SP, STG[:, 0:1536], c_mask.rearrange("p a b -> p (a b)"), (), STG.res)
        cp(DVE, MASK[:, :, :].rearrange("p a b -> p (a b)"), STG[:, 0:1536], STG.res, MASK.res)
        mset(DVE, ONES[:, :], 1.0, ONES.res)
        mset(DVE, EPSB[:, :], EPS, EPSB.res)
        dma(SP, STG[:, 0:8], cpT, (), STG.res)
        act(STG[:, 8:16], STG[:, 0:8], AF.Silu, STG.res, STG.res)
        cp(DVE, SCP[:, :, :], STG[:, 8:16].unsqueeze(2).to_broadcast([128, 8, 128]), STG.res, SCP.res)
        dma(SP, STG[:, 0:32], csT.rearrange("p a b -> p (a b)"), (), STG.res)
        act(SCS[:, :, :].rearrange("p a b -> p (a b)"), STG[:, 0:32], AF.Silu, STG.res, SCS.res)
        dma(SP, STG[0:NS, 0:512], c_sel.rearrange("p a b -> p (a b)"), (), STG.res)
        cp(DVE, SEL[:, :, :].rearrange("p a b -> p (a b)"), STG[0:NS, 0:512], STG.res, SEL.res)
        dma(SP, SELC[:, :, :], c_selc, (), SELC.res)
        dma(SP, PSEL[:, :, :], c_psel, (), PSEL.res)
        dma(SP, PCOEF[:, :], c_pcoef, (), PCOEF.res)
        dma(SP, ROPES[:, :], c_ropes, (), ROPES.res)
        for t in range(NT):
            dma(SP, X[:, t, :], xp[t * 128:(t + 1) * 128, :], (), [X.res[t]])
        dma(SP, XS[:, :], xs, (), XS.res)

        def xtile(t):
            if t < NT:
                return X[:, t, :], X.res[t], 128
            return XS[:, :], XS.res[0], NS

        def tcols(t):
            if t < NT:
                return slice(t * 128, (t + 1) * 128)
            return slice(TOK, TT)

        def ada(l, j, MP, MS, BB, NG, scale_norm):
            dma(SP, BB[:, :], b_ada[l:l + 1, j * D:(j + 1) * D].partition_broadcast(128), (), BB.res)
            if scale_norm is not None:
                dma(SP, NG[:, :], scale_norm[l:l + 1, :].partition_broadcast(128), (), NG.res)
            for c in range(2):
                W = wslot()
                wv = W[:, 0:4096].rearrange("p (a b) -> p a b", a=8)
                dma(POOL, wv, kcv(w_ada[l])[:, :, j * D + c * 512: j * D + (c + 1) * 512], (), W.res)
                for (sc, M, np_) in ((SCP, MP, 128), (SCS, MS, NS)):
                    pb, pr = bank()
                    for kc in range(8):
                        mm(pb[0:np_, :], sc[:, kc, :], wv[:, kc, :], kc == 0, kc == 7, W.res + sc.res, [pr])
                    cs = slice(c * 512, (c + 1) * 512)
                    tt(DVE, M[0:np_, cs], pb[0:np_, :], BB[0:np_, cs], ALU.add, [pr] + BB.res, M.res)
                    if scale_norm is not None:
                        stt(DVE, M[0:np_, cs], M[0:np_, cs], 1.0, NG[0:np_, cs], ALU.add, ALU.mult,
                            M.res + NG.res, M.res)

        def norm_phase(GP, SHP, GS, SHS, toff):
            TMP = [alloc("NTMP%d" % i, [128, D], F32, toff + i * 6144) for i in range(2)]
            HB = [alloc("NHB%d" % i, [128, D], BF16, toff + i * 6144 + 4096) for i in range(2)]
            def tile_gen(t):
                xt, xr, np_ = xtile(t)
                G, SH = (GP, SHP) if t < NT else (GS, SHS)
                tmp = TMP[t % 2]
                hb = HB[t % 2]
                ss = SS[t % 2]
                mset(DVE, ss[:, 0:1], 0.0, ss.res)
                act(tmp[0:np_, :], xt, AF.Square, [xr], tmp.res + ss.res, accum_out=ss[0:np_, 0:1])
                act(ss[0:np_, 0:1], ss[0:np_, 0:1], AF.Ln, ss.res + EPSB.res, ss.res, scale=1.0 / D,
                    bias=EPSB[0:np_, 0:1])
                act(ss[0:np_, 0:1], ss[0:np_, 0:1], AF.Exp, ss.res, ss.res, scale=-0.5)
                yield
                stt(DVE, tmp[0:np_, :], xt, ss[0:np_, 0:1], G[0:np_, :], ALU.mult, ALU.mult,
                    [xr] + ss.res + G.res, tmp.res)
                tt(DVE, hb[0:np_, :], tmp[0:np_, :], SH[0:np_, :], ALU.add, tmp.res + SH.res, hb.res)
                yield
                pb, pr = bank()
                pbb = pb.bitcast(BF16)
                for kc in range(8):
                    tr(pbb[:, kc * 128: kc * 128 + np_], hb[0:np_, kc * 128:(kc + 1) * 128], IDB[0:np_, 0:np_],
                       hb.res + IDB.res, [pr])
                cp(ACT, HT[:, :, tcols(t)], pbb[:, 0:1024].rearrange("p (a b) -> p a b", a=8)[:, :, 0:np_],
                   [pr], [HT.res[t]])

            gens = [tile_gen(t) for t in range(NT + 1)]
            n = len(gens)
            for it in range(n + 2):
                for k in (2, 1, 0):
                    idx = it - k
                    if 0 <= idx < n:
                        next(gens[idx], None)

        for l in range(NLAYER):
          try:
            MPA = alloc("MPA", [128, D], F32, R2)
            MPB = alloc("MPB", [128, D], F32, R2 + 4096)
            MSA = alloc("MSA", [NS, D], F32, R2 + 8192)
            MSB = alloc("MSB", [NS, D], F32, R2 + 12288)
            BB = alloc("BB", [128, D], F32, R3)
            NG = alloc("NG", [128, D], F32, R3 + 4096)
            ada(l, 0, MPB, MSB, BB, NG, None)
            ada(l, 1, MPA, MSA, BB, NG, norm1_g)
            stage(1)
            norm_phase(MPA, MPB, MSA, MSB, R4)

            stage(2)
            ACC = alloc("ACC", [128, 4, TOK], F32, R1, nres=NT)
            KV = [alloc("KV%d" % i, [128, 512], BF16, R2 + i * 1024) for i in range(8)]
            H0 = alloc("H0", [128, 512], BF16, R2 + 8192)
            H1 = [alloc("H1_%d" % i, [128, 512], BF16, R2 + 9216 + i * 1024) for i in range(4)]
            H2 = [alloc("H2_%d" % i, [128, 512], BF16, R2 + 13312 + i * 1024) for i in range(3)]
            o = R3
            QK = [alloc("QK%d" % i, [128, 512], F32, o + i * 2048) for i in range(2)]
            o += 4096
            QO = [alloc("QO%d" % i, [128, 512], F32, o + i * 2048) for i in range(2)]
            o += 4096
            SQ = alloc("SQ", [128, 512], F32, o)
            o += 2048
            TA = alloc("TA", [128, 256], F32, o)
            o += 1024
            TB = alloc("TB", [128, 256], F32, o)
            o += 1024
            QB = [alloc("QB%d" % i_, [128, 512], BF16, o + i_ * 1024) for i_ in range(2)]
            o += 2048
            VF = [alloc("VF%d" % i, [128, 256], F32, o + i * 1024) for i in range(2)]
            o += 2048
            QT = [alloc("QT%d" % i, [128, 2, 128], BF16, o + i * 512) for i in range(2)]
            o += 1024
            PTS = [alloc("PTS%d" % i, [128, 1024], BF16, o + i * 2048) for i in range(2)]
            o += 4096
            ROPE = [alloc("ROPE%d" % i, [128, 64], F32, o + i * 256) for i in range(4)]
            o += 1024
            assert o <= top, (o, top)
            dma(SP, GQK[:, :, :].rearrange("p a b -> p (a b)"), qk_g[l:l + 1, :].partition_broadcast(128), (), GQK.res)
            mset(DVE, SACC[:, :], 0.0, SACC.res)

            blkc = [0]

            def blk_cols(g, b):
                if g == 0:
                    return slice(128 * b, 128 * b + 128), [b]
                if g == 1:
                    n1, r1 = b // 4, b % 4
                    return slice(512 * n1 + r1, 512 * n1 + 512, 4), list(range(4 * n1, 4 * n1 + 4))
                return slice(b, TOK, 16), list(range(NT))

            def load_wq(g):
                W = wslot()
                wv = W[:, :].rearrange("p (a b) -> p a b", a=8)
                for i, c0 in enumerate((512, 1280, 2048)):
                    dma(POOL, wv[:, :, i * 256:(i + 1) * 256], kcv(w_in[l])[:, :, c0 + 256 * g: c0 + 256 * g + 256],
                        (), W.res)
                return W, wv

            def project(g, b, W, wv, kvslot, out_rows=None, sample=False):
                i = blkc[0]
                blkc[0] += 1
                qk, qo, vf, qt, rp, ss = QK[i % 2], QO[i % 2], VF[i % 2], QT[i % 2], ROPE[i % 4], SS[i % 2]
                if sample:
                    np_ = NS
                    cols, tl = slice(TOK, TT), [NT]
                    ropeap, roper = ROPES, ROPES.res
                else:
                    np_ = 128
                    cols, tl = blk_cols(g, b)
                    dma(SP, rp[:, :], c_rope[g * 16 + b], (), rp.res)
                    ropeap, roper = rp, rp.res
                hres = [HT.res[t] for t in tl]
                p1, r1_ = bank()
                p2, r2_ = bank()
                for kc in range(8):
                    mm(p1[0:np_, :], HT[:, kc, cols], wv[:, kc, 0:512], kc == 0, kc == 7, hres + W.res, [r1_])
                for kc in range(8):
                    mm(p2[0:np_, 0:256], HT[:, kc, cols], wv[:, kc, 512:768], kc == 0, kc == 7, hres + W.res, [r2_])
                P = slice(0, np_)
                cp(DVE, qk[P, :], p1[P, :], [r1_], qk.res)
                KP = int(os.environ.get("KPROJ", "9"))
                if KP <= 1:
                    return None
                tt(DVE, SQ[P, :], qk[P, :], qk[P, :], ALU.mult, qk.res, SQ.res)
                red(DVE, ss[P, 0:8], SQ[P, :].rearrange("p (h d) -> p h d", h=8), SQ.res, ss.res)
                act(ss[P, 0:8], ss[P, 0:8], AF.Ln, ss.res + EPSB.res, ss.res, scale=1.0 / 64, bias=EPSB[P, 0:1])
                act(ss[P, 0:8], ss[P, 0:8], AF.Exp, ss.res, ss.res, scale=-0.5)
                tt(DVE, qk[P, :].rearrange("p (a h d) -> p a h d", a=2, h=4),
                   qk[P, :].rearrange("p (a h d) -> p a h d", a=2, h=4),
                   GQK[P, :, :].unsqueeze(2).to_broadcast([np_, 2, 4, 64]), ALU.mult, qk.res + GQK.res, qk.res)
                qv = qk[P, :].rearrange("p (h d) -> p h d", h=8)
                ov = qo[P, :].rearrange("p (h d) -> p h d", h=8)
                cosb = ropeap[P, 0:32].unsqueeze(1).to_broadcast([np_, 8, 32])
                sinb = ropeap[P, 32:64].unsqueeze(1).to_broadcast([np_, 8, 32])
                ta = TA[P, :].rearrange("p (h d) -> p h d", h=8)
                tb = TB[P, :].rearrange("p (h d) -> p h d", h=8)
                tt(DVE, ta, qv[:, :, 0:32], cosb, ALU.mult, qk.res + roper, TA.res)
                tt(DVE, tb, qv[:, :, 32:64], sinb, ALU.mult, qk.res + roper, TB.res)
                tt(DVE, ov[:, :, 0:32], ta, tb, ALU.subtract, TA.res + TB.res, qo.res)
                tt(DVE, ta, qv[:, :, 32:64], cosb, ALU.mult, qk.res + roper, TA.res)
                tt(DVE, tb, qv[:, :, 0:32], sinb, ALU.mult, qk.res + roper, TB.res)
                tt(DVE, ov[:, :, 32:64], ta, tb, ALU.add, TA.res + TB.res, qo.res)
                tt(DVE, ov, ov, ss[P, 0:8].unsqueeze(2).to_broadcast([np_, 8, 64]), ALU.mult, qo.res + ss.res, qo.res)
                cp(ACT, vf[P, :], p2[P, 0:256], [r2_], vf.res)
                if KP <= 2:
                    return None
                if sample:
                    cp(DVE, QS[:, g, :], qo[P, 0:256], qo.res, QS.res)
                    tt(DVE, TA[P, :], qo[P, 0:256], qo[P, 256:512], ALU.mult, qo.res, TA.res)
                    red(DVE, SSELF[:, 0:4], TA[P, :].rearrange("p (h d) -> p h d", h=4), TA.res, SSELF.res)
                    act(SSELF[:, 4:8], SSELF[:, 0:4], AF.Exp, SSELF.res, SSELF.res, scale=0.125)
                    tt(DVE, TA[P, :].rearrange("p (h d) -> p h d", h=4), vf[P, :].rearrange("p (h d) -> p h d", h=4),
                       SSELF[:, 4:8].unsqueeze(2).to_broadcast([NS, 4, 64]), ALU.mult, vf.res + SSELF.res, TA.res)
                    tt(DVE, SACC[:, 0:256], SACC[:, 0:256], TA[P, :], ALU.add, SACC.res + TA.res, SACC.res)
                    tt(DVE, SACC[:, 256:260], SACC[:, 256:260], SSELF[:, 4:8], ALU.add, SACC.res + SSELF.res, SACC.res)
                    dma(SP, ks[g][l], qo[P, 256:512], qo.res, ())
                    dma(SP, vs[g][l], vf[P, :], vf.res, ())
                    return None
                cp(DVE, QB[0][:, :], qo[:, :], qo.res, QB[0].res)
                cp(ACT, kvslot[:, 256:512], p2[:, 0:256], [r2_], kvslot.res)
                pt, rt = bank()
                ptb = pt.bitcast(BF16)
                for j in range(4):
                    tr(ptb[:, j * 128:(j + 1) * 128], QB[0][:, j * 128:(j + 1) * 128], IDB[:, :], QB[0].res + IDB.res, [rt])
                cp(ACT, qt[:, :, :].rearrange("p a b -> p (a b)"), ptb[:, 0:256], [rt], qt.res)
                cp(ACT, kvslot[:, 0:256], ptb[:, 256:512], [rt], kvslot.res)
                if KP <= 3:
                    return qt
                if out_rows is not None:
                    kd, vd = out_rows
                    dma(SP, kd, qo[:, 256:512], qo.res, ())
                    dma(SP, vd, vf[:, :], vf.res, ())
                return qt

            def attend(g, b, qt, kvc, kvp, mprev, first):
                i = blkc[0]
                pts = PTS[i % 2]
                cols, tl = blk_cols(g, b)
                for half in range(2):
                    rows = slice(64 * half, 64 * half + 64)
                    pb, pr = bank()
                    mm(pb[:, :], IDB[:, :], MASK[:, mprev, :], True, False, IDB.res + MASK.res, [pr])
                    for j, (kvb, pair) in enumerate(((kvp, 0), (kvp, 1), (kvc, 0), (kvc, 1))):
                        mm(pb[:, 128 * j:128 * j + 128], kvb[rows, pair * 128:(pair + 1) * 128], qt[rows, pair, :],
                           False, j == 3, kvb.res + qt.res, [pr])
                    act(pts[:, half * 512:(half + 1) * 512], pb[:, :], AF.Exp, [pr], pts.res, scale=0.125)
                po, pro = bank()
                for h in range(4):
                    pair, half = h // 2, h % 2
                    rows = slice(64 * half, 64 * half + 64)
                    vsl = slice(256 + 64 * h, 256 + 64 * h + 64)
                    pp = pts[:, half * 512 + pair * 128:half * 512 + pair * 128 + 128]
                    pc = pts[:, half * 512 + 256 + pair * 128:half * 512 + 256 + pair * 128 + 128]
                    mm(po[rows, pair * 128:(pair + 1) * 128], kvp[:, vsl], pp, True, False, kvp.res + pts.res, [pro])
                    mm(po[rows, pair * 128:(pair + 1) * 128], kvc[:, vsl], pc, False, True, kvc.res + pts.res, [pro])
                    mm(po[rows, 256 + pair * 128:256 + (pair + 1) * 128], ONES[:, :], pp, True, False,
                       ONES.res + pts.res, [pro])
                    mm(po[rows, 256 + pair * 128:256 + (pair + 1) * 128], ONES[:, :], pc, False, True,
                       ONES.res + pts.res, [pro])
                av = ACC[:, :, cols]
                ares = [ACC.res[t] for t in tl]
                pov = po[:, :].rearrange("p (a b) -> p a b", a=4)
                if first:
                    cp(DVE, av, pov, [pro], ares)
                else:
                    tt(DVE, av, av, pov, ALU.add, [pro] + ares, ares)

            def block_task(g, b, W, wv, kvslot, orows=None, send_dst=None, prev=None, mprev=1, first=False,
                           hist=None):
                i = blkc[0]
                blkc[0] += 1
                qk, qo, vf, qt, rp, ss = QK[i % 2], QO[i % 2], VF[i % 2], QT[i % 2], ROPE[i % 4], SS[i % 2]
                pts = PTS[i % 2]
                cols, tl = blk_cols(g, b)
                if hist is not None:
                    hbuf, hsrc, hres_ = hist
                    dma(SP, hbuf[:, :], hsrc, [hres_], hbuf.res)
                dma(SP, rp[:, :], c_rope[g * 16 + b], (), rp.res)
                hres = [HT.res[t] for t in tl]
                p1, r1_ = bank()
                p2, r2_ = bank()
                for kc in range(8):
                    mm(p1[:, :], HT[:, kc, cols], wv[:, kc, 0:512], kc == 0, kc == 7, hres + W.res, [r1_])
                for kc in range(8):
                    mm(p2[:, 0:256], HT[:, kc, cols], wv[:, kc, 512:768], kc == 0, kc == 7, hres + W.res, [r2_])
                cp(ACT, qk[:, :], p1[:, :], [r1_], qk.res)
                act(SQ[:, :], p1[:, :], AF.Square, [r1_], SQ.res)
                red(DVE, ss[:, 0:8], SQ[:, :].rearrange("p (h d) -> p h d", h=8), SQ.res, ss.res)
                act(ss[:, 0:8], ss[:, 0:8], AF.Ln, ss.res + EPSB.res, ss.res, scale=1.0 / 64, bias=EPSB[:, 0:1])
                act(ss[:, 0:8], ss[:, 0:8], AF.Exp, ss.res, ss.res, scale=-0.5)
                tt(DVE, qk[:, :].rearrange("p (a h d) -> p a h d", a=2, h=4),
                   qk[:, :].rearrange("p (a h d) -> p a h d", a=2, h=4),
                   GQK[:, :, :].unsqueeze(2).to_broadcast([128, 2, 4, 64]), ALU.mult, qk.res + GQK.res, qk.res)
                qv = qk[:, :].rearrange("p (h d) -> p h d", h=8)
                ov = qo[:, :].rearrange("p (h d) -> p h d", h=8)
                cosb = rp[:, 0:32].unsqueeze(1).to_broadcast([128, 8, 32])
                sinb = rp[:, 32:64].unsqueeze(1).to_broadcast([128, 8, 32])
                ta = TA[:, :].rearrange("p (h d) -> p h d", h=8)
                tb = TB[:, :].rearrange("p (h d) -> p h d", h=8)
                tc_ = SQ[:, 0:256].rearrange("p (h d) -> p h d", h=8)
                td_ = SQ[:, 256:512].rearrange("p (h d) -> p h d", h=8)
                tt(DVE, ta, qv[:, :, 0:32], cosb, ALU.mult, qk.res + rp.res, TA.res)
                tt(DVE, tb, qv[:, :, 32:64], sinb, ALU.mult, qk.res + rp.res, TB.res)
                tt(DVE, tc_, qv[:, :, 32:64], cosb, ALU.mult, qk.res + rp.res, SQ.res)
                tt(DVE, td_, qv[:, :, 0:32], sinb, ALU.mult, qk.res + rp.res, SQ.res)
                tt(DVE, ov[:, :, 32:64], tc_, td_, ALU.add, SQ.res, qo.res)
                tt(DVE, ov[:, :, 0:32], ta, tb, ALU.subtract, TA.res + TB.res, qo.res)
                tt(DVE, ov, ov, ss[:, 0:8].unsqueeze(2).to_broadcast([128, 8, 64]), ALU.mult, qo.res + ss.res, qo.res)
                cp(ACT, vf[:, :], p2[:, 0:256], [r2_], vf.res)
                cp(ACT, kvslot[:, 256:512], p2[:, 0:256], [r2_], kvslot.res)
                yield
                cp(ACT, QB[i % 2][:, :], qo[:, :], qo.res, QB[i % 2].res)
                pt, rt = bank()
                ptb = pt.bitcast(BF16)
                qb = QB[i % 2]
                for j in range(4):
                    tr(ptb[:, j * 128:(j + 1) * 128], qb[:, j * 128:(j + 1) * 128], IDB[:, :], qb.res + IDB.res, [rt])
                cp(ACT, qt[:, :, :].rearrange("p a b -> p (a b)"), ptb[:, 0:256], [rt], qt.res)
                cp(ACT, kvslot[:, 0:256], ptb[:, 256:512], [rt], kvslot.res)
                if orows is not None:
                    kd, vd = orows
                    dma(SP, kd, qo[:, 256:512], qo.res, ())
                    dma(SP, vd, vf[:, :], vf.res, ())
                if send_dst is not None:
                    sd, rs = send_dst
                    dma(SP, sd, kvslot[:, :], kvslot.res, [rs])
                yield
                if prev is None:
                    return
                kvc, kvp = kvslot, prev
                for half in range(2):
                    rows = slice(64 * half, 64 * half + 64)
                    pb, pr = bank()
                    mm(pb[:, :], IDB[:, :], MASK[:, mprev, :], True, False, IDB.res + MASK.res, [pr])
                    for j, (kvb, pair) in enumerate(((kvp, 0), (kvp, 1), (kvc, 0), (kvc, 1))):
                        mm(pb[:, 128 * j:128 * j + 128], kvb[rows, pair * 128:(pair + 1) * 128], qt[rows, pair, :],
                           False, j == 3, kvb.res + qt.res, [pr])
                    act(pts[:, half * 512:(half + 1) * 512], pb[:, :], AF.Exp, [pr], pts.res, scale=0.125)
                yield
                po, pro = bank()
                for h in range(4):
                    pair, half = h // 2, h % 2
                    rows = slice(64 * half, 64 * half + 64)
                    vsl = slice(256 + 64 * h, 256 + 64 * h + 64)
                    pp = pts[:, half * 512 + pair * 128:half * 512 + pair * 128 + 128]
                    pc = pts[:, half * 512 + 256 + pair * 128:half * 512 + 256 + pair * 128 + 128]
                    mm(po[rows, pair * 128:(pair + 1) * 128], kvp[:, vsl], pp, True, False, kvp.res + pts.res, [pro])
                    mm(po[rows, pair * 128:(pair + 1) * 128], kvc[:, vsl], pc, False, True, kvc.res + pts.res, [pro])
                    mm(po[rows, 256 + pair * 128:256 + (pair + 1) * 128], ONES[:, :], pp, True, False,
                       ONES.res + pts.res, [pro])
                    mm(po[rows, 256 + pair * 128:256 + (pair + 1) * 128], ONES[:, :], pc, False, True,
                       ONES.res + pts.res, [pro])
                av = ACC[:, :, cols]
                ares = [ACC.res[t] for t in tl]
                pov = po[:, :].rearrange("p (a b) -> p a b", a=4)
                if first:
                    cp(DVE, av, pov, [pro], ares)
                else:
                    tt(DVE, av, av, pov, ALU.add, [pro] + ares, ares)

            def run_pipeline(gens):
                n = len(gens)
                for it in range(n + 3):
                    for k in (3, 2, 0, 1):
                        idx = it - k
                        if 0 <= idx < n:
                            next(gens[idx], None)

            def send_blocks(g, blks, W, wv, sb0):
                sd, rs = (send[l], R_send[l]) if g == 2 else (sendb[l], R_sendb[l])
                for si, b in enumerate(blks):
                    kvs = KV[si % 8]
                    project(g, b, W, wv, kvs)
                    dma(SP, sd[(sb0 + si) * 128:(sb0 + si + 1) * 128, :], kvs[:, :], kvs.res, [rs])

            def collective(sd, rv, rs, rr):
                if os.environ.get("KNOCC"):
                    return
                S.op(POOL, lambda e: e.collective_compute("AllGather", ALU.bypass, replica_groups=RG,
                                                          ins=[sd], outs=[rv]),
                     [rs], [rr], dma=True, inc=1, semname="cc_sem")
                S.op(POOL, lambda e: e.memset(CCD[:, :], 0.0), [rr], CCD.res)

            def out_rows(g, b):
                if g == 0:
                    return (kp[0][l], vp[0][l]) if b == 15 else None
                if g == 1:
                    if b < 12:
                        return None
                    r1 = b - 12
                    return (kp[1][l].rearrange("(i f) c -> f i c", f=4)[r1],
                            vp[1][l].rearrange("(i f) c -> f i c", f=4)[r1])
                return (kp[2][l].rearrange("(i f) c -> f i c", f=16)[b],
                        vp[2][l].rearrange("(i f) c -> f i c", f=16)[b])

            W2, wv2 = load_wq(2)
            run_pipeline([block_task(2, b, W2, wv2, KV[b % 8],
                                     send_dst=(send[l][b * 128:(b + 1) * 128, :], R_send[l])) for b in range(16)])
            W0, wv0 = load_wq(0)
            W1, wv1 = load_wq(1)
            collective(send[l], recv[l], R_send[l], R_recv[l])
            stage(3)
            run_pipeline([block_task(0, 15, W0, wv0, KV[0], send_dst=(sendb[l][0:128, :], R_sendb[l]))] +
                         [block_task(1, 12 + i_, W1, wv1, KV[1 + i_],
                                     send_dst=(sendb[l][(1 + i_) * 128:(2 + i_) * 128, :], R_sendb[l]))
                          for i_ in range(4)])
            WU = wslot()
            wu = WU[:, 0:4096].rearrange("p (a b) -> p a b", a=8)
            dma(POOL, wu, kcv(w_in[l])[:, :, 0:512], (), WU.res)
            pb, pr = bank()
            for kc in range(8):
                mm(pb[:, :], HT[:, kc, tcols(15)], wu[:, kc, :], kc == 0, kc == 7, [HT.res[15]] + WU.res, [pr])
            cp(ACT, H2[0][:, :], pb[:, :], [pr], H2[0].res)
            cp(ACT, SQ[:, :], pb[:, :], [pr], SQ.res)
            dma(SP, sendb[l][5 * 128:6 * 128, :], H2[0][:, :], H2[0].res, [R_sendb[l]])
            dma(SP, poolp[l], SQ[113:128, :], SQ.res, ())
            W0, wv0 = load_wq(0)
            collective(sendb[l], recvb[l], R_sendb[l], R_recvb[l])
            stage(3.5)
            tasks = [block_task(0, 0, W0, wv0, KV[0])]
            for b in range(1, 16):
                tasks.append(block_task(0, b, W0, wv0, KV[b % 8], orows=out_rows(0, b), prev=KV[(b - 1) % 8],
                                        mprev=1, first=True))
            tasks.append(block_task(0, 0, W0, wv0, KV[0], prev=H0, mprev=2, first=True,
                                    hist=(H0, recvb[l][0:128, :], R_recvb[l])))
            run_pipeline(tasks)
            project(0, 0, W0, wv0, None, sample=True)
            W1, wv1 = load_wq(1)
            tasks = [block_task(1, b, W1, wv1, KV[b % 8]) for b in range(4)]
            for b in range(4, 16):
                tasks.append(block_task(1, b, W1, wv1, KV[b % 8], orows=out_rows(1, b), prev=KV[(b - 4) % 8],
                                        mprev=1, first=False))
            for b in range(4):
                tasks.append(block_task(1, b, W1, wv1, KV[b % 8], prev=H1[b], mprev=2, first=False,
                                        hist=(H1[b], recvb[l][(1 + b) * 128:(2 + b) * 128, :], R_recvb[l])))
            run_pipeline(tasks)
            project(1, 0, W1, wv1, None, sample=True)
            W2, wv2 = load_wq(2)
            run_pipeline([block_task(2, b, W2, wv2, KV[b % 8], orows=out_rows(2, b), prev=H2[b % 3], mprev=2,
                                     first=False, hist=(H2[b % 3], recv[l][b * 128:(b + 1) * 128, :], R_recv[l]))
                          for b in range(16)])
            project(2, 0, W2, wv2, None, sample=True)

            stage(4)
            AYT = alloc("AYT", [128, 2, TT], BF16, R3, nres=5)
            RD = [alloc("RD%d" % i, [128, 2, 512], F32, R4 + i * 4096) for i in range(2)]
            for tg in range(4):
                cs = slice(tg * 512, (tg + 1) * 512)
                ares = [ACC.res[t] for t in range(4 * tg, 4 * tg + 4)]
                rd = RD[tg % 2]
                recip(rd[:, :, :], ACC[:, 2:4, cs], ares, rd.res)
                tt(DVE, AYT[:, :, cs], ACC[:, 0:2, cs], rd[:, :, :], ALU.mult, ares + rd.res, [AYT.res[tg]])

            stage(5)
            KC = alloc("KC", [128, NS, 256], F32, R1)
            VC = alloc("VC", [128, NS, 256], F32, R1 + 4096)
            PROD = alloc("PROD", [128, NS * 256], F32, R1 + 8192)
            PVP = alloc("PVP", [128, NS, 260], F32, R1 + 12288)
            SCO = alloc("SCO", [128, 16], F32, R1 + 12288 + 4160)
            SPX = alloc("SPX", [NS, 16], F32, R1 + 12288 + 4160 + 64)
            AYS = alloc("AYS", [NS, 256], BF16, R1 + 12288 + 4160 + 128)
            for g in range(3):
                dil = DILS[g]
                dma(SP, KC[:, :, :], ck[g][l][:, 0:WINS[g]:dil, :].rearrange("b j c -> j b c"), (), KC.res)
                dma(SP, VC[:, :, :], cv[g][l][:, 0:WINS[g]:dil, :].rearrange("b j c -> j b c"), (), VC.res)
                pq = [bank(), bank()]
                for bb in range(NS):
                    pb, pr = pq[bb // 2]
                    mm(pb[:, (bb % 2) * 256:(bb % 2) * 256 + 256], SEL[:, bb, :], QS[:, g, :], True, True,
                       SEL.res + QS.res, [pr])
                for hf in range(2):
                    pb, pr = pq[hf]
                    tt(DVE, PROD[:, hf * 512:(hf + 1) * 512],
                       KC[:, 2 * hf:2 * hf + 2, :].rearrange("p a b -> p (a b)"), pb[:, :], ALU.mult,
                       KC.res + [pr], PROD.res)
                red(DVE, SCO[:, :], PROD[:, :].rearrange("p (a d) -> p a d", d=64), PROD.res, SCO.res)
                act(PVP[:, :, 256:260], SCO[:, :].rearrange("p (a b) -> p a b", a=NS), AF.Exp, SCO.res, PVP.res,
                    scale=0.125)
                tt(DVE, PVP[:, :, 0:256].rearrange("p a (h d) -> p a h d", h=4),
                   VC[:, :, :].rearrange("p a (h d) -> p a h d", h=4),
                   PVP[:, :, 256:260].unsqueeze(3).to_broadcast([128, NS, 4, 64]), ALU.mult,
                   VC.res + PVP.res, PVP.res)
                pb, pr = bank()
                for bb in range(NS):
                    mm(pb[0:NS, 0:260], SELC[:, bb, :], PVP[:, bb, :], bb == 0, bb == NS - 1, SELC.res + PVP.res, [pr])
                tt(DVE, SACC[:, :], SACC[:, :], pb[0:NS, 0:260], ALU.add, SACC.res + [pr], SACC.res)
            recip(SPX[:, 4:8], SACC[:, 256:260], SACC.res, SPX.res)
            tt(DVE, AYS[:, :].rearrange("p (h d) -> p h d", h=4), SACC[:, 0:256].rearrange("p (h d) -> p h d", h=4),
               SPX[:, 4:8].unsqueeze(2).to_broadcast([NS, 4, 64]), ALU.mult, SACC.res + SPX.res, AYS.res)
            pt, rt = bank()
            ptb = pt.bitcast(BF16)
            for pr_ in range(2):
                tr(ptb[:, pr_ * 128:pr_ * 128 + NS], AYS[:, pr_ * 128:(pr_ + 1) * 128], IDB[0:NS, 0:NS],
                   AYS.res + IDB.res, [rt])
            cp(ACT, AYT[:, :, TOK:TT], ptb[:, 0:256].rearrange("p (a b) -> p a b", a=2)[:, :, 0:NS], [rt],
               [AYT.res[4]])

            stage(6)
            PYT = alloc("PYT", [128, 4, TT], BF16, R2, nres=NT + 1)
            o = R1 + 17408
            UB = [alloc("UB%d" % i, [128, 512], BF16, o + i * 1024) for i in range(3)]
            o += 3072
            UH = alloc("UH", [128, 512], BF16, o)
            o += 1024
            UF = alloc("UF", [128, 512], F32, o)
            o += 2048
            PTB = [alloc("PTB%d" % i, [128, 512], BF16, o + i * 1024) for i in range(2)]
            o += 2048
            AM = alloc("AM", [128, 16, 128], BF16, o)
            o += 4096
            assert o <= R2
            o = R4
            ST = alloc("ST", [NS * 15, 512], F32, o)
            o += 2048
            USF = alloc("USF", [NS, 512], F32, o)
            o += 2048
            PSB = alloc("PSB", [NS, 512], BF16, o)
            o += 1024
            PTS_ = alloc("PTSs", [128, 4, NS], BF16, o)
            o += 64
            assert o <= top
            STG2 = alloc("STG2", [128, 2048], F32, R1)
            dma(SP, STG2[:, :], c_amat.rearrange("p a b c -> p (a b c)"), (), STG2.res)
            cp(DVE, AM[:, :, :].rearrange("p a b -> p (a b)"), STG2[:, :], STG2.res, AM.res)
            dma(POOL, WG[:, :, :], w_pool_grp[l].rearrange("g c e -> c g e"), (), WG.res)
            dma(SP, PSC[:, :], pscT[l], (), PSC.res)
            WU = wslot()
            wu = WU[:, 0:4096].rearrange("p (a b) -> p a b", a=8)
            dma(POOL, wu, kcv(w_in[l])[:, :, 0:512], (), WU.res)

            def uproj(t, dst, fp32dst=None):
                np_ = 128 if t < NT else NS
                pb, pr = bank()
                for kc in range(8):
                    mm(pb[0:np_, :], HT[:, kc, tcols(t)], wu[:, kc, :], kc == 0, kc == 7, [HT.res[t]] + WU.res, [pr])
                if dst is not None:
                    cp(ACT, dst[0:np_, :], pb[0:np_, :], [pr], dst.res)
                if fp32dst is not None:
                    cp(ACT, fp32dst[0:np_, :], pb[0:np_, :], [pr], fp32dst.res)

            def pool_tile(t, ucur, uprev, acur, aprev):
                pb, pr = bank()
                for gi in range(4):
                    gs = slice(gi * 128, (gi + 1) * 128)
                    mm(pb[:, gs], ucur[:, gs], AM[:, acur * 4 + gi, :], True, False, ucur.res + AM.res, [pr])
                    mm(pb[:, gs], uprev[:, gs], AM[:, aprev * 4 + gi, :], False, True, uprev.res + AM.res, [pr])
                ptb_ = PTB[t % 2]
                cp(ACT, ptb_[:, :], pb[:, :], [pr], ptb_.res)
                pb2, pr2 = bank()
                for gi in range(4):
                    gs = slice(gi * 128, (gi + 1) * 128)
                    mm(pb2[:, gs], WG[:, gi, :], ptb_[:, gs], True, True, WG.res + ptb_.res, [pr2])
                tt(DVE, PYT[:, :, tcols(t)], pb2[:, :].rearrange("p (a b) -> p a b", a=4),
                   PSC[:, :].unsqueeze(2).to_broadcast([128, 4, 128]), ALU.mult, [pr2] + PSC.res, [PYT.res[t]])

            uproj(0, UB[0])
            uproj(1, UB[1])
            for t in range(1, NT):
                if t + 1 < NT:
                    uproj(t + 1, UB[(t + 1) % 3])
                pool_tile(t, UB[t % 3], UB[(t - 1) % 3], 0, 1)
            dma(SP, UH[:, :], recvb[l][5 * 128:6 * 128, :], [R_recvb[l]], UH.res)
            uproj(0, UB[0])
            pool_tile(0, UB[0], UH, 2, 3)
            uproj(NT, None, USF)
            dma(SP, ST[:, :], spool[l], (), ST.res)
            dma(SP, pools[l][:, 0:14, :], spool[l].rearrange("(b r) c -> b r c", r=15)[:, 1:15, :], (), ())
            dma(SP, pools[l][:, 14, :], USF[:, :], USF.res, ())
            pb, pr = bank()
            for gi in range(4):
                gs = slice(gi * 128, (gi + 1) * 128)
                mm(pb[0:NS, gs], PSEL[:, gi, :], ST[:, gs], True, True, PSEL.res + ST.res, [pr])
            tt(DVE, UF[0:NS, :], USF[:, :], PCOEF[:, :], ALU.mult, USF.res + PCOEF.res, UF.res)
            tt(DVE, PSB[:, :], UF[0:NS, :], pb[0:NS, :], ALU.add, UF.res + [pr], PSB.res)
            pt, rt = bank()
            ptb = pt.bitcast(BF16)
            for gi in range(4):
                tr(ptb[:, gi * 128:gi * 128 + NS], PSB[:, gi * 128:(gi + 1) * 128], IDB[0:NS, 0:NS],
                   PSB.res + IDB.res, [rt])
            cp(ACT, PTS_[:, :, :], ptb[:, 0:512].rearrange("p (a b) -> p a b", a=4)[:, :, 0:NS], [rt], PTS_.res)
            pb2, pr2 = bank()
            for gi in range(4):
                mm(pb2[:, gi * NS:(gi + 1) * NS], WG[:, gi, :], PTS_[:, gi, :], True, True, WG.res + PTS_.res, [pr2])
            tt(DVE, PYT[:, :, TOK:TT], pb2[:, 0:4 * NS].rearrange("p (a b) -> p a b", a=4),
               PSC[:, :].unsqueeze(2).to_broadcast([128, 4, NS]), ALU.mult, [pr2] + PSC.res, [PYT.res[NT]])

            stage(7)
            MGT = alloc("MGT", [128, 8, TT], BF16, R1, nres=5)
            SG = [alloc("SG%d" % i, [128, 512], F32, R4 + i * 2048) for i in range(4)]
            TG = [alloc("TG%d" % i, [128, 512], F32, R4 + 8192 + i * 2048) for i in range(2)]
            for f in range(8):
                W = wslot()
                wap = W[:, 0:1024].rearrange("p (a b) -> p a b", a=8)
                waa = W[:, 1024:2048].rearrange("p (a b) -> p a b", a=8)
                wpb = W[:, 2048:2560].rearrange("p (a b) -> p a b", a=4)
                wab = W[:, 2560:2816].rearrange("p (a b) -> p a b", a=2)
                fs = slice(f * 128, (f + 1) * 128)
                dma(POOL, wap, kcv(w_in[l])[:, :, 2816 + f * 128:2816 + (f + 1) * 128], (), W.res)
                dma(POOL, waa, kcv(w_in[l])[:, :, 3840 + f * 128:3840 + (f + 1) * 128], (), W.res)
                dma(POOL, wpb, kcv(w_pool_br[l])[:, :, fs], (), W.res)
                dma(POOL, wab, kcv(w_attn_br[l])[:, :, fs], (), W.res)
                for tg in range(5):
                    cs = slice(tg * 512, (tg + 1) * 512) if tg < 4 else slice(TOK, TT)
                    n = 512 if tg < 4 else NS
                    tl = list(range(4 * tg, 4 * tg + 4)) if tg < 4 else [NT]
                    hres = [HT.res[t] for t in tl]
                    pyres = [PYT.res[t] for t in tl]
                    b1, r1_ = bank()
                    b2, r2_ = bank()
                    b3, r3_ = bank()
                    b4, r4_ = bank()
                    for kc in range(8):
                        mm(b1[:, 0:n], wap[:, kc, :], HT[:, kc, cs], kc == 0, kc == 7, W.res + hres, [r1_])
                    for kc in range(8):
                        mm(b2[:, 0:n], waa[:, kc, :], HT[:, kc, cs], kc == 0, kc == 7, W.res + hres, [r2_])
                    for gi in range(4):
                        mm(b3[:, 0:n], wpb[:, gi, :], PYT[:, gi, cs], gi == 0, gi == 3, W.res + pyres, [r3_])
                    for p_ in range(2):
                        mm(b4[:, 0:n], wab[:, p_, :], AYT[:, p_, cs], p_ == 0, p_ == 1, W.res + [AYT.res[tg]], [r4_])
                    k = (f * 5 + tg) % 2
                    sp_, sa_, tg_ = SG[2 * k], SG[2 * k + 1], TG[k]
                    act(sp_[:, 0:n], b1[:, 0:n], AF.Sigmoid, [r1_], sp_.res)
                    act(sa_[:, 0:n], b2[:, 0:n], AF.Sigmoid, [r2_], sa_.res)
                    tt(DVE, sp_[:, 0:n], sp_[:, 0:n], b3[:, 0:n], ALU.mult, sp_.res + [r3_], sp_.res)
                    tt(DVE, tg_[:, 0:n], sa_[:, 0:n], b4[:, 0:n], ALU.mult, sa_.res + [r4_], tg_.res)
                    tt(DVE, MGT[:, f, cs], sp_[:, 0:n], tg_[:, 0:n], ALU.add, sp_.res + tg_.res, [MGT.res[tg]])

            stage(8)
            MPA = alloc("MPA", [128, D], F32, R2)
            MPB = alloc("MPB", [128, D], F32, R2 + 4096)
            MSA = alloc("MSA", [NS, D], F32, R2 + 8192)
            MSB = alloc("MSB", [NS, D], F32, R2 + 12288)
            BB = alloc("BB", [128, D], F32, R3)
            NG = alloc("NG", [128, D], F32, R3 + 4096)
            ada(l, 2, MPA, MSA, BB, NG, None)
            TO = [alloc("TO%d" % i, [128, 512], F32, R4 + i * 2048) for i in range(2)]

            def resid_update(t, c, pb, pr, MP, MS, k):
                xt, xr, np_ = xtile(t)
                M = MP if t < NT else MS
                cs = slice(c * 512, (c + 1) * 512)
                to = TO[k % 2]
                tt(DVE, to[0:np_, :], pb[0:np_, :], M[0:np_, cs], ALU.mult, [pr] + M.res, to.res)
                tt(DVE, xt[:, cs], xt[:, cs], to[0:np_, :], ALU.add, [xr] + to.res, [xr])

            kk = 0
            for c in range(2):
                W = wslot()
                wo = W[:, 0:4096].rearrange("p (a b) -> p a b", a=8)
                dma(POOL, wo, kcv(w_out[l])[:, :, c * 512:(c + 1) * 512], (), W.res)
                for t in range(NT + 1):
                    np_ = 128 if t < NT else NS
                    tg = t // 4 if t < NT else 4
                    pb, pr = bank()
                    for kc in range(8):
                        mm(pb[0:np_, :], MGT[:, kc, tcols(t)], wo[:, kc, :], kc == 0, kc == 7, [MGT.res[tg]] + W.res, [pr])
                    resid_update(t, c, pb, pr, MPA, MSA, kk)
                    kk += 1

            stage(9)
            ada(l, 3, MPB, MSB, BB, NG, None)
            MPC = alloc("MPC", [128, D], F32, R1)
            MSC = alloc("MSC", [NS, D], F32, R1 + 4096)
            ada(l, 4, MPC, MSC, BB, NG, norm2_g)
            norm_phase(MPC, MPB, MSC, MSB, R4)
            stage(10)
            ada(l, 5, MPA, MSA, BB, NG, None)
            AT = alloc("AT", [128, 8, TT], BF16, R1, nres=5)
            RL = [alloc("RL%d" % i, [128, 512], F32, R4 + 4096 + i * 2048) for i in range(2)]
            kk = 0
            for j in range(4):
                for c2 in range(2):
                    W = wslot()
                    wup = W[:, 0:4096].rearrange("p (a b) -> p a b", a=8)
                    dma(POOL, wup, kcv(w_up[l])[:, :, 1024 * j + 512 * c2:1024 * j + 512 * (c2 + 1)], (), W.res)
                    for fc in range(4):
                        for tg in range(5):
                            cs = slice(tg * 512, (tg + 1) * 512) if tg < 4 else slice(TOK, TT)
                            n = 512 if tg < 4 else NS
                            tl = list(range(4 * tg, 4 * tg + 4)) if tg < 4 else [NT]
                            hres = [HT.res[t] for t in tl]
                            pb, pr = bank()
                            for kc in range(8):
                                mm(pb[:, 0:n], wup[:, kc, fc * 128:(fc + 1) * 128], HT[:, kc, cs], kc == 0, kc == 7,
                                   W.res + hres, [pr])
                            rl = RL[kk % 2]
                            kk += 1
                            act(rl[:, 0:n], pb[:, 0:n], AF.Relu, [pr], rl.res)
                            tt(DVE, AT[:, 4 * c2 + fc, cs], rl[:, 0:n], rl[:, 0:n], ALU.mult, rl.res, [AT.res[tg]])
                for c in range(2):
                    W = wslot()
                    wd = W[:, 0:4096].rearrange("p (a b) -> p a b", a=8)
                    dma(POOL, wd, kcv(w_down[l])[:, 8 * j:8 * j + 8, c * 512:(c + 1) * 512], (), W.res)
                    for t in range(NT + 1):
                        np_ = 128 if t < NT else NS
                        tg = t // 4 if t < NT else 4
                        pb, pr = bank()
                        for kc in range(8):
                            mm(pb[0:np_, :], AT[:, kc, tcols(t)], wd[:, kc, :], kc == 0, kc == 7,
                               [AT.res[tg]] + W.res, [pr])
                        resid_update(t, c, pb, pr, MPA, MSA, kk)
                        kk += 1

          except _Stop:
            break
        for t in range(NT):
            dma(SP, yp[t * 128:(t + 1) * 128, :], X[:, t, :], [X.res[t]], ())
        dma(SP, ys, XS[:, :], XS.res, ())

        S.finalize()
        sems = {n: es.enter_context(nc.semaphore(n)) for n in sorted(S.semnames)}
        with nc.Block() as block:
            @block.tensor
            def _(e):
                S.emit_engine(PE, e, sems)

            @block.scalar
            def _(e):
                S.emit_engine(ACT, e, sems)

            @block.vector
            def _(e):
                S.emit_engine(DVE, e, sems)

            @block.gpsimd
            def _(e):
                S.emit_engine(POOL, e, sems)

            @block.sync
            def _(e):
                S.emit_engine(SP, e, sems)
    return nc


def _consts(core):
    half = core % 2
    c = {}
    c["c_ident"] = np.eye(128, dtype=np.float32)
    kk = np.arange(128)[:, None]
    qq = np.arange(128)[None, :]
    cur = np.where(kk <= qq, 0.0, NEG).astype(np.float32)
    prev = np.where(kk >= qq, 0.0, NEG).astype(np.float32)
    pf = prev if half == 1 else np.full((128, 128), NEG, np.float32)
    m = np.stack([np.tile(cur, (1, 4)), np.concatenate([prev, prev, cur, cur], 1),
                  np.concatenate([pf, pf, cur, cur], 1)], axis=1)
    c["c_mask"] = np.ascontiguousarray(m, dtype=np.float32)
    inv = 10000.0 ** (-np.arange(0, 64, 2, dtype=np.float64) / 64)
    rope = np.zeros((48, 128, 64), np.float32)
    i = np.arange(128)
    for g in range(3):
        for b in range(16):
            if g == 0:
                tk = 128 * b + i
            elif g == 1:
                tk = 512 * (b // 4) + (b % 4) + 4 * i
            else:
                tk = b + 16 * i
            pos = (2048 * half + tk).astype(np.float32)
            ang = (pos[:, None] * inv[None, :].astype(np.float32)).astype(np.float32)
            rope[g * 16 + b, :, 0:32] = np.cos(ang)
            rope[g * 16 + b, :, 32:64] = np.sin(ang)
    c["c_rope"] = rope
    angs = (np.float32(8192.0) * inv.astype(np.float32)).astype(np.float32)
    c["c_ropes"] = np.tile(np.concatenate([np.cos(angs), np.sin(angs)])[None, :], (NS, 1)).astype(np.float32)
    am = np.zeros((128, 4, 4, 128), np.float32)
    tp = np.arange(128)[:, None]
    t = np.arange(128)[None, :]
    for gi, w in enumerate((2, 4, 8, 16)):
        inwin = (tp <= t) & (t - tp < w)
        curm = np.where(inwin, 1.0 / w, 0.0) - np.eye(128)
        prevm = np.where(t + 128 - tp < w, 1.0 / w, 0.0)
        am[:, 0, gi, :] = curm
        am[:, 1, gi, :] = prevm
        if half == 0:
            cntv = np.minimum(t + 1, w).astype(np.float64)
            am[:, 2, gi, :] = np.where(inwin, 1.0 / cntv, 0.0) - np.eye(128)
            am[:, 3, gi, :] = 0.0
        else:
            am[:, 2, gi, :] = curm
            am[:, 3, gi, :] = prevm
    c["c_amat"] = am
    sel = np.zeros((NS, NS, 128), np.float32)
    selc = np.zeros((128, NS, NS), np.float32)
    for b in range(NS):
        sel[b, b, :] = 1.0
        selc[:, b, b] = 1.0
    c["c_sel"] = sel
    c["c_selc"] = selc
    psel = np.zeros((NS * 15, 4, NS), np.float32)
    pcoef = np.zeros((NS, 512), np.float32)
    for gi, w in enumerate((2, 4, 8, 16)):
        for b in range(NS):
            for r in range(16 - w, 15):
                psel[b * 15 + r, gi, b] = 1.0 / w
        pcoef[:, gi * 128:(gi + 1) * 128] = 1.0 / w - 1.0
    c["c_psel"] = psel
    c["c_pcoef"] = pcoef
    return c


_NC_CACHE = {}


def kernel(x_prompt, x_sample, cache_k_w128, cache_v_w128, cache_k_w512, cache_v_w512,
           cache_k_w2048, cache_v_w2048, state_pool, c_prompt, c_sample, norm1_g, norm2_g,
           w_ada, b_ada, w_in, q_norm_g, k_norm_g, w_pool_grp, pool_scale, w_pool_br,
           w_attn_br, w_out, w_up, w_down):
    f = lambda a: np.ascontiguousarray(np.asarray(a), dtype=np.float32)
    L = NLAYER
    if L < DEPTH:
        (cache_k_w128, cache_v_w128, cache_k_w512, cache_v_w512, cache_k_w2048, cache_v_w2048, state_pool,
         norm1_g, norm2_g, w_ada, b_ada, w_in, q_norm_g, k_norm_g, w_pool_grp, pool_scale, w_pool_br,
         w_attn_br, w_out, w_up, w_down) = [np.asarray(a)[:L] for a in (
            cache_k_w128, cache_v_w128, cache_k_w512, cache_v_w512, cache_k_w2048, cache_v_w2048, state_pool,
            norm1_g, norm2_g, w_ada, b_ada, w_in, q_norm_g, k_norm_g, w_pool_grp, pool_scale, w_pool_br,
            w_attn_br, w_out, w_up, w_down)]
    x_prompt, x_sample = f(x_prompt), f(x_sample)
    cks = [f(cache_k_w128), f(cache_k_w512), f(cache_k_w2048)]
    cvs = [f(cache_v_w128), f(cache_v_w512), f(cache_v_w2048)]
    state_pool, c_prompt, c_sample = f(state_pool), f(c_prompt), f(c_sample)
    shared = {
        "norm1_g": f(norm1_g), "norm2_g": f(norm2_g), "w_ada": f(w_ada), "b_ada": f(b_ada), "w_in": f(w_in),
        "qk_g": f(np.concatenate([np.asarray(q_norm_g), np.asarray(k_norm_g)], axis=1)),
        "w_pool_grp": f(w_pool_grp),
        "pscT": f(np.asarray(pool_scale).reshape(L, 4, 128).transpose(0, 2, 1)),
        "w_pool_br": f(w_pool_br), "w_attn_br": f(w_attn_br), "w_out": f(w_out), "w_up": f(w_up),
        "w_down": f(w_down),
    }
    in_maps = []
    for c in range(8):
        b, h = c // 2, c % 2
        m = dict(shared)
        m["xp"] = np.ascontiguousarray(x_prompt[b, h * TOK:(h + 1) * TOK, :])
        m["xs"] = np.ascontiguousarray(x_sample[NS * c:NS * (c + 1), 0, :])
        m["cpT"] = np.ascontiguousarray(c_prompt[b].reshape(8, 128).T)
        m["csT"] = np.ascontiguousarray(c_sample[NS * c:NS * (c + 1)].reshape(NS, 8, 128).transpose(2, 1, 0))
        for g in range(3):
            m["ck%d" % g] = np.ascontiguousarray(cks[g][:, NS * c:NS * (c + 1)].reshape(L, NS, WINS[g], 256))
            m["cv%d" % g] = np.ascontiguousarray(cvs[g][:, NS * c:NS * (c + 1)].reshape(L, NS, WINS[g], 256))
        m["spool"] = np.ascontiguousarray(state_pool[:, NS * c:NS * (c + 1)].reshape(L, NS * 15, 512))
        m.update(_consts(c))
        in_maps.append(m)
    if "nc" not in _NC_CACHE:
        _NC_CACHE["nc"] = build_program()
    res = run_bass_kernel_spmd(_NC_CACHE["nc"], in_maps, core_ids=list(range(8)))
    R = res.results
    B = 4
    y_prompt = np.zeros((B, 2 * TOK, D), np.float32)
    y_sample = np.zeros((32, 1, D), np.float32)
    for c in range(8):
        y_prompt[c // 2, (c % 2) * TOK:(c % 2 + 1) * TOK] = R[c]["yp"]
        y_sample[NS * c:NS * (c + 1), 0] = R[c]["ys"]
    outs = [y_prompt, y_sample]
    for g in range(3):
        for nm in ("kp", "vp"):
            outs.append(np.stack([R[2 * b + 1]["%s%d" % (nm, g)] for b in range(B)], axis=1)
                        .reshape(L, B, WINS[g], 4, 64).astype(np.float32))
    outs.append(np.stack([R[2 * b + 1]["poolp"] for b in range(B)], axis=1).astype(np.float32))
    for g in range(3):
        for nm in ("ks", "vs"):
            outs.append(np.concatenate([R[c]["%s%d" % (nm, g)] for c in range(8)], axis=1)
                        .reshape(L, 32, 1, 4, 64).astype(np.float32))
    outs.append(np.concatenate([R[c]["pools"] for c in range(8)], axis=1).astype(np.float32))
    return tuple(outs)
```

```python
import contextlib
import os
import numpy as np
import concourse.bass as bass
import concourse.mybir as mybir
from concourse.bass_utils import run_bass_kernel_spmd

F32 = mybir.dt.float32
BF16 = mybir.dt.bfloat16
ALU = mybir.AluOpType
AF = mybir.ActivationFunctionType
AX = mybir.AxisListType
PE, ACT, DVE, POOL, SP = "pe", "act", "dve", "pool", "sp"
ENGS = (PE, ACT, DVE, POOL, SP)

DEPTH = 4
D = 1024
NT = 16
TOK = 2048
NS = 4
TT = TOK + NS
INW = 4864
DFF = 4096
EPS = 1e-6
WINS = (128, 512, 2048)
DILS = (1, 4, 16)
NBLK_SEND = (1, 4, 16)
RG = [[0, 1], [2, 3], [4, 5], [6, 7]]
NEG = -30000.0
STAGE = float(os.environ.get('KSTAGE', '99'))
NLAYER = int(os.environ.get('KLAYERS', str(DEPTH)))


class _Stop(Exception):
    pass


def stage(n):
    if STAGE <= n:
        raise _Stop()


class Res:
    __slots__ = ("name", "writers", "readers")

    def __init__(self, name):
        self.name = name
        self.writers = {}
        self.readers = {}

    def inherit(self, other):
        for d in (other.writers, other.readers):
            for k, o in d.items():
                c = self.writers.get(k)
                if c is None or c.seq < o.seq:
                    self.writers[k] = o


class Op:
    __slots__ = ("eng", "fn", "deps", "token", "needs_inc", "is_dma", "key", "inc", "seq")

    def __init__(self, eng, fn, is_dma=False):
        self.eng = eng
        self.fn = fn
        self.deps = []
        self.token = None
        self.needs_inc = False
        self.is_dma = is_dma
        self.key = eng
        self.inc = 16


class Sched:
    SEM_ROT = 30000
    NDMA = {SP: 14, POOL: 6, ACT: 2}

    def __init__(self):
        self.ops = {e: [] for e in ENGS}
        self.dma_rr = {q: 0 for q in self.NDMA}
        self.dma_last = {}
        self.dma_val = {}
        self.semnames = set()
        self.nseq = 0

    def op(self, eng, fn, reads=(), writes=(), dma=False, inc=16, semname=None):
        o = Op(eng, fn, is_dma=dma)
        o.inc = inc
        self.nseq += 1
        o.seq = self.nseq
        deps = []
        if dma:
            if semname is None:
                k = self.dma_rr[eng]
                self.dma_rr[eng] = (k + 1) % self.NDMA[eng]
                sname = "dq_%s_%d" % (eng, k)
            else:
                sname = semname
            self.semnames.add(sname)
            self.dma_val[sname] = self.dma_val.get(sname, 0) + inc
            o.token = (sname, self.dma_val[sname])
            o.key = sname
            o.needs_inc = True
            prev = self.dma_last.get(sname)
            if prev is not None:
                deps.append(prev)
            self.dma_last[sname] = o
        for r in reads:
            deps.extend(r.writers.values())
        for r in writes:
            deps.extend(r.writers.values())
            deps.extend(r.readers.values())
        for r in reads:
            r.readers[o.key] = o
        for r in writes:
            r.readers = {}
            r.writers[o.key] = o
        seen = set()
        for d in deps:
            if id(d) in seen or d is o:
                continue
            seen.add(id(d))
            if d.eng == PE and eng == PE and not d.is_dma and not dma:
                continue
            o.deps.append(d)
            if not d.is_dma:
                d.needs_inc = True
        self.ops[eng].append(o)
        return o

    def finalize(self):
        for e in ENGS:
            c = 0
            idx = 0
            for o in self.ops[e]:
                if o.is_dma or not o.needs_inc:
                    continue
                if c >= self.SEM_ROT:
                    idx += 1
                    c = 0
                c += 1
                sname = "pg_%s_%d" % (e, idx)
                self.semnames.add(sname)
                o.token = (sname, c)

    def emit_engine(self, eng, engobj, sems):
        waited = {}
        for o in self.ops[eng]:
            need = {}
            for d in o.deps:
                s, v = d.token
                if waited.get(s, 0) >= v:
                    continue
                if need.get(s, 0) < v:
                    need[s] = v
            for s, v in need.items():
                engobj.wait_ge(sems[s], v)
                waited[s] = v
            ins = o.fn(engobj)
            if o.needs_inc:
                s, v = o.token
                ins.then_inc(sems[s], o.inc if o.is_dma else 1)
        last = {}
        for o in self.ops[eng]:
            if o.is_dma:
                last[o.token[0]] = o.token[1]
        for s, v in last.items():
            if waited.get(s, 0) < v:
                engobj.wait_ge(sems[s], v)


class Buf:
    def __init__(self, t, off, size, nres, name):
        self.t = t
        self.off = off
        self.size = size
        self.res = [Res("%s_%d" % (name, i)) for i in range(nres)]

    def __getitem__(self, k):
        return self.t[k]


def build_program():
    nc = bass.Bass("TRN2", target_bir_lowering=False)
    S = Sched()

    def din(name, shape, dt=F32):
        return nc.dram_tensor(name, list(shape), dt, kind="ExternalInput").ap()

    def dout(name, shape, dt=F32):
        return nc.dram_tensor(name, list(shape), dt, kind="ExternalOutput").ap()

    xp = din("xp", [TOK, D])
    xs = din("xs", [NS, D])
    cpT = din("cpT", [128, 8])
    csT = din("csT", [128, 8, NS])
    ck = [din("ck%d" % g, [NLAYER, NS, WINS[g], 256]) for g in range(3)]
    cv = [din("cv%d" % g, [NLAYER, NS, WINS[g], 256]) for g in range(3)]
    spool = din("spool", [NLAYER, NS * 15, 512])
    norm1_g = din("norm1_g", [NLAYER, D])
    norm2_g = din("norm2_g", [NLAYER, D])
    w_ada = din("w_ada", [NLAYER, D, 6 * D])
    b_ada = din("b_ada", [NLAYER, 6 * D])
    w_in = din("w_in", [NLAYER, D, INW])
    qk_g = din("qk_g", [NLAYER, 128])
    w_pool_grp = din("w_pool_grp", [NLAYER, 4, 128, 128])
    pscT = din("pscT", [NLAYER, 128, 4])
    w_pool_br = din("w_pool_br", [NLAYER, 512, D])
    w_attn_br = din("w_attn_br", [NLAYER, 256, D])
    w_out = din("w_out", [NLAYER, D, D])
    w_up = din("w_up", [NLAYER, D, DFF])
    w_down = din("w_down", [NLAYER, DFF, D])
    c_ident = din("c_ident", [128, 128])
    c_mask = din("c_mask", [128, 3, 512])
    c_rope = din("c_rope", [48, 128, 64])
    c_ropes = din("c_ropes", [NS, 64])
    c_amat = din("c_amat", [128, 4, 4, 128])
    c_sel = din("c_sel", [NS, NS, 128])
    c_selc = din("c_selc", [128, NS, NS])
    c_psel = din("c_psel", [NS * 15, 4, NS])
    c_pcoef = din("c_pcoef", [NS, 512])

    yp = dout("yp", [TOK, D])
    ys = dout("ys", [NS, D])
    kp = [dout("kp%d" % g, [NLAYER, WINS[g], 256]) for g in range(3)]
    vp = [dout("vp%d" % g, [NLAYER, WINS[g], 256]) for g in range(3)]
    poolp = dout("poolp", [NLAYER, 15, 512])
    ks = [dout("ks%d" % g, [NLAYER, NS, 256]) for g in range(3)]
    vs = [dout("vs%d" % g, [NLAYER, NS, 256]) for g in range(3)]
    pools = dout("pools", [NLAYER, NS, 15, 512])

    NSB = 16
    send = [nc.dram_tensor("send_%d" % l, [NSB * 128, 512], BF16).ap() for l in range(NLAYER)]
    recv = [nc.dram_tensor("recv_%d" % l, [2 * NSB * 128, 512], BF16).ap() for l in range(NLAYER)]
    sendb = [nc.dram_tensor("sendb_%d" % l, [NSB * 128, 512], BF16).ap() for l in range(NLAYER)]
    recvb = [nc.dram_tensor("recvb_%d" % l, [2 * NSB * 128, 512], BF16).ap() for l in range(NLAYER)]
    R_send = [Res("send") for l in range(NLAYER)]
    R_recv = [Res("recv") for l in range(NLAYER)]
    R_sendb = [Res("sendb") for l in range(NLAYER)]
    R_recvb = [Res("recvb") for l in range(NLAYER)]

    es = contextlib.ExitStack()
    with es:
        base0 = (nc.sbuf_base + 63) // 64 * 64
        top = nc.sbuf_top
        allbufs = []
        cnt = [0]

        def alloc(name, shape, dt, off, nres=1):
            nb = int(np.prod(shape[1:])) * (2 if dt == BF16 else 4)
            assert off % 32 == 0, (name, off)
            assert off + nb <= top, (name, off, nb, top)
            cnt[0] += 1
            t = nc.alloc_sbuf_tensor_at("%s_%d" % (name, cnt[0]), list(shape), dt, offset=off)
            b = Buf(t, off, nb, nres, name)
            for o in allbufs:
                if o.off < off + nb and off < o.off + o.size:
                    for r in b.res:
                        for ro in o.res:
                            r.inherit(ro)
            allbufs.append(b)
            return b

        cur = [base0]

        def palloc(name, shape, dt, nres=1):
            nb = int(np.prod(shape[1:])) * (2 if dt == BF16 else 4)
            nb = (nb + 31) // 32 * 32
            b = alloc(name, shape, dt, cur[0], nres)
            cur[0] += nb
            return b

        X = palloc("X", [128, NT, D], F32, nres=NT)
        XS = palloc("XS", [NS, D], F32)
        HT = palloc("HT", [128, 8, TT], BF16, nres=NT + 1)
        WR = [palloc("WR%d" % i, [128, 6144], BF16) for i in range(2)]
        IDB = palloc("IDB", [128, 128], BF16)
        MASK = palloc("MASK", [128, 3, 512], BF16)
        ONES = palloc("ONES", [128, 64], BF16)
        SCP = palloc("SCP", [128, 8, 128], BF16)
        SCS = palloc("SCS", [128, 8, NS], BF16)
        SEL = palloc("SEL", [NS, NS, 128], BF16)
        SELC = palloc("SELC", [128, NS, NS], F32)
        PSEL = palloc("PSEL", [NS * 15, 4, NS], F32)
        PCOEF = palloc("PCOEF", [NS, 512], F32)
        EPSB = palloc("EPSB", [128, 1], F32)
        GQK = palloc("GQK", [128, 2, 64], F32)
        PSC = palloc("PSC", [128, 4], F32)
        WG = palloc("WG", [128, 4, 128], BF16)
        ROPES = palloc("ROPES", [NS, 64], F32)
        SS = [palloc("SS%d" % i, [128, 8], F32) for i in range(2)]
        QS = palloc("QS", [NS, 3, 256], BF16)
        SSELF = palloc("SSELF", [NS, 8], F32)
        SACC = palloc("SACC", [NS, 260], F32)
        CCD = palloc("CCD", [128, 8], F32)
        ARENA = (cur[0] + 63) // 64 * 64
        AR_SIZE = top - ARENA
        R1 = ARENA
        R2 = R1 + 32832 + 64
        R3 = R2 + 16416 + 32
        R4 = R3 + 8224
        assert R4 + 14 * 1024 <= top, (R4, top)

        psall = es.enter_context(nc.psum_tensor("psall", [128, 8, 512], F32))
        PB = [Res("bank%d" % i) for i in range(8)]
        pbi = [0]

        def bank():
            i = pbi[0]
            pbi[0] = (i + 1) % 8
            return psall[:, i, :], PB[i]

        def dma(q, out, in_, reads=(), writes=()):
            return S.op(q, lambda e: e.dma_start(out=out, in_=in_), reads, writes, dma=True)

        def mm(out, lhsT, rhs, start, stop, reads, writes):
            return S.op(PE, lambda e: e.matmul(out, lhsT=lhsT, rhs=rhs, start=start, stop=stop), reads, writes)

        def tr(out, in_, ident, reads, writes):
            return S.op(PE, lambda e: e.transpose(out=out, in_=in_, identity=ident), reads, writes)

        def act(out, in_, func, reads, writes, **kw):
            return S.op(ACT, lambda e: e.activation(out=out, in_=in_, func=func, **kw), reads, writes)

        def cp(eng, out, in_, reads, writes):
            if eng == ACT:
                return S.op(ACT, lambda e: e.copy(out=out, in_=in_), reads, writes)
            return S.op(eng, lambda e: e.tensor_copy(out=out, in_=in_), reads, writes)

        def tt(eng, out, in0, in1, op, reads, writes):
            return S.op(eng, lambda e: e.tensor_tensor(out=out, in0=in0, in1=in1, op=op), reads, writes)

        def stt(eng, out, in0, scalar, in1, op0, op1, reads, writes):
            return S.op(eng, lambda e: e.scalar_tensor_tensor(out=out, in0=in0, scalar=scalar, in1=in1,
                                                              op0=op0, op1=op1), reads, writes)

        def red(eng, out, in_, reads, writes):
            return S.op(eng, lambda e: e.tensor_reduce(out=out, in_=in_, axis=AX.X, op=ALU.add), reads, writes)

        def recip(out, in_, reads, writes):
            return S.op(DVE, lambda e: e.reciprocal(out=out, in_=in_), reads, writes)

        def mset(eng, ap, v, writes):
            return S.op(eng, lambda e: e.memset(ap, v), (), writes)

        def kcv(ap2d):
            return ap2d.rearrange("(kc k) n -> k kc n", k=128)

        wri = [0]

        def wslot():
            i = wri[0]
            wri[0] = (i + 1) % len(WR)
            return WR[i]

        stg_off = R1
        STG = alloc("STG", [128, 2048], F32, stg_off)
        dma(SP, STG[:, 0:128], c_ident, (), STG.res)
        cp(DVE, IDB[:, :], STG[:, 0:128], STG.res, IDB.res)
        dma(SP, STG[:, 0:1536], c_mask.rearrange("p a b -> p (a b)"), (), STG.res)
        cp(DVE, MASK[:, :, :].rearrange("p a b -> p (a b)"), STG[:, 0:1536], STG.res, MASK.res)
        mset(DVE, ONES[:, :], 1.0, ONES.res)
        mset(DVE, EPSB[:, :], EPS, EPSB.res)
        dma(SP, STG[:, 0:8], cpT, (), STG.res)
        act(STG[:, 8:16], STG[:, 0:8], AF.Silu, STG.res, STG.res)
        cp(DVE, SCP[:, :, :], STG[:, 8:16].unsqueeze(2).to_broadcast([128, 8, 128]), STG.res, SCP.res)
        dma(SP, STG[:, 0:32], csT.rearrange("p a b -> p (a b)"), (), STG.res)
        act(SCS[:, :, :].rearrange("p a b -> p (a b)"), STG[:, 0:32], AF.Silu, STG.res, SCS.res)
        dma(SP, STG[0:NS, 0:512], c_sel.rearrange("p a b -> p (a b)"), (), STG.res)
        cp(DVE, SEL[:, :, :].rearrange("p a b -> p (a b)"), STG[0:NS, 0:512], STG.res, SEL.res)
        dma(SP, SELC[:, :, :], c_selc, (), SELC.res)
        dma(SP, PSEL[:, :, :], c_psel, (), PSEL.res)
        dma(SP, PCOEF[:, :], c_pcoef, (), PCOEF.res)
        dma(SP, ROPES[:, :], c_ropes, (), ROPES.res)
        for t in range(NT):
            dma(SP, X[:, t, :], xp[t * 128:(t + 1) * 128, :], (), [X.res[t]])
        dma(SP, XS[:, :], xs, (), XS.res)

        def xtile(t):
            if t < NT:
                return X[:, t, :], X.res[t], 128
            return XS[:, :], XS.res[0], NS

        def tcols(t):
            if t < NT:
                return slice(t * 128, (t + 1) * 128)
            return slice(TOK, TT)

        def ada(l, j, MP, MS, BB, NG, scale_norm):
            dma(SP, BB[:, :], b_ada[l:l + 1, j * D:(j + 1) * D].partition_broadcast(128), (), BB.res)
            if scale_norm is not None:
                dma(SP, NG[:, :], scale_norm[l:l + 1, :].partition_broadcast(128), (), NG.res)
            for c in range(2):
                W = wslot()
                wv = W[:, 0:4096].rearrange("p (a b) -> p a b", a=8)
                dma(POOL, wv, kcv(w_ada[l])[:, :, j * D + c * 512: j * D + (c + 1) * 512], (), W.res)
                for (sc, M, np_) in ((SCP, MP, 128), (SCS, MS, NS)):
                    pb, pr = bank()
                    for kc in range(8):
                        mm(pb[0:np_, :], sc[:, kc, :], wv[:, kc, :], kc == 0, kc == 7, W.res + sc.res, [pr])
                    cs = slice(c * 512, (c + 1) * 512)
                    tt(DVE, M[0:np_, cs], pb[0:np_, :], BB[0:np_, cs], ALU.add, [pr] + BB.res, M.res)
                    if scale_norm is not None:
                        stt(DVE, M[0:np_, cs], M[0:np_, cs], 1.0, NG[0:np_, cs], ALU.add, ALU.mult,
                            M.res + NG.res, M.res)

        def norm_phase(GP, SHP, GS, SHS, toff):
            TMP = [alloc("NTMP%d" % i, [128, D], F32, toff + i * 6144) for i in range(2)]
            HB = [alloc("NHB%d" % i, [128, D], BF16, toff + i * 6144 + 4096) for i in range(2)]
            def tile_gen(t):
                xt, xr, np_ = xtile(t)
                G, SH = (GP, SHP) if t < NT else (GS, SHS)
                tmp = TMP[t % 2]
                hb = HB[t % 2]
                ss = SS[t % 2]
                mset(DVE, ss[:, 0:1], 0.0, ss.res)
                act(tmp[0:np_, :], xt, AF.Square, [xr], tmp.res + ss.res, accum_out=ss[0:np_, 0:1])
                act(ss[0:np_, 0:1], ss[0:np_, 0:1], AF.Ln, ss.res + EPSB.res, ss.res, scale=1.0 / D,
                    bias=EPSB[0:np_, 0:1])
                act(ss[0:np_, 0:1], ss[0:np_, 0:1], AF.Exp, ss.res, ss.res, scale=-0.5)
                yield
                stt(DVE, tmp[0:np_, :], xt, ss[0:np_, 0:1], G[0:np_, :], ALU.mult, ALU.mult,
                    [xr] + ss.res + G.res, tmp.res)
                tt(DVE, hb[0:np_, :], tmp[0:np_, :], SH[0:np_, :], ALU.add, tmp.res + SH.res, hb.res)
                yield
                pb, pr = bank()
                pbb = pb.bitcast(BF16)
                for kc in range(8):
                    tr(pbb[:, kc * 128: kc * 128 + np_], hb[0:np_, kc * 128:(kc + 1) * 128], IDB[0:np_, 0:np_],
                       hb.res + IDB.res, [pr])
                cp(ACT, HT[:, :, tcols(t)], pbb[:, 0:1024].rearrange("p (a b) -> p a b", a=8)[:, :, 0:np_],
                   [pr], [HT.res[t]])

            gens = [tile_gen(t) for t in range(NT + 1)]
            n = len(gens)
            for it in range(n + 2):
                for k in (2, 1, 0):
                    idx = it - k
                    if 0 <= idx < n:
                        next(gens[idx], None)

        for l in range(NLAYER):
          try:
            MPA = alloc("MPA", [128, D], F32, R2)
            MPB = alloc("MPB", [128, D], F32, R2 + 4096)
            MSA = alloc("MSA", [NS, D], F32, R2 + 8192)
            MSB = alloc("MSB", [NS, D], F32, R2 + 12288)
            BB = alloc("BB", [128, D], F32, R3)
            NG = alloc("NG", [128, D], F32, R3 + 4096)
            ada(l, 0, MPB, MSB, BB, NG, None)
            ada(l, 1, MPA, MSA, BB, NG, norm1_g)
            stage(1)
            norm_phase(MPA, MPB, MSA, MSB, R4)

            stage(2)
            ACC = alloc("ACC", [128, 4, TOK], F32, R1, nres=NT)
            KV = [alloc("KV%d" % i, [128, 512], BF16, R2 + i * 1024) for i in range(8)]
            H0 = alloc("H0", [128, 512], BF16, R2 + 8192)
            H1 = [alloc("H1_%d" % i, [128, 512], BF16, R2 + 9216 + i * 1024) for i in range(4)]
            H2 = [alloc("H2_%d" % i, [128, 512], BF16, R2 + 13312 + i * 1024) for i in range(3)]
            o = R3
            QK = [alloc("QK%d" % i, [128, 512], F32, o + i * 2048) for i in range(2)]
            o += 4096
            QO = [alloc("QO%d" % i, [128, 512], F32, o + i * 2048) for i in range(2)]
            o += 4096
            SQ = alloc("SQ", [128, 512], F32, o)
            o += 2048
            TA = alloc("TA", [128, 256], F32, o)
            o += 1024
            TB = alloc("TB", [128, 256], F32, o)
            o += 1024
            QB = [alloc("QB%d" % i_, [128, 512], BF16, o + i_ * 1024) for i_ in range(2)]
            o += 2048
            VF = [alloc("VF%d" % i, [128, 256], F32, o + i * 1024) for i in range(2)]
            o += 2048
            QT = [alloc("QT%d" % i, [128, 2, 128], BF16, o + i * 512) for i in range(2)]
            o += 1024
            PTS = [alloc("PTS%d" % i, [128, 1024], BF16, o + i * 2048) for i in range(2)]
            o += 4096
            ROPE = [alloc("ROPE%d" % i, [128, 64], F32, o + i * 256) for i in range(4)]
            o += 1024
            assert o <= top, (o, top)
            dma(SP, GQK[:, :, :].rearrange("p a b -> p (a b)"), qk_g[l:l + 1, :].partition_broadcast(128), (), GQK.res)
            mset(DVE, SACC[:, :], 0.0, SACC.res)

            blkc = [0]

            def blk_cols(g, b):
                if g == 0:
                    return slice(128 * b, 128 * b + 128), [b]
                if g == 1:
                    n1, r1 = b // 4, b % 4
                    return slice(512 * n1 + r1, 512 * n1 + 512, 4), list(range(4 * n1, 4 * n1 + 4))
                return slice(b, TOK, 16), list(range(NT))

            def load_wq(g):
                W = wslot()
                wv = W[:, :].rearrange("p (a b) -> p a b", a=8)
                for i, c0 in enumerate((512, 1280, 2048)):
                    dma(POOL, wv[:, :, i * 256:(i + 1) * 256], kcv(w_in[l])[:, :, c0 + 256 * g: c0 + 256 * g + 256],
                        (), W.res)
                return W, wv

            def project(g, b, W, wv, kvslot, out_rows=None, sample=False):
                i = blkc[0]
                blkc[0] += 1
                qk, qo, vf, qt, rp, ss = QK[i % 2], QO[i % 2], VF[i % 2], QT[i % 2], ROPE[i % 4], SS[i % 2]
                if sample:
                    np_ = NS
                    cols, tl = slice(TOK, TT), [NT]
                    ropeap, roper = ROPES, ROPES.res
                else:
                    np_ = 128
                    cols, tl = blk_cols(g, b)
                    dma(SP, rp[:, :], c_rope[g * 16 + b], (), rp.res)
                    ropeap, roper = rp, rp.res
                hres = [HT.res[t] for t in tl]
                p1, r1_ = bank()
                p2, r2_ = bank()
                for kc in range(8):
                    mm(p1[0:np_, :], HT[:, kc, cols], wv[:, kc, 0:512], kc == 0, kc == 7, hres + W.res, [r1_])
                for kc in range(8):
                    mm(p2[0:np_, 0:256], HT[:, kc, cols], wv[:, kc, 512:768], kc == 0, kc == 7, hres + W.res, [r2_])
                P = slice(0, np_)
                cp(DVE, qk[P, :], p1[P, :], [r1_], qk.res)
                KP = int(os.environ.get("KPROJ", "9"))
                if KP <= 1:
                    return None
                tt(DVE, SQ[P, :], qk[P, :], qk[P, :], ALU.mult, qk.res, SQ.res)
                red(DVE, ss[P, 0:8], SQ[P, :].rearrange("p (h d) -> p h d", h=8), SQ.res, ss.res)
                act(ss[P, 0:8], ss[P, 0:8], AF.Ln, ss.res + EPSB.res, ss.res, scale=1.0 / 64, bias=EPSB[P, 0:1])
                act(ss[P, 0:8], ss[P, 0:8], AF.Exp, ss.res, ss.res, scale=-0.5)
                tt(DVE, qk[P, :].rearrange("p (a h d) -> p a h d", a=2, h=4),
                   qk[P, :].rearrange("p (a h d) -> p a h d", a=2, h=4),
                   GQK[P, :, :].unsqueeze(2).to_broadcast([np_, 2, 4, 64]), ALU.mult, qk.res + GQK.res, qk.res)
                qv = qk[P, :].rearrange("p (h d) -> p h d", h=8)
                ov = qo[P, :].rearrange("p (h d) -> p h d", h=8)
                cosb = ropeap[P, 0:32].unsqueeze(1).to_broadcast([np_, 8, 32])
                sinb = ropeap[P, 32:64].unsqueeze(1).to_broadcast([np_, 8, 32])
                ta = TA[P, :].rearrange("p (h d) -> p h d", h=8)
                tb = TB[P, :].rearrange("p (h d) -> p h d", h=8)
                tt(DVE, ta, qv[:, :, 0:32], cosb, ALU.mult, qk.res + roper, TA.res)
                tt(DVE, tb, qv[:, :, 32:64], sinb, ALU.mult, qk.res + roper, TB.res)
                tt(DVE, ov[:, :, 0:32], ta, tb, ALU.subtract, TA.res + TB.res, qo.res)
                tt(DVE, ta, qv[:, :, 32:64], cosb, ALU.mult, qk.res + roper, TA.res)
                tt(DVE, tb, qv[:, :, 0:32], sinb, ALU.mult, qk.res + roper, TB.res)
                tt(DVE, ov[:, :, 32:64], ta, tb, ALU.add, TA.res + TB.res, qo.res)
                tt(DVE, ov, ov, ss[P, 0:8].unsqueeze(2).to_broadcast([np_, 8, 64]), ALU.mult, qo.res + ss.res, qo.res)
                cp(ACT, vf[P, :], p2[P, 0:256], [r2_], vf.res)
                if KP <= 2:
                    return None
                if sample:
                    cp(DVE, QS[:, g, :], qo[P, 0:256], qo.res, QS.res)
                    tt(DVE, TA[P, :], qo[P, 0:256], qo[P, 256:512], ALU.mult, qo.res, TA.res)
                    red(DVE, SSELF[:, 0:4], TA[P, :].rearrange("p (h d) -> p h d", h=4), TA.res, SSELF.res)
                    act(SSELF[:, 4:8], SSELF[:, 0:4], AF.Exp, SSELF.res, SSELF.res, scale=0.125)
                    tt(DVE, TA[P, :].rearrange("p (h d) -> p h d", h=4), vf[P, :].rearrange("p (h d) -> p h d", h=4),
                       SSELF[:, 4:8].unsqueeze(2).to_broadcast([NS, 4, 64]), ALU.mult, vf.res + SSELF.res, TA.res)
                    tt(DVE, SACC[:, 0:256], SACC[:, 0:256], TA[P, :], ALU.add, SACC.res + TA.res, SACC.res)
                    tt(DVE, SACC[:, 256:260], SACC[:, 256:260], SSELF[:, 4:8], ALU.add, SACC.res + SSELF.res, SACC.res)
                    dma(SP, ks[g][l], qo[P, 256:512], qo.res, ())
                    dma(SP, vs[g][l], vf[P, :], vf.res, ())
                    return None
                cp(DVE, QB[0][:, :], qo[:, :], qo.res, QB[0].res)
                cp(ACT, kvslot[:, 256:512], p2[:, 0:256], [r2_], kvslot.res)
                pt, rt = bank()
                ptb = pt.bitcast(BF16)
                for j in range(4):
                    tr(ptb[:, j * 128:(j + 1) * 128], QB[0][:, j * 128:(j + 1) * 128], IDB[:, :], QB[0].res + IDB.res, [rt])
                cp(ACT, qt[:, :, :].rearrange("p a b -> p (a b)"), ptb[:, 0:256], [rt], qt.res)
                cp(ACT, kvslot[:, 0:256], ptb[:, 256:512], [rt], kvslot.res)
                if KP <= 3:
                    return qt
                if out_rows is not None:
                    kd, vd = out_rows
                    dma(SP, kd, qo[:, 256:512], qo.res, ())
                    dma(SP, vd, vf[:, :], vf.res, ())
                return qt

            def attend(g, b, qt, kvc, kvp, mprev, first):
                i = blkc[0]
                pts = PTS[i % 2]
                cols, tl = blk_cols(g, b)
                for half in range(2):
                    rows = slice(64 * half, 64 * half + 64)
                    pb, pr = bank()
                    mm(pb[:, :], IDB[:, :], MASK[:, mprev, :], True, False, IDB.res + MASK.res, [pr])
                    for j, (kvb, pair) in enumerate(((kvp, 0), (kvp, 1), (kvc, 0), (kvc, 1))):
                        mm(pb[:, 128 * j:128 * j + 128], kvb[rows, pair * 128:(pair + 1) * 128], qt[rows, pair, :],
                           False, j == 3, kvb.res + qt.res, [pr])
                    act(pts[:, half * 512:(half + 1) * 512], pb[:, :], AF.Exp, [pr], pts.res, scale=0.125)
                po, pro = bank()
                for h in range(4):
                    pair, half = h // 2, h % 2
                    rows = slice(64 * half, 64 * half + 64)
                    vsl = slice(256 + 64 * h, 256 + 64 * h + 64)
                    pp = pts[:, half * 512 + pair * 128:half * 512 + pair * 128 + 128]
                    pc = pts[:, half * 512 + 256 + pair * 128:half * 512 + 256 + pair * 128 + 128]
                    mm(po[rows, pair * 128:(pair + 1) * 128], kvp[:, vsl], pp, True, False, kvp.res + pts.res, [pro])
                    mm(po[rows, pair * 128:(pair + 1) * 128], kvc[:, vsl], pc, False, True, kvc.res + pts.res, [pro])
                    mm(po[rows, 256 + pair * 128:256 + (pair + 1) * 128], ONES[:, :], pp, True, False,
                       ONES.res + pts.res, [pro])
                    mm(po[rows, 256 + pair * 128:256 + (pair + 1) * 128], ONES[:, :], pc, False, True,
                       ONES.res + pts.res, [pro])
                av = ACC[:, :, cols]
                ares = [ACC.res[t] for t in tl]
                pov = po[:, :].rearrange("p (a b) -> p a b", a=4)
                if first:
                    cp(DVE, av, pov, [pro], ares)
                else:
                    tt(DVE, av, av, pov, ALU.add, [pro] + ares, ares)

            def block_task(g, b, W, wv, kvslot, orows=None, send_dst=None, prev=None, mprev=1, first=False,
                           hist=None):
                i = blkc[0]
                blkc[0] += 1
                qk, qo, vf, qt, rp, ss = QK[i % 2], QO[i % 2], VF[i % 2], QT[i % 2], ROPE[i % 4], SS[i % 2]
                pts = PTS[i % 2]
                cols, tl = blk_cols(g, b)
                if hist is not None:
                    hbuf, hsrc, hres_ = hist
                    dma(SP, hbuf[:, :], hsrc, [hres_], hbuf.res)
                dma(SP, rp[:, :], c_rope[g * 16 + b], (), rp.res)
                hres = [HT.res[t] for t in tl]
                p1, r1_ = bank()
                p2, r2_ = bank()
                for kc in range(8):
                    mm(p1[:, :], HT[:, kc, cols], wv[:, kc, 0:512], kc == 0, kc == 7, hres + W.res, [r1_])
                for kc in range(8):
                    mm(p2[:, 0:256], HT[:, kc, cols], wv[:, kc, 512:768], kc == 0, kc == 7, hres + W.res, [r2_])
                cp(ACT, qk[:, :], p1[:, :], [r1_], qk.res)
                act(SQ[:, :], p1[:, :], AF.Square, [r1_], SQ.res)
                red(DVE, ss[:, 0:8], SQ[:, :].rearrange("p (h d) -> p h d", h=8), SQ.res, ss.res)
                act(ss[:, 0:8], ss[:, 0:8], AF.Ln, ss.res + EPSB.res, ss.res, scale=1.0 / 64, bias=EPSB[:, 0:1])
                act(ss[:, 0:8], ss[:, 0:8], AF.Exp, ss.res, ss.res, scale=-0.5)
                tt(DVE, qk[:, :].rearrange("p (a h d) -> p a h d", a=2, h=4),
                   qk[:, :].rearrange("p (a h d) -> p a h d", a=2, h=4),
                   GQK[:, :, :].unsqueeze(2).to_broadcast([128, 2, 4, 64]), ALU.mult, qk.res + GQK.res, qk.res)
                qv = qk[:, :].rearrange("p (h d) -> p h d", h=8)
                ov = qo[:, :].rearrange("p (h d) -> p h d", h=8)
                cosb = rp[:, 0:32].unsqueeze(1).to_broadcast([128, 8, 32])
                sinb = rp[:, 32:64].unsqueeze(1).to_broadcast([128, 8, 32])
                ta = TA[:, :].rearrange("p (h d) -> p h d", h=8)
                tb = TB[:, :].rearrange("p (h d) -> p h d", h=8)
                tc_ = SQ[:, 0:256].rearrange("p (h d) -> p h d", h=8)
                td_ = SQ[:, 256:512].rearrange("p (h d) -> p h d", h=8)
                tt(DVE, ta, qv[:, :, 0:32], cosb, ALU.mult, qk.res + rp.res, TA.res)
                tt(DVE, tb, qv[:, :, 32:64], sinb, ALU.mult, qk.res + rp.res, TB.res)
                tt(DVE, tc_, qv[:, :, 32:64], cosb, ALU.mult, qk.res + rp.res, SQ.res)
                tt(DVE, td_, qv[:, :, 0:32], sinb, ALU.mult, qk.res + rp.res, SQ.res)
                tt(DVE, ov[:, :, 32:64], tc_, td_, ALU.add, SQ.res, qo.res)
                tt(DVE, ov[:, :, 0:32], ta, tb, ALU.subtract, TA.res + TB.res, qo.res)
                tt(DVE, ov, ov, ss[:, 0:8].unsqueeze(2).to_broadcast([128, 8, 64]), ALU.mult, qo.res + ss.res, qo.res)
                cp(ACT, vf[:, :], p2[:, 0:256], [r2_], vf.res)
                cp(ACT, kvslot[:, 256:512], p2[:, 0:256], [r2_], kvslot.res)
                yield
                cp(ACT, QB[i % 2][:, :], qo[:, :], qo.res, QB[i % 2].res)
                pt, rt = bank()
                ptb = pt.bitcast(BF16)
                qb = QB[i % 2]
                for j in range(4):
                    tr(ptb[:, j * 128:(j + 1) * 128], qb[:, j * 128:(j + 1) * 128], IDB[:, :], qb.res + IDB.res, [rt])
                cp(ACT, qt[:, :, :].rearrange("p a b -> p (a b)"), ptb[:, 0:256], [rt], qt.res)
                cp(ACT, kvslot[:, 0:256], ptb[:, 256:512], [rt], kvslot.res)
                if orows is not None:
                    kd, vd = orows
                    dma(SP, kd, qo[:, 256:512], qo.res, ())
                    dma(SP, vd, vf[:, :], vf.res, ())
                if send_dst is not None:
                    sd, rs = send_dst
                    dma(SP, sd, kvslot[:, :], kvslot.res, [rs])
                yield
                if prev is None:
                    return
                kvc, kvp = kvslot, prev
                for half in range(2):
                    rows = slice(64 * half, 64 * half + 64)
                    pb, pr = bank()
                    mm(pb[:, :], IDB[:, :], MASK[:, mprev, :], True, False, IDB.res + MASK.res, [pr])
                    for j, (kvb, pair) in enumerate(((kvp, 0), (kvp, 1), (kvc, 0), (kvc, 1))):
                        mm(pb[:, 128 * j:128 * j + 128], kvb[rows, pair * 128:(pair + 1) * 128], qt[rows, pair, :],
                           False, j == 3, kvb.res + qt.res, [pr])
                    act(pts[:, half * 512:(half + 1) * 512], pb[:, :], AF.Exp, [pr], pts.res, scale=0.125)
                yield
                po, pro = bank()
                for h in range(4):
                    pair, half = h // 2, h % 2
                    rows = slice(64 * half, 64 * half + 64)
                    vsl = slice(256 + 64 * h, 256 + 64 * h + 64)
                    pp = pts[:, half * 512 + pair * 128:half * 512 + pair * 128 + 128]
                    pc = pts[:, half * 512 + 256 + pair * 128:half * 512 + 256 + pair * 128 + 128]
                    mm(po[rows, pair * 128:(pair + 1) * 128], kvp[:, vsl], pp, True, False, kvp.res + pts.res, [pro])
                    mm(po[rows, pair * 128:(pair + 1) * 128], kvc[:, vsl], pc, False, True, kvc.res + pts.res, [pro])
                    mm(po[rows, 256 + pair * 128:256 + (pair + 1) * 128], ONES[:, :], pp, True, False,
                       ONES.res + pts.res, [pro])
                    mm(po[rows, 256 + pair * 128:256 + (pair + 1) * 128], ONES[:, :], pc, False, True,
                       ONES.res + pts.res, [pro])
                av = ACC[:, :, cols]
                ares = [ACC.res[t] for t in tl]
                pov = po[:, :].rearrange("p (a b) -> p a b", a=4)
                if first:
                    cp(DVE, av, pov, [pro], ares)
                else:
                    tt(DVE, av, av, pov, ALU.add, [pro] + ares, ares)

            def run_pipeline(gens):
                n = len(gens)
                for it in range(n + 3):
                    for k in (3, 2, 0, 1):
                        idx = it - k
                        if 0 <= idx < n:
                            next(gens[idx], None)

            def send_blocks(g, blks, W, wv, sb0):
                sd, rs = (send[l], R_send[l]) if g == 2 else (sendb[l], R_sendb[l])
                for si, b in enumerate(blks):
                    kvs = KV[si % 8]
                    project(g, b, W, wv, kvs)
                    dma(SP, sd[(sb0 + si) * 128:(sb0 + si + 1) * 128, :], kvs[:, :], kvs.res, [rs])

            def collective(sd, rv, rs, rr):
                if os.environ.get("KNOCC"):
                    return
                S.op(POOL, lambda e: e.collective_compute("AllGather", ALU.bypass, replica_groups=RG,
                                                          ins=[sd], outs=[rv]),
                     [rs], [rr], dma=True, inc=1, semname="cc_sem")
                S.op(POOL, lambda e: e.memset(CCD[:, :], 0.0), [rr], CCD.res)

            def out_rows(g, b):
                if g == 0:
                    return (kp[0][l], vp[0][l]) if b == 15 else None
                if g == 1:
                    if b < 12:
                        return None
                    r1 = b - 12
                    return (kp[1][l].rearrange("(i f) c -> f i c", f=4)[r1],
                            vp[1][l].rearrange("(i f) c -> f i c", f=4)[r1])
                return (kp[2][l].rearrange("(i f) c -> f i c", f=16)[b],
                        vp[2][l].rearrange("(i f) c -> f i c", f=16)[b])

            W2, wv2 = load_wq(2)
            run_pipeline([block_task(2, b, W2, wv2, KV[b % 8],
                                     send_dst=(send[l][b * 128:(b + 1) * 128, :], R_send[l])) for b in range(16)])
            W0, wv0 = load_wq(0)
            W1, wv1 = load_wq(1)
            collective(send[l], recv[l], R_send[l], R_recv[l])
            stage(3)
            run_pipeline([block_task(0, 15, W0, wv0, KV[0], send_dst=(sendb[l][0:128, :], R_sendb[l]))] +
                         [block_task(1, 12 + i_, W1, wv1, KV[1 + i_],
                                     send_dst=(sendb[l][(1 + i_) * 128:(2 + i_) * 128, :], R_sendb[l]))
                          for i_ in range(4)])
            WU = wslot()
            wu = WU[:, 0:4096].rearrange("p (a b) -> p a b", a=8)
            dma(POOL, wu, kcv(w_in[l])[:, :, 0:512], (), WU.res)
            pb, pr = bank()
            for kc in range(8):
                mm(pb[:, :], HT[:, kc, tcols(15)], wu[:, kc, :], kc == 0, kc == 7, [HT.res[15]] + WU.res, [pr])
            cp(ACT, H2[0][:, :], pb[:, :], [pr], H2[0].res)
            cp(ACT, SQ[:, :], pb[:, :], [pr], SQ.res)
            dma(SP, sendb[l][5 * 128:6 * 128, :], H2[0][:, :], H2[0].res, [R_sendb[l]])
            dma(SP, poolp[l], SQ[113:128, :], SQ.res, ())
            W0, wv0 = load_wq(0)
            collective(sendb[l], recvb[l], R_sendb[l], R_recvb[l])
            stage(3.5)
            tasks = [block_task(0, 0, W0, wv0, KV[0])]
            for b in range(1, 16):
                tasks.append(block_task(0, b, W0, wv0, KV[b % 8], orows=out_rows(0, b), prev=KV[(b - 1) % 8],
                                        mprev=1, first=True))
            tasks.append(block_task(0, 0, W0, wv0, KV[0], prev=H0, mprev=2, first=True,
                                    hist=(H0, recvb[l][0:128, :], R_recvb[l])))
            run_pipeline(tasks)
            project(0, 0, W0, wv0, None, sample=True)
            W1, wv1 = load_wq(1)
            tasks = [block_task(1, b, W1, wv1, KV[b % 8]) for b in range(4)]
            for b in range(4, 16):
                tasks.append(block_task(1, b, W1, wv1, KV[b % 8], orows=out_rows(1, b), prev=KV[(b - 4) % 8],
                                        mprev=1, first=False))
            for b in range(4):
                tasks.append(block_task(1, b, W1, wv1, KV[b % 8], prev=H1[b], mprev=2, first=False,
                                        hist=(H1[b], recvb[l][(1 + b) * 128:(2 + b) * 128, :], R_recvb[l])))
            run_pipeline(tasks)
            project(1, 0, W1, wv1, None, sample=True)
            W2, wv2 = load_wq(2)
            run_pipeline([block_task(2, b, W2, wv2, KV[b % 8], orows=out_rows(2, b), prev=H2[b % 3], mprev=2,
                                     first=False, hist=(H2[b % 3], recv[l][b * 128:(b + 1) * 128, :], R_recv[l]))
                          for b in range(16)])
            project(2, 0, W2, wv2, None, sample=True)

            stage(4)
            AYT = alloc("AYT", [128, 2, TT], BF16, R3, nres=5)
            RD = [alloc("RD%d" % i, [128, 2, 512], F32, R4 + i * 4096) for i in range(2)]
            for tg in range(4):
                cs = slice(tg * 512, (tg + 1) * 512)
                ares = [ACC.res[t] for t in range(4 * tg, 4 * tg + 4)]
                rd = RD[tg % 2]
                recip(rd[:, :, :], ACC[:, 2:4, cs], ares, rd.res)
                tt(DVE, AYT[:, :, cs], ACC[:, 0:2, cs], rd[:, :, :], ALU.mult, ares + rd.res, [AYT.res[tg]])

            stage(5)
            KC = alloc("KC", [128, NS, 256], F32, R1)
            VC = alloc("VC", [128, NS, 256], F32, R1 + 4096)
            PROD = alloc("PROD", [128, NS * 256], F32, R1 + 8192)
            PVP = alloc("PVP", [128, NS, 260], F32, R1 + 12288)
            SCO = alloc("SCO", [128, 16], F32, R1 + 12288 + 4160)
            SPX = alloc("SPX", [NS, 16], F32, R1 + 12288 + 4160 + 64)
            AYS = alloc("AYS", [NS, 256], BF16, R1 + 12288 + 4160 + 128)
            for g in range(3):
                dil = DILS[g]
                dma(SP, KC[:, :, :], ck[g][l][:, 0:WINS[g]:dil, :].rearrange("b j c -> j b c"), (), KC.res)
                dma(SP, VC[:, :, :], cv[g][l][:, 0:WINS[g]:dil, :].rearrange("b j c -> j b c"), (), VC.res)
                pq = [bank(), bank()]
                for bb in range(NS):
                    pb, pr = pq[bb // 2]
                    mm(pb[:, (bb % 2) * 256:(bb % 2) * 256 + 256], SEL[:, bb, :], QS[:, g, :], True, True,
                       SEL.res + QS.res, [pr])
                for hf in range(2):
                    pb, pr = pq[hf]
                    tt(DVE, PROD[:, hf * 512:(hf + 1) * 512],
                       KC[:, 2 * hf:2 * hf + 2, :].rearrange("p a b -> p (a b)"), pb[:, :], ALU.mult,
                       KC.res + [pr], PROD.res)
                red(DVE, SCO[:, :], PROD[:, :].rearrange("p (a d) -> p a d", d=64), PROD.res, SCO.res)
                act(PVP[:, :, 256:260], SCO[:, :].rearrange("p (a b) -> p a b", a=NS), AF.Exp, SCO.res, PVP.res,
                    scale=0.125)
                tt(DVE, PVP[:, :, 0:256].rearrange("p a (h d) -> p a h d", h=4),
                   VC[:, :, :].rearrange("p a (h d) -> p a h d", h=4),
                   PVP[:, :, 256:260].unsqueeze(3).to_broadcast([128, NS, 4, 64]), ALU.mult,
                   VC.res + PVP.res, PVP.res)
                pb, pr = bank()
                for bb in range(NS):
                    mm(pb[0:NS, 0:260], SELC[:, bb, :], PVP[:, bb, :], bb == 0, bb == NS - 1, SELC.res + PVP.res, [pr])
                tt(DVE, SACC[:, :], SACC[:, :], pb[0:NS, 0:260], ALU.add, SACC.res + [pr], SACC.res)
            recip(SPX[:, 4:8], SACC[:, 256:260], SACC.res, SPX.res)
            tt(DVE, AYS[:, :].rearrange("p (h d) -> p h d", h=4), SACC[:, 0:256].rearrange("p (h d) -> p h d", h=4),
               SPX[:, 4:8].unsqueeze(2).to_broadcast([NS, 4, 64]), ALU.mult, SACC.res + SPX.res, AYS.res)
            pt, rt = bank()
            ptb = pt.bitcast(BF16)
            for pr_ in range(2):
                tr(ptb[:, pr_ * 128:pr_ * 128 + NS], AYS[:, pr_ * 128:(pr_ + 1) * 128], IDB[0:NS, 0:NS],
                   AYS.res + IDB.res, [rt])
            cp(ACT, AYT[:, :, TOK:TT], ptb[:, 0:256].rearrange("p (a b) -> p a b", a=2)[:, :, 0:NS], [rt],
               [AYT.res[4]])

            stage(6)
            PYT = alloc("PYT", [128, 4, TT], BF16, R2, nres=NT + 1)
            o = R1 + 17408
            UB = [alloc("UB%d" % i, [128, 512], BF16, o + i * 1024) for i in range(3)]
            o += 3072
            UH = alloc("UH", [128, 512], BF16, o)
            o += 1024
            UF = alloc("UF", [128, 512], F32, o)
            o += 2048
            PTB = [alloc("PTB%d" % i, [128, 512], BF16, o + i * 1024) for i in range(2)]
            o += 2048
            AM = alloc("AM", [128, 16, 128], BF16, o)
            o += 4096
            assert o <= R2
            o = R4
            ST = alloc("ST", [NS * 15, 512], F32, o)
            o += 2048
            USF = alloc("USF", [NS, 512], F32, o)
            o += 2048
            PSB = alloc("PSB", [NS, 512], BF16, o)
            o += 1024
            PTS_ = alloc("PTSs", [128, 4, NS], BF16, o)
            o += 64
            assert o <= top
            STG2 = alloc("STG2", [128, 2048], F32, R1)
            dma(SP, STG2[:, :], c_amat.rearrange("p a b c -> p (a b c)"), (), STG2.res)
            cp(DVE, AM[:, :, :].rearrange("p a b -> p (a b)"), STG2[:, :], STG2.res, AM.res)
            dma(POOL, WG[:, :, :], w_pool_grp[l].rearrange("g c e -> c g e"), (), WG.res)
            dma(SP, PSC[:, :], pscT[l], (), PSC.res)
            WU = wslot()
            wu = WU[:, 0:4096].rearrange("p (a b) -> p a b", a=8)
            dma(POOL, wu, kcv(w_in[l])[:, :, 0:512], (), WU.res)

            def uproj(t, dst, fp32dst=None):
                np_ = 128 if t < NT else NS
                pb, pr = bank()
                for kc in range(8):
                    mm(pb[0:np_, :], HT[:, kc, tcols(t)], wu[:, kc, :], kc == 0, kc == 7, [HT.res[t]] + WU.res, [pr])
                if dst is not None:
                    cp(ACT, dst[0:np_, :], pb[0:np_, :], [pr], dst.res)
                if fp32dst is not None:
                    cp(ACT, fp32dst[0:np_, :], pb[0:np_, :], [pr], fp32dst.res)

            def pool_tile(t, ucur, uprev, acur, aprev):
                pb, pr = bank()
                for gi in range(4):
                    gs = slice(gi * 128, (gi + 1) * 128)
                    mm(pb[:, gs], ucur[:, gs], AM[:, acur * 4 + gi, :], True, False, ucur.res + AM.res, [pr])
                    mm(pb[:, gs], uprev[:, gs], AM[:, aprev * 4 + gi, :], False, True, uprev.res + AM.res, [pr])
                ptb_ = PTB[t % 2]
                cp(ACT, ptb_[:, :], pb[:, :], [pr], ptb_.res)
                yield
                pb2, pr2 = bank()
                for gi in range(4):
                    gs = slice(gi * 128, (gi + 1) * 128)
                    mm(pb2[:, gs], WG[:, gi, :], ptb_[:, gs], True, True, WG.res + ptb_.res, [pr2])
                tt(DVE, PYT[:, :, tcols(t)], pb2[:, :].rearrange("p (a b) -> p a b", a=4),
                   PSC[:, :].unsqueeze(2).to_broadcast([128, 4, 128]), ALU.mult, [pr2] + PSC.res, [PYT.res[t]])

            uproj(0, UB[0])
            uproj(1, UB[1])
            pend = None
            for t in range(1, NT):
                if t + 1 < NT:
                    uproj(t + 1, UB[(t + 1) % 3])
                g_ = pool_tile(t, UB[t % 3], UB[(t - 1) % 3], 0, 1)
                next(g_, None)
                if pend is not None:
                    next(pend, None)
                pend = g_
            dma(SP, UH[:, :], recvb[l][5 * 128:6 * 128, :], [R_recvb[l]], UH.res)
            uproj(0, UB[0])
            g_ = pool_tile(0, UB[0], UH, 2, 3)
            next(g_, None)
            next(pend, None)
            next(g_, None)
            uproj(NT, None, USF)
            dma(SP, ST[:, :], spool[l], (), ST.res)
            dma(SP, pools[l][:, 0:14, :], spool[l].rearrange("(b r) c -> b r c", r=15)[:, 1:15, :], (), ())
            dma(SP, pools[l][:, 14, :], USF[:, :], USF.res, ())
            pb, pr = bank()
            for gi in range(4):
                gs = slice(gi * 128, (gi + 1) * 128)
                mm(pb[0:NS, gs], PSEL[:, gi, :], ST[:, gs], True, True, PSEL.res + ST.res, [pr])
            tt(DVE, UF[0:NS, :], USF[:, :], PCOEF[:, :], ALU.mult, USF.res + PCOEF.res, UF.res)
            tt(DVE, PSB[:, :], UF[0:NS, :], pb[0:NS, :], ALU.add, UF.res + [pr], PSB.res)
            pt, rt = bank()
            ptb = pt.bitcast(BF16)
            for gi in range(4):
                tr(ptb[:, gi * 128:gi * 128 + NS], PSB[:, gi * 128:(gi + 1) * 128], IDB[0:NS, 0:NS],
                   PSB.res + IDB.res, [rt])
            cp(ACT, PTS_[:, :, :], ptb[:, 0:512].rearrange("p (a b) -> p a b", a=4)[:, :, 0:NS], [rt], PTS_.res)
            pb2, pr2 = bank()
            for gi in range(4):
                mm(pb2[:, gi * NS:(gi + 1) * NS], WG[:, gi, :], PTS_[:, gi, :], True, True, WG.res + PTS_.res, [pr2])
            tt(DVE, PYT[:, :, TOK:TT], pb2[:, 0:4 * NS].rearrange("p (a b) -> p a b", a=4),
               PSC[:, :].unsqueeze(2).to_broadcast([128, 4, NS]), ALU.mult, [pr2] + PSC.res, [PYT.res[NT]])

            stage(7)
            MGT = alloc("MGT", [128, 8, TT], BF16, R1, nres=5)
            SG = [alloc("SG%d" % i, [128, 512], F32, R4 + i * 2048) for i in range(4)]
            TG = [alloc("TG%d" % i, [128, 512], F32, R4 + 8192 + i * 2048) for i in range(2)]
            for f in range(8):
                W = wslot()
                wap = W[:, 0:1024].rearrange("p (a b) -> p a b", a=8)
                waa = W[:, 1024:2048].rearrange("p (a b) -> p a b", a=8)
                wpb = W[:, 2048:2560].rearrange("p (a b) -> p a b", a=4)
                wab = W[:, 2560:2816].rearrange("p (a b) -> p a b", a=2)
                fs = slice(f * 128, (f + 1) * 128)
                dma(POOL, wap, kcv(w_in[l])[:, :, 2816 + f * 128:2816 + (f + 1) * 128], (), W.res)
                dma(POOL, waa, kcv(w_in[l])[:, :, 3840 + f * 128:3840 + (f + 1) * 128], (), W.res)
                dma(POOL, wpb, kcv(w_pool_br[l])[:, :, fs], (), W.res)
                dma(POOL, wab, kcv(w_attn_br[l])[:, :, fs], (), W.res)
                for tg in range(5):
                    cs = slice(tg * 512, (tg + 1) * 512) if tg < 4 else slice(TOK, TT)
                    n = 512 if tg < 4 else NS
                    tl = list(range(4 * tg, 4 * tg + 4)) if tg < 4 else [NT]
                    hres = [HT.res[t] for t in tl]
                    pyres = [PYT.res[t] for t in tl]
                    b1, r1_ = bank()
                    b2, r2_ = bank()
                    b3, r3_ = bank()
                    b4, r4_ = bank()
                    for kc in range(8):
                        mm(b1[:, 0:n], wap[:, kc, :], HT[:, kc, cs], kc == 0, kc == 7, W.res + hres, [r1_])
                    for kc in range(8):
                        mm(b2[:, 0:n], waa[:, kc, :], HT[:, kc, cs], kc == 0, kc == 7, W.res + hres, [r2_])
                    for gi in range(4):
                        mm(b3[:, 0:n], wpb[:, gi, :], PYT[:, gi, cs], gi == 0, gi == 3, W.res + pyres, [r3_])
                    for p_ in range(2):
                        mm(b4[:, 0:n], wab[:, p_, :], AYT[:, p_, cs], p_ == 0, p_ == 1, W.res + [AYT.res[tg]], [r4_])
                    k = (f * 5 + tg) % 2
                    sp_, sa_, tg_ = SG[2 * k], SG[2 * k + 1], TG[k]
                    act(sp_[:, 0:n], b1[:, 0:n], AF.Sigmoid, [r1_], sp_.res)
                    act(sa_[:, 0:n], b2[:, 0:n], AF.Sigmoid, [r2_], sa_.res)
                    tt(DVE, sp_[:, 0:n], sp_[:, 0:n], b3[:, 0:n], ALU.mult, sp_.res + [r3_], sp_.res)
                    tt(DVE, tg_[:, 0:n], sa_[:, 0:n], b4[:, 0:n], ALU.mult, sa_.res + [r4_], tg_.res)
                    tt(DVE, MGT[:, f, cs], sp_[:, 0:n], tg_[:, 0:n], ALU.add, sp_.res + tg_.res, [MGT.res[tg]])

            stage(8)
            MPA = alloc("MPA", [128, D], F32, R2)
            MPB = alloc("MPB", [128, D], F32, R2 + 4096)
            MSA = alloc("MSA", [NS, D], F32, R2 + 8192)
            MSB = alloc("MSB", [NS, D], F32, R2 + 12288)
            BB = alloc("BB", [128, D], F32, R3)
            NG = alloc("NG", [128, D], F32, R3 + 4096)
            ada(l, 2, MPA, MSA, BB, NG, None)
            TO = [alloc("TO%d" % i, [128, 512], F32, R4 + i * 2048) for i in range(2)]

            def resid_update(t, c, pb, pr, MP, MS, k):
                xt, xr, np_ = xtile(t)
                M = MP if t < NT else MS
                cs = slice(c * 512, (c + 1) * 512)
                to = TO[k % 2]
                tt(DVE, to[0:np_, :], pb[0:np_, :], M[0:np_, cs], ALU.mult, [pr] + M.res, to.res)
                tt(DVE, xt[:, cs], xt[:, cs], to[0:np_, :], ALU.add, [xr] + to.res, [xr])

            kk = 0
            for c in range(2):
                W = wslot()
                wo = W[:, 0:4096].rearrange("p (a b) -> p a b", a=8)
                dma(POOL, wo, kcv(w_out[l])[:, :, c * 512:(c + 1) * 512], (), W.res)
                for t in range(NT + 1):
                    np_ = 128 if t < NT else NS
                    tg = t // 4 if t < NT else 4
                    pb, pr = bank()
                    for kc in range(8):
                        mm(pb[0:np_, :], MGT[:, kc, tcols(t)], wo[:, kc, :], kc == 0, kc == 7, [MGT.res[tg]] + W.res, [pr])
                    resid_update(t, c, pb, pr, MPA, MSA, kk)
                    kk += 1

            stage(9)
            ada(l, 3, MPB, MSB, BB, NG, None)
            MPC = alloc("MPC", [128, D], F32, R1)
            MSC = alloc("MSC", [NS, D], F32, R1 + 4096)
            ada(l, 4, MPC, MSC, BB, NG, norm2_g)
            norm_phase(MPC, MPB, MSC, MSB, R4)
            stage(10)
            ada(l, 5, MPA, MSA, BB, NG, None)
            AT = alloc("AT", [128, 8, TT], BF16, R1, nres=5)
            RL = [alloc("RL%d" % i, [128, 512], F32, R4 + 4096 + i * 2048) for i in range(2)]
            kk = 0
            for j in range(4):
                for c2 in range(2):
                    W = wslot()
                    wup = W[:, 0:4096].rearrange("p (a b) -> p a b", a=8)
                    dma(POOL, wup, kcv(w_up[l])[:, :, 1024 * j + 512 * c2:1024 * j + 512 * (c2 + 1)], (), W.res)
                    for fc in range(4):
                        for tg in range(5):
                            cs = slice(tg * 512, (tg + 1) * 512) if tg < 4 else slice(TOK, TT)
                            n = 512 if tg < 4 else NS
                            tl = list(range(4 * tg, 4 * tg + 4)) if tg < 4 else [NT]
                            hres = [HT.res[t] for t in tl]
                            pb, pr = bank()
                            for kc in range(8):
                                mm(pb[:, 0:n], wup[:, kc, fc * 128:(fc + 1) * 128], HT[:, kc, cs], kc == 0, kc == 7,
                                   W.res + hres, [pr])
                            rl = RL[kk % 2]
                            kk += 1
                            act(rl[:, 0:n], pb[:, 0:n], AF.Relu, [pr], rl.res)
                            tt(DVE, AT[:, 4 * c2 + fc, cs], rl[:, 0:n], rl[:, 0:n], ALU.mult, rl.res, [AT.res[tg]])
                for c in range(2):
                    W = wslot()
                    wd = W[:, 0:4096].rearrange("p (a b) -> p a b", a=8)
                    dma(POOL, wd, kcv(w_down[l])[:, 8 * j:8 * j + 8, c * 512:(c + 1) * 512], (), W.res)
                    for t in range(NT + 1):
                        np_ = 128 if t < NT else NS
                        tg = t // 4 if t < NT else 4
                        pb, pr = bank()
                        for kc in range(8):
                            mm(pb[0:np_, :], AT[:, kc, tcols(t)], wd[:, kc, :], kc == 0, kc == 7,
                               [AT.res[tg]] + W.res, [pr])
                        resid_update(t, c, pb, pr, MPA, MSA, kk)
                        kk += 1

          except _Stop:
            break
        for t in range(NT):
            dma(SP, yp[t * 128:(t + 1) * 128, :], X[:, t, :], [X.res[t]], ())
        dma(SP, ys, XS[:, :], XS.res, ())

        S.finalize()
        sems = {n: es.enter_context(nc.semaphore(n)) for n in sorted(S.semnames)}
        with nc.Block() as block:
            @block.tensor
            def _(e):
                S.emit_engine(PE, e, sems)

            @block.scalar
            def _(e):
                S.emit_engine(ACT, e, sems)

            @block.vector
            def _(e):
                S.emit_engine(DVE, e, sems)

            @block.gpsimd
            def _(e):
                S.emit_engine(POOL, e, sems)

            @block.sync
            def _(e):
                S.emit_engine(SP, e, sems)
    return nc


def _consts(core):
    half = core % 2
    c = {}
    c["c_ident"] = np.eye(128, dtype=np.float32)
    kk = np.arange(128)[:, None]
    qq = np.arange(128)[None, :]
    cur = np.where(kk <= qq, 0.0, NEG).astype(np.float32)
    prev = np.where(kk >= qq, 0.0, NEG).astype(np.float32)
    pf = prev if half == 1 else np.full((128, 128), NEG, np.float32)
    m = np.stack([np.tile(cur, (1, 4)), np.concatenate([prev, prev, cur, cur], 1),
                  np.concatenate([pf, pf, cur, cur], 1)], axis=1)
    c["c_mask"] = np.ascontiguousarray(m, dtype=np.float32)
    inv = 10000.0 ** (-np.arange(0, 64, 2, dtype=np.float64) / 64)
    rope = np.zeros((48, 128, 64), np.float32)
    i = np.arange(128)
    for g in range(3):
        for b in range(16):
            if g == 0:
                tk = 128 * b + i
            elif g == 1:
                tk = 512 * (b // 4) + (b % 4) + 4 * i
            else:
                tk = b + 16 * i
            pos = (2048 * half + tk).astype(np.float32)
            ang = (pos[:, None] * inv[None, :].astype(np.float32)).astype(np.float32)
            rope[g * 16 + b, :, 0:32] = np.cos(ang)
            rope[g * 16 + b, :, 32:64] = np.sin(ang)
    c["c_rope"] = rope
    angs = (np.float32(8192.0) * inv.astype(np.float32)).astype(np.float32)
    c["c_ropes"] = np.tile(np.concatenate([np.cos(angs), np.sin(angs)])[None, :], (NS, 1)).astype(np.float32)
    am = np.zeros((128, 4, 4, 128), np.float32)
    tp = np.arange(128)[:, None]
    t = np.arange(128)[None, :]
    for gi, w in enumerate((2, 4, 8, 16)):
        inwin = (tp <= t) & (t - tp < w)
        curm = np.where(inwin, 1.0 / w, 0.0) - np.eye(128)
        prevm = np.where(t + 128 - tp < w, 1.0 / w, 0.0)
        am[:, 0, gi, :] = curm
        am[:, 1, gi, :] = prevm
        if half == 0:
            cntv = np.minimum(t + 1, w).astype(np.float64)
            am[:, 2, gi, :] = np.where(inwin, 1.0 / cntv, 0.0) - np.eye(128)
            am[:, 3, gi, :] = 0.0
        else:
            am[:, 2, gi, :] = curm
            am[:, 3, gi, :] = prevm
    c["c_amat"] = am
    sel = np.zeros((NS, NS, 128), np.float32)
    selc = np.zeros((128, NS, NS), np.float32)
    for b in range(NS):
        sel[b, b, :] = 1.0
        selc[:, b, b] = 1.0
    c["c_sel"] = sel
    c["c_selc"] = selc
    psel = np.zeros((NS * 15, 4, NS), np.float32)
    pcoef = np.zeros((NS, 512), np.float32)
    for gi, w in enumerate((2, 4, 8, 16)):
        for b in range(NS):
            for r in range(16 - w, 15):
                psel[b * 15 + r, gi, b] = 1.0 / w
        pcoef[:, gi * 128:(gi + 1) * 128] = 1.0 / w - 1.0
    c["c_psel"] = psel
    c["c_pcoef"] = pcoef
    return c


_NC_CACHE = {}


def kernel(x_prompt, x_sample, cache_k_w128, cache_v_w128, cache_k_w512, cache_v_w512,
           cache_k_w2048, cache_v_w2048, state_pool, c_prompt, c_sample, norm1_g, norm2_g,
           w_ada, b_ada, w_in, q_norm_g, k_norm_g, w_pool_grp, pool_scale, w_pool_br,
           w_attn_br, w_out, w_up, w_down):
    f = lambda a: np.ascontiguousarray(np.asarray(a), dtype=np.float32)
    L = NLAYER
    if L < DEPTH:
        (cache_k_w128, cache_v_w128, cache_k_w512, cache_v_w512, cache_k_w2048, cache_v_w2048, state_pool,
         norm1_g, norm2_g, w_ada, b_ada, w_in, q_norm_g, k_norm_g, w_pool_grp, pool_scale, w_pool_br,
         w_attn_br, w_out, w_up, w_down) = [np.asarray(a)[:L] for a in (
            cache_k_w128, cache_v_w128, cache_k_w512, cache_v_w512, cache_k_w2048, cache_v_w2048, state_pool,
            norm1_g, norm2_g, w_ada, b_ada, w_in, q_norm_g, k_norm_g, w_pool_grp, pool_scale, w_pool_br,
            w_attn_br, w_out, w_up, w_down)]
    x_prompt, x_sample = f(x_prompt), f(x_sample)
    cks = [f(cache_k_w128), f(cache_k_w512), f(cache_k_w2048)]
    cvs = [f(cache_v_w128), f(cache_v_w512), f(cache_v_w2048)]
    state_pool, c_prompt, c_sample = f(state_pool), f(c_prompt), f(c_sample)
    shared = {
        "norm1_g": f(norm1_g), "norm2_g": f(norm2_g), "w_ada": f(w_ada), "b_ada": f(b_ada), "w_in": f(w_in),
        "qk_g": f(np.concatenate([np.asarray(q_norm_g), np.asarray(k_norm_g)], axis=1)),
        "w_pool_grp": f(w_pool_grp),
        "pscT": f(np.asarray(pool_scale).reshape(L, 4, 128).transpose(0, 2, 1)),
        "w_pool_br": f(w_pool_br), "w_attn_br": f(w_attn_br), "w_out": f(w_out), "w_up": f(w_up),
        "w_down": f(w_down),
    }
    in_maps = []
    for c in range(8):
        b, h = c // 2, c % 2
        m = dict(shared)
        m["xp"] = np.ascontiguousarray(x_prompt[b, h * TOK:(h + 1) * TOK, :])
        m["xs"] = np.ascontiguousarray(x_sample[NS * c:NS * (c + 1), 0, :])
        m["cpT"] = np.ascontiguousarray(c_prompt[b].reshape(8, 128).T)
        m["csT"] = np.ascontiguousarray(c_sample[NS * c:NS * (c + 1)].reshape(NS, 8, 128).transpose(2, 1, 0))
        for g in range(3):
            m["ck%d" % g] = np.ascontiguousarray(cks[g][:, NS * c:NS * (c + 1)].reshape(L, NS, WINS[g], 256))
            m["cv%d" % g] = np.ascontiguousarray(cvs[g][:, NS * c:NS * (c + 1)].reshape(L, NS, WINS[g], 256))
        m["spool"] = np.ascontiguousarray(state_pool[:, NS * c:NS * (c + 1)].reshape(L, NS * 15, 512))
        m.update(_consts(c))
        in_maps.append(m)
    if "nc" not in _NC_CACHE:
        _NC_CACHE["nc"] = build_program()
    res = run_bass_kernel_spmd(_NC_CACHE["nc"], in_maps, core_ids=list(range(8)))
    R = res.results
    B = 4
    y_prompt = np.zeros((B, 2 * TOK, D), np.float32)
    y_sample = np.zeros((32, 1, D), np.float32)
    for c in range(8):
        y_prompt[c // 2, (c % 2) * TOK:(c % 2 + 1) * TOK] = R[c]["yp"]
        y_sample[NS * c:NS * (c + 1), 0] = R[c]["ys"]
    outs = [y_prompt, y_sample]
    for g in range(3):
        for nm in ("kp", "vp"):
            outs.append(np.stack([R[2 * b + 1]["%s%d" % (nm, g)] for b in range(B)], axis=1)
                        .reshape(L, B, WINS[g], 4, 64).astype(np.float32))
    outs.append(np.stack([R[2 * b + 1]["poolp"] for b in range(B)], axis=1).astype(np.float32))
    for g in range(3):
        for nm in ("ks", "vs"):
            outs.append(np.concatenate([R[c]["%s%d" % (nm, g)] for c in range(8)], axis=1)
                        .reshape(L, 32, 1, 4, 64).astype(np.float32))
    outs.append(np.concatenate([R[c]["pools"] for c in range(8)], axis=1).astype(np.float32))
    return tuple(outs)
```
